# Optimizing a Trainium2 kernel written in Bass

```python
import jax, jax.numpy as jnp
from jax import lax
import numpy as np

D_MODEL = 2048
BATCH = 4
SEQ = 4096
DEPTH = 2

POOL_WINDOWS = (2, 4, 8, 16)
POOL_GROUPS = 4
POOL_GROUP_DIM = D_MODEL // 16
POOL_WIDTH = POOL_GROUPS * POOL_GROUP_DIM
GDN_HEAD_DIM = 128
GDN_HEADS = (3 * D_MODEL // 8) // GDN_HEAD_DIM
GDN_WIDTH = GDN_HEADS * GDN_HEAD_DIM
GDN_CONV = 4
GDN_CHUNK = 64
LRU_BLOCK_DIM = 128
LRU_WIDTH = D_MODEL - POOL_WIDTH - GDN_WIDTH
LRU_BLOCKS = LRU_WIDTH // LRU_BLOCK_DIM
LRU_CONV = 4
LRU_C = 8.0
IN_SIZES = (POOL_WIDTH, GDN_WIDTH, GDN_WIDTH, GDN_WIDTH, GDN_WIDTH, GDN_HEADS, GDN_HEADS, LRU_WIDTH, LRU_WIDTH)
IN_COLS = POOL_WIDTH + 4 * GDN_WIDTH + 2 * GDN_HEADS + 2 * LRU_WIDTH
D_FF = 3 * D_MODEL
FFN_CONV = 3
EPS = 1e-6

kernel_name = 'hymba_pool_gdn_rglru_convffn'


def rmsnorm(x, w):
    xf = x.astype(jnp.float32)
    y = xf * lax.rsqrt(jnp.mean(xf * xf, axis=-1, keepdims=True) + EPS)
    return (y * w.astype(jnp.float32)).astype(x.dtype)


def l2norm(t):
    return t * lax.rsqrt(jnp.sum(t * t, axis=-1, keepdims=True) + EPS)


def causal_dwconv(x, w):
    k = w.shape[0]
    return lax.conv_general_dilated(
        x, w[:, None, :].astype(x.dtype), window_strides=(1,), padding=[(k - 1, 0)],
        dimension_numbers=('NWC', 'WIO', 'NWC'), feature_group_count=x.shape[-1])


def split_points():
    return np.cumsum(np.array(IN_SIZES))[:-1].tolist()


def pool_mixer(u, w, b, scale):
    bsz, s, _ = u.shape
    uf = u.astype(jnp.float32).reshape(bsz, s, POOL_GROUPS, POOL_GROUP_DIM)
    cs = jnp.cumsum(uf, axis=1)
    pos = jnp.arange(s)
    outs = []
    for g, win in enumerate(POOL_WINDOWS):
        c = cs[:, :, g]
        lag = jnp.pad(c, ((0, 0), (win, 0), (0, 0)))[:, :s]
        cnt = jnp.minimum(pos + 1, win).astype(jnp.float32)[None, :, None]
        outs.append((c - lag) / cnt - uf[:, :, g])
    d = jnp.stack(outs, axis=2).astype(u.dtype)
    y = jnp.einsum('bsgc,gcd->bsgd', d, w) + b
    return y.reshape(bsz, s, POOL_WIDTH) * scale


def gated_deltanet(q, k, v, z, a, bt, conv_w, a_log, dt_bias, norm_w):
    bsz, s, _ = q.shape
    H, Dh, C = GDN_HEADS, GDN_HEAD_DIM, GDN_CHUNK
    N = s // C
    out_dtype = z.dtype
    qkv = jax.nn.silu(causal_dwconv(jnp.concatenate([q, k, v], axis=-1), conv_w)).astype(jnp.float32)
    q, k, v = jnp.split(qkv, 3, axis=-1)

    def heads(t):
        return t.reshape(bsz, N, C, H, Dh).transpose(0, 3, 1, 2, 4)

    def per_head(t):
        return t.reshape(bsz, N, C, H).transpose(0, 3, 1, 2)

    q = l2norm(heads(q)) * (Dh ** -0.5)
    k = l2norm(heads(k))
    v = heads(v)
    beta = jax.nn.sigmoid(per_head(bt.astype(jnp.float32)))
    g = -jnp.exp(a_log.astype(jnp.float32)) * jax.nn.softplus(a.astype(jnp.float32) + dt_bias.astype(jnp.float32))
    g = jnp.cumsum(per_head(g), axis=-1)

    causal = jnp.tril(jnp.ones((C, C), dtype=bool))
    strict = jnp.tril(jnp.ones((C, C), dtype=bool), -1)
    decay = jnp.exp(jnp.where(causal, g[..., :, None] - g[..., None, :], -jnp.inf))
    kk = jnp.einsum('bhncd,bhnsd->bhncs', k, k)
    m = jnp.where(strict, beta[..., None] * kk * decay, 0.0) + jnp.eye(C, dtype=jnp.float32)
    rhs = jnp.concatenate([k * (beta * jnp.exp(g))[..., None], v * beta[..., None]], axis=-1)
    wu = lax.linalg.triangular_solve(m, rhs, left_side=True, lower=True)
    w_c, u_c = wu[..., :Dh], wu[..., Dh:]
    attn = jnp.einsum('bhncd,bhnsd->bhncs', q, k) * decay
    g_last = g[..., -1]
    q_dec = q * jnp.exp(g)[..., None]
    k_dec = k * jnp.exp(g_last[..., None] - g)[..., None]

    def step(state, xs):
        qd, kd, wc, uc, at, gl = xs
        v_new = uc - jnp.einsum('bhcd,bhde->bhce', wc, state)
        o = jnp.einsum('bhcd,bhde->bhce', qd, state) + jnp.einsum('bhcs,bhse->bhce', at, v_new)
        state = state * jnp.exp(gl)[..., None, None] + jnp.einsum('bhcd,bhce->bhde', kd, v_new)
        return state, o

    xs = tuple(jnp.moveaxis(t, 2, 0) for t in (q_dec, k_dec, w_c, u_c, attn, g_last))
    s0 = jnp.zeros((bsz, H, Dh, Dh), jnp.float32)
    _, o = lax.scan(step, s0, xs)
    o = o.transpose(1, 0, 3, 2, 4).reshape(bsz, s, H, Dh)
    zf = z.astype(jnp.float32).reshape(bsz, s, H, Dh)
    o = o * lax.rsqrt(jnp.mean(o * o, axis=-1, keepdims=True) + EPS) * norm_w.astype(jnp.float32) * jax.nn.silu(zf)
    return o.reshape(bsz, s, GDN_WIDTH).astype(out_dtype)


def rglru_mixer(xb, gate, conv_w, conv_b, wa, ba, wx, bx, lam):
    bsz, s, _ = xb.shape
    xc = causal_dwconv(xb, conv_w) + conv_b
    xh = xc.reshape(bsz, s, LRU_BLOCKS, LRU_BLOCK_DIM)
    r = jax.nn.sigmoid((jnp.einsum('bshc,hcd->bshd', xh, wa).reshape(bsz, s, LRU_WIDTH) + ba).astype(jnp.float32))
    i = jax.nn.sigmoid((jnp.einsum('bshc,hcd->bshd', xh, wx).reshape(bsz, s, LRU_WIDTH) + bx).astype(jnp.float32))
    log_a = -LRU_C * r * jax.nn.softplus(-lam.astype(jnp.float32))
    a = jnp.exp(log_a)
    mult = jnp.sqrt(-jnp.expm1(2.0 * log_a))
    pos = jnp.arange(s)[None, :, None]
    mult = jnp.where(pos == 0, 1.0, mult)
    b_in = mult * i * xc.astype(jnp.float32)

    def combine(lhs, rhs):
        return (lhs[0] * rhs[0], rhs[0] * lhs[1] + rhs[1])

    _, h = lax.associative_scan(combine, (a, b_in), axis=1)
    y = h * jax.nn.gelu(gate.astype(jnp.float32))
    return y.astype(xb.dtype)


def conv_ffn(h, w_up, conv_w, w_down):
    up = h @ w_up
    gate, val = jnp.split(up, 2, axis=-1)
    gate = causal_dwconv(gate, conv_w)
    return (jax.nn.gelu(gate) * val) @ w_down


def setup_inputs(seed: int = 0) -> dict:
    key = jax.random.key(seed)
    ks = jax.random.split(key, 24)
    f32 = jnp.float32
    L = DEPTH
    nrm = lambda k, shape, sc: jax.random.normal(k, shape, f32) * sc
    gain = lambda k, shape: 1.0 + 0.02 * jax.random.normal(k, shape, f32)
    dt = jnp.exp(jax.random.uniform(ks[8], (L, GDN_HEADS), f32, np.log(1e-3), np.log(1e-1)))
    a0 = jax.random.uniform(ks[17], (L, LRU_WIDTH), f32, 0.9, 0.999) ** (1.0 / LRU_C)
    return {
        'x': nrm(ks[0], (BATCH, SEQ, D_MODEL), 1.0),
        'norm1_w': gain(ks[1], (L, D_MODEL)),
        'w_in': nrm(ks[2], (L, D_MODEL, IN_COLS), D_MODEL ** -0.5),
        'pool_w': nrm(ks[3], (L, POOL_GROUPS, POOL_GROUP_DIM, POOL_GROUP_DIM), POOL_GROUP_DIM ** -0.5),
        'pool_b': nrm(ks[4], (L, POOL_GROUPS, POOL_GROUP_DIM), 0.01),
        'pool_scale': 0.5 + 0.05 * jax.random.normal(ks[5], (L, POOL_WIDTH), f32),
        'gdn_conv_w': nrm(ks[6], (L, GDN_CONV, 3 * GDN_WIDTH), GDN_CONV ** -0.5),
        'gdn_a_log': jnp.log(jax.random.uniform(ks[7], (L, GDN_HEADS), f32, 1.0, 16.0)),
        'gdn_dt_bias': dt + jnp.log(-jnp.expm1(-dt)),
        'gdn_norm_w': gain(ks[9], (L, GDN_HEAD_DIM)),
        'lru_conv_w': nrm(ks[10], (L, LRU_CONV, LRU_WIDTH), LRU_CONV ** -0.5),
        'lru_conv_b': nrm(ks[11], (L, LRU_WIDTH), 0.01),
        'lru_wa': nrm(ks[12], (L, LRU_BLOCKS, LRU_BLOCK_DIM, LRU_BLOCK_DIM), LRU_BLOCK_DIM ** -0.5),
        'lru_ba': nrm(ks[13], (L, LRU_WIDTH), 0.01),
        'lru_wx': nrm(ks[14], (L, LRU_BLOCKS, LRU_BLOCK_DIM, LRU_BLOCK_DIM), LRU_BLOCK_DIM ** -0.5),
        'lru_bx': nrm(ks[15], (L, LRU_WIDTH), 0.01),
        'lru_lambda': jnp.log(a0 / (1.0 - a0)),
        'w_out': nrm(ks[16], (L, D_MODEL, D_MODEL), D_MODEL ** -0.5),
        'norm2_w': gain(ks[18], (L, D_MODEL)),
        'ffn_up': nrm(ks[19], (L, D_MODEL, 2 * D_FF), D_MODEL ** -0.5),
        'ffn_conv_w': nrm(ks[20], (L, FFN_CONV, D_FF), FFN_CONV ** -0.5),
        'ffn_down': nrm(ks[21], (L, D_FF, D_MODEL), D_FF ** -0.5),
        'final_norm_w': gain(ks[22], (D_MODEL,)),
    }


def reference(x, norm1_w, w_in, pool_w, pool_b, pool_scale, gdn_conv_w, gdn_a_log, gdn_dt_bias,
              gdn_norm_w, lru_conv_w, lru_conv_b, lru_wa, lru_ba, lru_wx, lru_bx, lru_lambda,
              w_out, norm2_w, ffn_up, ffn_conv_w, ffn_down, final_norm_w):
    cuts = split_points()
    for l in range(DEPTH):
        h = rmsnorm(x, norm1_w[l])
        proj = h @ w_in[l]
        u_pool, q, k, v, z, a, bt, xr, gr = jnp.split(proj, cuts, axis=-1)
        y_pool = pool_mixer(u_pool, pool_w[l], pool_b[l], pool_scale[l])
        y_gdn = gated_deltanet(q, k, v, z, a, bt, gdn_conv_w[l], gdn_a_log[l], gdn_dt_bias[l], gdn_norm_w[l])
        y_lru = rglru_mixer(xr, gr, lru_conv_w[l], lru_conv_b[l], lru_wa[l], lru_ba[l],
                            lru_wx[l], lru_bx[l], lru_lambda[l])
        mixed = jnp.concatenate([y_pool.astype(x.dtype), y_gdn.astype(x.dtype), y_lru.astype(x.dtype)], axis=-1)
        x = x + mixed @ w_out[l]
        x = x + conv_ffn(rmsnorm(x, norm2_w[l]), ffn_up[l], ffn_conv_w[l], ffn_down[l])
    return rmsnorm(x, final_norm_w)
```

```python
import numpy as np
from contextlib import ExitStack
import concourse.bass as bass
import concourse.mybir as mybir
from concourse.bass_utils import run_bass_kernel_spmd

F32 = mybir.dt.float32
BF16 = mybir.dt.bfloat16
AF = mybir.ActivationFunctionType
ALU = mybir.AluOpType
P = 128
EPS = 1e-6
NEG = -30000.0

D = 2048
ND = D // P
POOL_W = 512
GDN_W = 768
NH = 6
LRU_W = 768
NLB = 6
IN_COLS = POOL_W + 4 * GDN_W + 2 * NH + 2 * LRU_W
C_POOL, C_Q, C_K, C_V, C_Z = 0, 512, 1280, 2048, 2816
C_A, C_B, C_XR, C_GR = 3584, 3590, 3596, 4364
TM = 512
TF = 1024
CH = 128

CO_N1, CO_N2, CO_NF, CO_PB, CO_PS, CO_GCW, CO_GNW, CO_LCW, CO_LCB, CO_LBA, CO_LBX, CO_LLAM = (
    0, 16, 32, 48, 52, 56, 128, 129, 153, 159, 165, 171)
CO_FCW = 177


def n_cols(d_ff):
    return CO_FCW + 3 * (d_ff // P)


class View:
    __slots__ = ("ap", "regs")

    def __init__(self, ap, regs):
        self.ap = ap
        self.regs = regs


class Buf:
    def __init__(self, k, space, off, dtype, shape, np_=P):
        self.k, self.space, self.off, self.dtype, self.shape, self.np = k, space, off, dtype, tuple(shape), np_
        self.esz = 4 if dtype == F32 else 2
        n = 1
        for s in shape:
            n *= s
        self.n = n
        self.nbytes = n * self.esz
        if space == "sb":
            w0 = off // 4
            base = k.arena[0:np_, w0:w0 + (self.nbytes + 3) // 4]
            if dtype != F32:
                base = base.bitcast(dtype)
        else:
            bank = off // 2048
            w0 = (off % 2048) // 4
            base = k.psum[bank][0:np_, w0:w0 + (self.nbytes + 3) // 4]
            if dtype != F32:
                base = base.bitcast(dtype)
        if len(shape) == 2:
            base = base.rearrange("p (a b) -> p a b", a=shape[0])
        elif len(shape) == 3:
            base = base.rearrange("p (a b c) -> p a b c", a=shape[0], b=shape[1])
        self.base = base

    def __getitem__(self, idx):
        if not isinstance(idx, tuple):
            idx = (idx,)
        return self.v(idx)

    def v(self, idx=(), p=None):
        shape = self.shape
        idx = tuple(idx) + (slice(None),) * (len(shape) - len(idx))
        lo = 0
        hi = 0
        stride = self.n
        for s, i in zip(shape, idx):
            stride //= s
            if isinstance(i, int):
                a, b = i, i + 1
            else:
                a = 0 if i.start is None else i.start
                b = s if i.stop is None else i.stop
                assert i.step is None
            assert 0 <= a < b <= s, (shape, idx)
            lo += a * stride
            hi += (b - 1) * stride
        hi += 1
        p0, p1 = (0, self.np) if p is None else p
        ap = self.base[(slice(p0, p1),) + idx]
        return View(ap, [(self.space, self.off + lo * self.esz, self.off + hi * self.esz)])


class Op:
    __slots__ = ("eng", "fn", "deps", "seq", "key", "val", "prev_val", "i")


ENGS = ("pe", "act", "dve", "pool", "sp")
EPOCH = 12000
BUCKET = 512


class Rec:
    def __init__(self):
        self.ops = []
        self.by_eng = {e: [] for e in ENGS}
        self.wr = {}
        self.rd = {}
        self.dma_val = {}

    def _buckets(self, sp, lo, hi):
        return [(sp, b) for b in range(lo // BUCKET, (hi - 1) // BUCKET + 1)]

    def add(self, eng, fn, reads, writes, key=None, ndma=1):
        op = Op()
        op.eng, op.fn, op.key, op.i = eng, fn, key, len(self.ops)
        deps = set()
        rregs = [r for v in reads if v is not None and isinstance(v, View) for r in v.regs]
        wregs = [r for v in writes for r in v.regs]
        for (sp, lo, hi) in rregs:
            for bk in self._buckets(sp, lo, hi):
                for (a, b, o) in self.wr.get(bk, ()):
                    if a < hi and lo < b:
                        deps.add(o)
        for (sp, lo, hi) in wregs:
            for bk in self._buckets(sp, lo, hi):
                for (a, b, o) in self.wr.get(bk, ()):
                    if a < hi and lo < b:
                        deps.add(o)
                for (a, b, _e), o in self.rd.get(bk, {}).items():
                    if a < hi and lo < b:
                        deps.add(o)
        deps.discard(op.i)
        op.deps = deps
        if key is not None:
            pv = self.dma_val.get(key, 0)
            op.prev_val = pv
            op.val = pv + 16 * ndma
            self.dma_val[key] = op.val
            op.seq = None
        else:
            op.seq = len(self.by_eng[eng])
        for (sp, lo, hi) in wregs:
            for bk in self._buckets(sp, lo, hi):
                blo, bhi = bk[1] * BUCKET, (bk[1] + 1) * BUCKET
                l = self.wr.setdefault(bk, [])
                l[:] = [(a, b, o) for (a, b, o) in l if not (lo <= max(a, blo) and min(b, bhi) <= hi)]
                l.append((lo, hi, op.i))
                d = self.rd.get(bk)
                if d:
                    for kk in [kk for kk in d if lo <= max(kk[0], blo) and min(kk[1], bhi) <= hi]:
                        del d[kk]
        ek = eng if key is None else ("dma", op.i)
        for (sp, lo, hi) in rregs:
            for bk in self._buckets(sp, lo, hi):
                self.rd.setdefault(bk, {})[(lo, hi, ek)] = op.i
        self.ops.append(op)
        self.by_eng[eng].append(op)
        return op

    def emit(self, nc, stack):
        esem = {}
        for e in ("pe", "act", "dve", "pool"):
            n = len(self.by_eng[e])
            esem[e] = [stack.enter_context(nc.semaphore(f"s_{e}{i}")) for i in range(n // EPOCH + 1)]
        dsem = {k: stack.enter_context(nc.semaphore(f"d_{k}")) for k in self.dma_val}
        ops = self.ops
        block = stack.enter_context(nc.Block())

        def run(eng, e):
            waited = {}

            def wait(sem, val, tag):
                if waited.get(tag, 0) < val:
                    e.wait_ge(sem, val)
                    waited[tag] = val

            for op in self.by_eng[eng]:
                for di in sorted(op.deps):
                    d = ops[di]
                    if d.key is not None:
                        wait(dsem[d.key], d.val, ("d", d.key))
                    else:
                        if d.eng == "pe" and eng == "pe":
                            continue
                        ep = d.seq // EPOCH
                        wait(esem[d.eng][ep], d.seq % EPOCH + 1, (d.eng, ep))
                if op.key is not None:
                    if op.prev_val:
                        wait(dsem[op.key], op.prev_val, ("d", op.key))
                    op.fn(e, dsem[op.key])
                else:
                    ins = op.fn(e)
                    ins.then_inc(esem[eng][op.seq // EPOCH], 1)
            if eng == "sp":
                for k, v in self.dma_val.items():
                    wait(dsem[k], v, ("d", k))

        @block.tensor
        def _(e):
            run("pe", e)

        @block.scalar
        def _(e):
            run("act", e)

        @block.vector
        def _(e):
            run("dve", e)

        @block.gpsimd
        def _(e):
            run("pool", e)

        @block.sync
        def _(e):
            run("sp", e)


class Cfg:
    def __init__(self, S=4096, depth=2, d_ff=6144):
        self.S, self.depth, self.d_ff = S, depth, d_ff
        self.NF = d_ff // P
        assert S % TF == 0 and self.NF % 4 == 0


class K:
    def __init__(self, cfg):
        self.cfg = cfg
        self.nc = bass.Bass("TRN2", target_bir_lowering=False)
        self.rec = Rec()
        self.cur = None
        self.stage_sel = None
        self.banks = list(range(7))
        self.bank_i = 0

    def _emit(self, eng, fn, reads, writes, key=None):
        if self.cur is not None:
            self.cur.append((eng, fn, reads, writes, key))
        else:
            self.rec.add(eng, fn, reads, writes, key=key)

    def stream(self, fn, banks):
        assert self.cur is None
        save = (self.banks, self.bank_i)
        self.cur, self.banks, self.bank_i = [], banks, 0
        fn()
        out = self.cur
        self.cur = None
        self.banks, self.bank_i = save
        return out

    def _cost(self, a):
        eng, fn, reads, writes, key = a
        n = 1
        for d in writes[0].ap.shape[1:]:
            n *= d
        if key is not None:
            return 2.0 + n * 128 * 4 / 300e3
        if eng == "pe":
            c = getattr(fn, "cost", None)
            return c if c is not None else 0.2
        if eng == "dve":
            return 0.12 + n / 960.0
        if eng == "act":
            return 0.15 + n / 1100.0
        return 0.2 + n * 0.0035

    def merge(self, lists):
        lists = [l for l in lists if l]
        deps = []
        for l in lists:
            d = []
            wr, rd = [], []
            for i, a in enumerate(l):
                rr = [r for v in a[2] if isinstance(v, View) for r in v.regs]
                ww = [r for v in a[3] for r in v.regs]
                s_ = set()
                for (sp, lo, hi) in rr:
                    for (sp2, a2, b2, o) in wr:
                        if sp == sp2 and a2 < hi and lo < b2:
                            s_.add(o)
                for (sp, lo, hi) in ww:
                    for (sp2, a2, b2, o) in wr:
                        if sp == sp2 and a2 < hi and lo < b2:
                            s_.add(o)
                    for (sp2, a2, b2, o) in rd:
                        if sp == sp2 and a2 < hi and lo < b2:
                            s_.add(o)
                for (sp, lo, hi) in ww:
                    wr = [w for w in wr if not (w[0] == sp and lo <= w[1] and w[2] <= hi)]
                    rd = [w for w in rd if not (w[0] == sp and lo <= w[1] and w[2] <= hi)]
                    wr.append((sp, lo, hi, i))
                for (sp, lo, hi) in rr:
                    rd.append((sp, lo, hi, i))
                if len(rd) > 200:
                    rd = rd[-200:]
                d.append(s_)
            deps.append(d)
        pos = [0] * len(lists)
        fin = [[0.0] * len(l) for l in lists]
        efree = {e: 0.0 for e in ENGS}
        while True:
            best, bi = None, -1
            for i, l in enumerate(lists):
                if pos[i] < len(l):
                    a = l[pos[i]]
                    rdy = 0.0
                    for o in deps[i][pos[i]]:
                        t = fin[i][o] + 0.3
                        if t > rdy:
                            rdy = t
                    st = max(rdy, efree[a[0]])
                    if best is None or st < best - 1e-9:
                        best, bi = st, i
            if bi < 0:
                break
            a = lists[bi][pos[bi]]
            c = self._cost(a)
            if a[4] is not None:
                efree[a[0]] = best + 0.05
            else:
                efree[a[0]] = best + c
            fin[bi][pos[bi]] = best + c
            pos[bi] += 1
            self.rec.add(a[0], a[1], a[2], a[3], key=a[4])

    def A(self, v):
        return v.ap if isinstance(v, View) else v

    def mm(self, out, pairs, start=True, stop=True):
        reads = [x for pr in pairs for x in pr]

        def fn(e):
            ins = None
            n = len(pairs)
            for i, (l, r) in enumerate(pairs):
                ins = e.matmul(out.ap, l.ap, r.ap, start=(start and i == 0), stop=(stop and i == n - 1))
            return ins
        cols = 1
        for d in pairs[0][1].ap.shape[1:]:
            cols *= d
        passes = 4 if pairs[0][0].ap.dtype == F32 else 1
        fn.cost = len(pairs) * (0.03 + cols * passes / 2400.0 * (0.6 if passes == 4 else 1.0))
        self._emit("pe", fn, reads + ([] if start else [out]), [out])

    def tr(self, out, in_):
        np_ = in_.ap.shape[0]
        idv = self.ident.v((slice(0, np_),), p=(0, np_))
        self._emit("pe", lambda e: e.transpose(out.ap, in_.ap, idv.ap), [in_, idv], [out])

    def act(self, out, in_, func, bias=None, scale=None, eng="act"):
        kw = {}
        if func == AF.Copy and (bias is not None or scale is not None):
            func = AF.Identity
        if bias is not None:
            kw["bias"] = self.A(bias)
        if scale is not None:
            kw["scale"] = self.A(scale)
        self._emit("act", lambda e: e.activation(out=out.ap, in_=in_.ap, func=func, **kw),
                     [in_, bias, scale], [out])

    def tt(self, out, in0, in1, op, eng="dve"):
        self._emit(eng, lambda e: e.tensor_tensor(out.ap, in0.ap, in1.ap, op), [in0, in1], [out])

    def ts(self, out, in0, s1, op0, s2=None, op1=None, eng="dve"):
        if op1 is None:
            fn = lambda e: e.tensor_scalar(out.ap, in0.ap, self.A(s1), None, op0)
        else:
            fn = lambda e: e.tensor_scalar(out.ap, in0.ap, self.A(s1), self.A(s2), op0, op1)
        self._emit(eng, fn, [in0, s1, s2], [out])

    def stt(self, out, in0, sc, in1, op0, op1, eng="dve"):
        eng = "dve"
        self._emit(eng, lambda e: e.scalar_tensor_tensor(out.ap, in0.ap, self.A(sc), in1.ap, op0, op1),
                     [in0, sc, in1], [out])

    def rsqrt(self, out, in_, mul, add):
        self.ts(out, in_, mul, ALU.mult, add, ALU.add)
        self.act(out, out, AF.Ln)
        self.act(out, out, AF.Exp, scale=-0.5)

    def scan(self, out, d0, d1, init, op0, op1, eng="dve"):
        eng = "dve"
        self._emit(eng, lambda e: e.tensor_tensor_scan(out.ap, d0.ap, d1.ap, self.A(init), op0, op1),
                     [d0, d1, init], [out])

    def copy(self, out, in_, eng="dve"):
        if eng == "act":
            self.act(out, in_, AF.Copy)
        else:
            self._emit(eng, lambda e: e.tensor_copy(out.ap, in_.ap), [in_], [out])

    def memset(self, out, val, eng="dve"):
        self._emit(eng, lambda e: e.memset(out.ap, val), [], [out])

    def dma(self, out, in_, key, eng="sp"):
        self._emit(eng, lambda e, sem: e.dma_start(out=out.ap, in_=in_.ap).then_inc(sem, 16),
                     [in_], [out], key=key)

    def xsv(self, d0, d1, t0, t1):
        if d1 - d0 == 1:
            ap = self.xs[:, d0, t0:t1]
        else:
            ap = self.xs[:, d0:d1, t0:t1]
        return View(ap, [("dr:xs%d" % d, t0, t1) for d in range(d0, d1)])

    def dview(self, name, ap, lo, hi):
        return View(ap, [("dr:" + name, lo, hi)])

    def sb(self, dtype, shape, np_=P):
        esz = 4 if dtype == F32 else 2
        n = int(np.prod(shape)) * esz
        n = (n + 31) // 32 * 32
        off = self.sb_top
        self.sb_top += n
        assert self.sb_top <= self.ARENA_BYTES, ("SBUF arena overflow", self.sb_top)
        return Buf(self, "sb", off, dtype, shape, np_)

    def ps(self, shape=(512,), np_=P, bank=None):
        if bank is None:
            bank = self.banks[self.bank_i % len(self.banks)]
            self.bank_i += 1
        return Buf(self, "ps", bank * 2048, F32, shape, np_)

    def build(self):
        cfg, nc = self.cfg, self.nc
        S, L, NF = cfg.S, cfg.depth, cfg.NF
        NCOL = n_cols(cfg.d_ff)
        dt = nc.dram_tensor
        self.x_in = dt("x", [S, D], F32, kind="ExternalInput").ap()
        self.w_in = dt("w_in", [L, D, IN_COLS], F32, kind="ExternalInput").ap()
        self.w_out = dt("w_out", [L, D, D], F32, kind="ExternalInput").ap()
        self.ffn_up = dt("ffn_up", [L, D, 2 * cfg.d_ff], F32, kind="ExternalInput").ap()
        self.ffn_down = dt("ffn_down", [L, cfg.d_ff, D], F32, kind="ExternalInput").ap()
        self.pool_w = dt("pool_w", [L, 4, P, P], F32, kind="ExternalInput").ap()
        self.lru_wa = dt("lru_wa", [L, NLB, P, P], F32, kind="ExternalInput").ap()
        self.lru_wx = dt("lru_wx", [L, NLB, P, P], F32, kind="ExternalInput").ap()
        self.cols_d = dt("cols", [L, P, NCOL], F32, kind="ExternalInput").ap()
        self.hrow_d = dt("hrow", [L, NH, 2], F32, kind="ExternalInput").ap()
        self.const_d = dt("consts", [P, 6 * P + NH * P + 64], F32, kind="ExternalInput").ap()
        self.y_out = dt("y", [S, D], F32, kind="ExternalOutput").ap()
        self.xs = dt("xs", [P, ND, S], F32).ap()
        self.n_units_M = 21 + 8
        self.n_units_F = NF // 2 * 2 + NF // 2
        self.wsM = dt("wsM", [self.n_units_M, P, 4096], BF16).ap()
        self.wsF = dt("wsF", [self.n_units_F, P, 4096], BF16).ap()

        self.ARENA_BYTES = 207 * 1024
        with ExitStack() as st:
            self.arena = st.enter_context(nc.sbuf_tensor("arena", [P, self.ARENA_BYTES // 4], F32))
            self.psum = [st.enter_context(nc.psum_tensor(f"psb{i}", [P, 512], F32)) for i in range(8)]
            self.sb_top = 0
            self.ident = self.sb(F32, (P,))
            self.ones = self.sb(F32, (P,))
            cd = self.const_d
            self.dma(self.ident.v(), self.dview("c", cd[:, 0:P], 0, 1), "c0")
            self.dma(self.ones.v(), self.dview("c", cd[:, P:2 * P], 0, 1), "c0")
            self.cols = self.sb(F32, (NCOL,))
            self.hrow = self.sb(F32, (2,), np_=NH)
            self.negA = self.sb(F32, (1,), np_=NH)
            self.lruc = self.sb(F32, (NLB,))
            self.pbs = self.sb(F32, (4,))
            self.sb_persist = self.sb_top
            for l in range(L):
                self.layer_setup(l)
                self.m_pass(l)
                self.f_pass(l)
            self.rec.emit(nc, st)
        return nc

    def layer_setup(self, l):
        self.sb_top = self.sb_persist
        t6 = self.sb(F32, (8,), np_=NH)
        tl = self.sb(F32, (NLB,))
        self.dma(self.cols.v(), self.dview("cols", self.cols_d[l], l, l + 1), "cols")
        self.dma(self.hrow.v(), self.dview("hrow", self.hrow_d[l], l, l + 1), "cols")
        self.act(t6.v((slice(0, 1),)), self.hrow.v((slice(0, 1),)), AF.Exp)
        self.ts(self.negA.v(), t6.v((slice(0, 1),)), -1.0, ALU.mult)
        lam = self.cols.v((slice(CO_LLAM, CO_LLAM + NLB),))
        self.act(tl.v(), lam, AF.Exp, scale=-1.0)
        self.act(tl.v(), tl.v(), AF.Ln, bias=1.0)
        self.ts(self.lruc.v(), tl.v(), -8.0, ALU.mult)
        self.tt(self.pbs.v(), self.cols.v((slice(CO_PB, CO_PB + 4),)), self.cols.v((slice(CO_PS, CO_PS + 4),)), ALU.mult)

    def fetch_unit(self, first, dst, ws, uidx, pieces, key):
        if first:
            for (src, dstv, stv) in pieces:
                if self.stage_sel is not None:
                    sidx = self.stage_sel
                else:
                    sidx = self.stage_rr
                    self.stage_rr = (self.stage_rr + 1) % len(self.stage)
                sv = stv(self.stage[sidx])
                self.dma(sv, src, f"stg{sidx}")
                self.copy(dstv, sv, eng=self.cast_engs[self.cast_rr % len(self.cast_engs)])
                self.cast_rr += 1
            self.dma(self.dview(ws[1], ws[0][uidx], uidx, uidx + 1), View(dst.base.rearrange("p a b -> p (a b)") if len(dst.shape) == 2 else dst.base, dst.v().regs), key + "o")
        else:
            self.dma(View(dst.base.rearrange("p a b -> p (a b)") if len(dst.shape) == 2 else dst.base, dst.v().regs),
                     self.dview(ws[1], ws[0][uidx], uidx, uidx + 1), key)

    def kunit_pieces(self, wl, cols, dst):
        pieces = []
        off = 0
        for (c0, w) in cols:
            for kh in range(2):
                src = wl[kh * 1024:(kh + 1) * 1024, c0:c0 + w].rearrange("(j p) c -> p j c", p=P)
                srcv = self.dview("w", src, 0, 1)
                dstv = dst.v((slice(kh * 8, kh * 8 + 8), slice(off, off + w)))
                pieces.append((srcv, dstv, (lambda w_: (lambda sb_: View(
                    sb_.base[:, 0:8 * w_].rearrange("p (j c) -> p j c", j=8), sb_.v().regs)))(w)))
            off += w
        return pieces

    def rmsnorm(self, xT, T, wcol0, out_fn, sq, rstd):
        for hh in range(T // 512):
            ts_ = slice(hh * 512, hh * 512 + 512)
            pss = self.ps()
            for j in range(ND):
                s = sq[j % 2].v((slice(0, 512),))
                self.act(s, xT.v((j, ts_)), AF.Square)
                self.mm(pss.v(), [(self.ones.v(), s)], start=(j == 0), stop=(j == ND - 1))
            r = rstd.v((ts_,))
            self.rsqrt(r, pss.v(), 1.0 / D, EPS)
            for j in range(ND):
                self.stt(out_fn(j, hh), xT.v((j, ts_)), self.cols.v((slice(wcol0 + j, wcol0 + j + 1),)), r,
                         ALU.mult, ALU.mult, eng=("dve" if j % 2 == 0 else "pool"))

    def m_pass(self, l):
        cfg = self.cfg
        S = cfg.S
        T = TM
        self.sb_top = self.sb_persist
        xT = self.sb(F32, (ND, TM))
        hT = self.sb(BF16, (ND, TM))
        mT = self.sb(BF16, (ND, TM))
        self.mT = mT
        self.stage = [self.sb(F32, (2048,)) for _ in range(2)]
        self.stage_rr = 0
        self.cast_engs = ["dve", "act"]
        self.cast_rr = 0
        wb = [self.sb(BF16, (ND, 256)) for _ in range(4)]
        wsmall = self.sb(F32, (16, P))
        self.wsmall = wsmall
        cd = self.const_d
        self.nmU = self.sb(F32, (P,))
        self.pmL = self.sb(F32, (P,))
        self.mU01 = self.sb(F32, (P,))
        self.sel6 = self.sb(F32, (NH * P,), np_=NH)
        self.poolrc = self.sb(F32, (4, 16))
        self.dma(self.nmU.v(), self.dview("c", cd[:, 2 * P:3 * P], 0, 1), "c0")
        self.dma(self.pmL.v(), self.dview("c", cd[:, 3 * P:4 * P], 0, 1), "c0")
        self.dma(self.mU01.v(), self.dview("c", cd[:, 4 * P:5 * P], 0, 1), "c0")
        self.dma(self.sel6.v(), self.dview("c", cd[0:NH, 6 * P:6 * P + NH * P], 0, 1), "c0")
        self.dma(self.poolrc.v(), self.dview("c", cd[:, 6 * P + NH * P:6 * P + NH * P + 64].rearrange(
            "p (a b) -> p a b", a=4), 0, 1), "c0")
        self.Sst = self.sb(F32, (NH, P))
        self.hst = self.sb(F32, (NLB,))
        self.car_g = self.sb(F32, (18, 3))
        self.car_l = self.sb(F32, (NLB, 3))
        self.car_p = self.sb(F32, (4, 16))
        for b_ in (self.Sst, self.hst, self.car_g, self.car_l, self.car_p):
            self.memset(b_.v(), 0.0, eng="pool")
        wt = lambda: self.sb(F32, (TM + 16,))
        TB = [wt() for _ in range(10)]
        TA = [wt() for _ in range(2)]
        TAd = [[wt() for _ in range(4)] for _ in range(2)]
        TC = [wt() for _ in range(8)]
        xtok = Buf(self, "sb", TA[0].off, F32, (D,))
        assert TAd[0][3].off + TAd[0][3].nbytes - TA[0].off >= D * 4
        rowb = self.sb(F32, (5, TM), np_=NH)
        cvo = [self.sb(BF16, (2048,)) for _ in range(2)]
        NF = cfg.NF
        wlu, wld = self.ffn_up[l], self.ffn_down[l]
        cv_pieces = []
        for pi in range(NF // 2):
            for gv in range(2):
                for kh in range(2):
                    c0 = gv * cfg.d_ff + pi * 256
                    src = wlu[kh * 1024:(kh + 1) * 1024, c0:c0 + 256].rearrange("(j p) c -> p j c", p=P)
                    cv_pieces.append((src, pi * 2 + gv, kh, True))
        for q in range(NF // 2):
            for i in range(2):
                f = q * 2 + i
                cv_pieces.append((wld[f * P:(f + 1) * P, :], NF + q, i, False))
        cv_pos = [0]
        NPc = len(cv_pieces)
        cv_seq = []
        for i in range(NPc + 2):
            if i < NPc:
                cv_seq.append(("in", i))
            if 1 <= i <= NPc:
                cv_seq.append(("cast", i - 1))
            if 2 <= i <= NPc + 1:
                cv_seq.append(("out", i - 2))
        colb = self.sb(F32, (4, 4, NH))
        cegl = self.sb(F32, (NH, 4))
        glast = self.sb(F32, (8,), np_=NH)
        for g in range(4):
            self.dma(wsmall.v((g,)), self.dview("pw", self.pool_w[l, g], 0, 1), "wsm")
        for j in range(NLB):
            self.dma(wsmall.v((4 + j,)), self.dview("pw", self.lru_wa[l, j], 0, 1), "wsm")
            self.dma(wsmall.v((10 + j,)), self.dview("pw", self.lru_wx[l, j], 0, 1), "wsm")
        wl_in, wl_out = self.w_in[l], self.w_out[l]
        ws = (self.wsM, "wsM")
        units = [[(C_POOL, 128), (C_POOL + 128, 128)], [(C_POOL + 256, 128), (C_POOL + 384, 128)], [(C_A, 12)]]
        for h in range(NH):
            units.append([(C_Q + h * P, P), (C_K + h * P, P)])
            units.append([(C_V + h * P, P), (C_Z + h * P, P)])
        for j in range(NLB):
            units.append([(C_XR + j * P, P), (C_GR + j * P, P)])
        nb = S // TM
        assert nb >= 2
        for b in range(nb):
            first = (b == 0)
            t0 = b * TM
            tsl = slice(t0, t0 + TM)

            def fetch(u, slot):
                if u < 21:
                    pcs = self.kunit_pieces(wl_in, units[u], wb[slot])
                else:
                    pcs = self.kunit_pieces(wl_out, [((u - 21) * 256, 256)], wb[slot])
                self.fetch_unit(first, wb[slot], ws, u, pcs, f"wb{slot}")

            def proj(slot, off, width):
                pso = self.ps((TM,), np_=width)
                self.mm(pso.v(), [(wb[slot].v((k, slice(off, off + width))), hT.v((k,))) for k in range(ND)])
                return pso
            def load_x(bb):
                tb0 = bb * TM
                if l == 0:
                    for tt_ in range(TM // P):
                        self.dma(xtok.v(), self.dview("x", self.x_in[tb0 + tt_ * P:tb0 + (tt_ + 1) * P, :], 0, 1), "xtok")
                        for g4 in range(4):
                            pst = self.ps((4, P))
                            for q in range(4):
                                self.tr(pst.v((q,)), xtok.v((slice((g4 * 4 + q) * P, (g4 * 4 + q + 1) * P),)))
                            self.copy(xT.v((slice(g4 * 4, g4 * 4 + 4), slice(tt_ * P, (tt_ + 1) * P))), pst.v(),
                                      eng=("act" if g4 % 2 else "dve"))
                    self.dma(self.xsv(0, ND, tb0, tb0 + TM), xT.v(), "xTo")
                else:
                    self.dma(xT.v(), self.xsv(0, ND, tb0, tb0 + TM), "xT")

            def norm_rows():
                self.rmsnorm(xT, TM, CO_N1, lambda j, hh: hT.v((j,)), [TB[0], TB[1]], TB[2])
                psa = proj(3, 0, NH)
                psb = proj(3, NH, NH)
                self.gdn_rows(psa, psb, rowb, colb, cegl, glast)

            if b == 0:
                fetch(2, 3)
                fetch(3, 0)
                fetch(4, 1)
                fetch(0, 2)
                load_x(0)
                norm_rows()

            def stream_A(h):
                self.stage_sel = 0
                pad, cv = TA
                qn, kn, vs, zs = TAd[h % 2]
                for (slot, off, ti, dst) in ((0, 0, 0, qn), (0, P, 1, kn), (1, 0, 2, vs)):
                    psx = proj(slot, off, P)
                    self.gdn_conv(psx, 3 * h + ti, pad, cv, dst)
                psz = proj(1, P, P)
                self.act(zs.v((slice(0, T),)), psz.v(), AF.Silu)
                if h + 1 < NH:
                    fetch(3 + 2 * (h + 1), 0)
                    fetch(4 + 2 * (h + 1), 1)
                sl = (slice(0, T),)
                sq, rn = pad, cv
                for (buf, scl) in ((kn, None), (qn, P ** -0.5)):
                    self.act(sq.v(sl), buf.v(sl), AF.Square)
                    pss = self.ps()
                    self.mm(pss.v(), [(self.ones.v(), sq.v(sl))])
                    self.rsqrt(rn.v(sl), pss.v(), 1.0, EPS)
                    if scl is None:
                        self.tt(buf.v(sl), buf.v(sl), rn.v(sl), ALU.mult)
                    else:
                        self.stt(buf.v(sl), buf.v(sl), scl, rn.v(sl), ALU.mult, ALU.mult)

            c_units = [0, 1, 15, 16, 17, 18, 19, 20]
            c_slot = lambda i: 2 + (i % 2)

            def stream_C(r):
                self.stage_sel = 1
                steps = [r] if r < 6 else [6, 7]
                for i in steps:
                    if i + 1 < len(c_units):
                        fetch(c_units[i + 1], c_slot(i + 1))
                    slot = c_slot(i)
                    if i < 2:
                        for q in range(2):
                            self.pool_group(first, i * 2 + q, proj(slot, q * P, P), TC)
                    else:
                        j = i - 2
                        psx = proj(slot, 0, P)
                        psg = proj(slot, P, P)
                        self.lru_block(b == 0, j, psx, psg, TC)
                if r == 6:
                    fetch(21, 0)
                    fetch(22, 1)

            def cv_op(kind, i):
                src, u, hf, is_up = cv_pieces[i]
                stg, ob = self.stage[i % 2], cvo[i % 2]
                if is_up:
                    sv = View(stg.base.rearrange("p (j c) -> p j c", j=8), stg.v().regs)
                    ov = View(ob.base.rearrange("p (j c) -> p j c", j=8), ob.v().regs)
                else:
                    sv, ov = stg.v(), ob.v()
                if kind == "in":
                    self.dma(sv, self.dview("w", src, 0, 1), f"cvi{i % 2}")
                elif kind == "cast":
                    self.copy(ov, sv, eng="pool")
                else:
                    self.dma(self.dview("wsF", self.wsF[u][:, hf * 2048:(hf + 1) * 2048], u, u + 1), ob.v(),
                             f"cvo{i % 2}")

            def stream_D(n):
                for _ in range(n):
                    if cv_pos[0] >= len(cv_seq):
                        return
                    kind, i = cv_seq[cv_pos[0]]
                    cv_pos[0] += 1
                    cv_op(kind, i)

            n_cv = -(-len(cv_seq) // (7 * (nb - 1)))
            for r in range(7):
                lists = []
                if b >= 1:
                    lists.append(self.stream(lambda: stream_D(n_cv), []))
                if r < NH:
                    lists.append(self.stream(lambda: stream_A(r), [0, 1]))
                if r >= 1:
                    lists.append(self.stream(lambda: self.gdn_B(r - 1, TB, TAd[(r - 1) % 2], rowb, colb, cegl), [2, 3, 4]))
                lists.append(self.stream(lambda: stream_C(r), [5, 6]))
                if r == 6 and b + 1 < nb:
                    lists.append(self.stream(lambda: load_x(b + 1), [0, 1]))
                self.merge(lists)
                self.stage_sel = None
            def out_proj():
                self.stage_sel = 0
                rot = TC[0:8]
                fetch(23, 2)

                def ld(dti):
                    self.dma(rot[dti % 8].v((slice(0, TM),)), self.xsv(dti, dti + 1, t0, t0 + TM), f"xr{dti % 8}")
                for dti in range(8):
                    ld(dti)
                for u in range(8):
                    slot = u % 3
                    for i in range(2):
                        dti = u * 2 + i
                        xt_ = rot[dti % 8].v((slice(0, TM),))
                        pso = self.ps()
                        self.mm(pso.v(), [(wb[slot].v((k, slice(i * P, i * P + P))), mT.v((k,))) for k in range(ND)])
                        self.tt(xt_, xt_, pso.v(), ALU.add)
                        self.dma(self.xsv(dti, dti + 1, t0, t0 + TM), xt_, f"xw{dti % 8}")
                        if dti + 8 < ND:
                            ld(dti + 8)
                    if u + 3 < 8:
                        fetch(21 + u + 3, slot)
                    elif b + 1 < nb:
                        fetch((3, 4, 0)[slot], slot)

            def next_head():
                self.stage_sel = 1
                fetch(2, 3)
                norm_rows()

            lists = [self.stream(out_proj, [0, 1, 2])]
            if b + 1 < nb:
                lists.append(self.stream(next_head, [3, 4, 5]))
            self.merge(lists)
            self.stage_sel = None
        assert cv_pos[0] == len(cv_seq)

    def pool_group(self, first, g, psu, TC):
        upad, la, lb, dd = TC[0], TC[1], TC[2], TC[3]
        mT, wsmall = self.mT, self.wsmall
        win = 2 ** (g + 1)
        T = TM
        self.copy(upad.v((slice(0, 16),)), self.car_p.v((g,)), eng="dve")
        self.act(upad.v((slice(16, 16 + T),)), psu.v(), AF.Copy)
        self.copy(self.car_p.v((g,)), upad.v((slice(T, T + 16),)), eng="dve")
        src = upad
        sh = 1
        bufs = [la, lb]
        for lev in range(g + 1):
            dst = bufs[lev % 2]
            self.tt(dst.v((slice(sh, 16 + T),)), src.v((slice(sh, 16 + T),)), src.v((slice(0, 16 + T - sh),)), ALU.add)
            src = dst
            sh *= 2
        self.stt(dd.v((slice(0, T),)), src.v((slice(16, 16 + T),)), 1.0 / win, upad.v((slice(16, 16 + T),)),
                 ALU.mult, ALU.subtract)
        if first:
            tmp = TC[4]
            self.tt(tmp.v((slice(0, 16),)), src.v((slice(16, 32),)), self.poolrc.v((g,)), ALU.mult)
            self.tt(dd.v((slice(0, 16),)), tmp.v((slice(0, 16),)), upad.v((slice(16, 32),)), ALU.subtract)
        psy = self.ps()
        self.mm(psy.v(), [(wsmall.v((g,)), dd.v((slice(0, T),)))])
        self.act(mT.v((g,)), psy.v(), AF.Identity, bias=self.pbs.v((slice(g, g + 1),)),
                 scale=self.cols.v((slice(CO_PS + g, CO_PS + g + 1),)))

    def gelu_tanh(self, out, x, t1, t2):
        self.act(t1, x, AF.Square)
        self.ts(t1, t1, 0.044715, ALU.mult, 1.0, ALU.add)
        self.tt(t1, t1, x, ALU.mult)
        self.act(t2, t1, AF.Sigmoid, scale=1.5957691216057308)
        self.tt(out, t2, x, ALU.mult)

    def lru_block(self, seq_start, j, psx, psg, TC):
        T = TM
        sl = (slice(0, T),)
        pad, xc, r, i_, a, gt, t1, t2 = TC
        th = pad
        mT, wsmall = self.mT, self.wsmall
        self.act(gt.v(sl), psg.v(), AF.Copy)
        self.copy(pad.v((slice(0, 3),)), self.car_l.v((j,)), eng="dve")
        self.act(pad.v((slice(3, 3 + T),)), psx.v(), AF.Copy)
        self.copy(self.car_l.v((j,)), pad.v((slice(T, T + 3),)), eng="dve")
        c = lambda tap: self.cols.v((slice(CO_LCW + tap * NLB + j, CO_LCW + tap * NLB + j + 1),))
        xcv = xc.v(sl)
        self.act(xcv, psx.v(), AF.Copy, scale=c(3), bias=self.cols.v((slice(CO_LCB + j, CO_LCB + j + 1),)))
        for tap in (2, 1, 0):
            self.stt(xcv, pad.v((slice(tap, tap + T),)), c(tap), xcv, ALU.mult, ALU.add)
        psr = self.ps()
        self.mm(psr.v(), [(wsmall.v((4 + j,)), xcv)])
        psi = self.ps()
        self.mm(psi.v(), [(wsmall.v((10 + j,)), xcv)])
        self.act(r.v(sl), psr.v(), AF.Sigmoid, bias=self.cols.v((slice(CO_LBA + j, CO_LBA + j + 1),)))
        self.act(i_.v(sl), psi.v(), AF.Sigmoid, bias=self.cols.v((slice(CO_LBX + j, CO_LBX + j + 1),)))
        lc = self.lruc.v((slice(j, j + 1),))
        self.act(a.v(sl), r.v(sl), AF.Exp, scale=lc)
        self.act(th.v(sl), r.v(sl), AF.Tanh, scale=lc)
        self.tt(t1.v(sl), a.v(sl), a.v(sl), ALU.mult)
        self.stt(t1.v(sl), t1.v(sl), 1.0, th.v(sl), ALU.add, ALU.mult)
        self.act(t1.v(sl), t1.v(sl), AF.Sqrt, scale=-1.0)
        if seq_start:
            self.memset(t1.v((slice(0, 1),)), 1.0)
        self.tt(t1.v(sl), t1.v(sl), i_.v(sl), ALU.mult)
        self.tt(t1.v(sl), t1.v(sl), xcv, ALU.mult)
        hv = r.v(sl)
        self.scan(hv, a.v(sl), t1.v(sl), self.hst.v((slice(j, j + 1),)), ALU.mult, ALU.add)
        self.copy(self.hst.v((slice(j, j + 1),)), r.v((slice(T - 1, T),)), eng="dve")
        self.gelu_tanh(gt.v(sl), gt.v(sl), t2.v(sl), i_.v(sl))
        self.tt(mT.v((10 + j,)), hv, gt.v(sl), ALU.mult)

    def gdn_rows(self, psa, psb, rowb, colb, cegl, glast):
        T = TM
        R = lambda i: rowb.v((i,))
        beta, gc, bg, egl, t1 = R(0), R(1), R(2), R(3), R(4)
        p6 = (0, NH)
        self.act(beta, psb.v(), AF.Sigmoid)
        self.act(t1, psa.v(), AF.Exp, bias=self.hrow.v((slice(1, 2),)))
        self.act(t1, t1, AF.Ln, bias=1.0)
        self.ts(t1, t1, self.negA.v(), ALU.mult)
        ones6 = self.ones.v((slice(0, CH),), p=p6)
        for c in range(T // CH):
            cs = slice(c * CH, (c + 1) * CH)
            self.scan(rowb.v((1, cs)), ones6, rowb.v((4, cs)), 0.0, ALU.mult, ALU.add)
            self.copy(glast.v((slice(c, c + 1),)), rowb.v((1, slice(c * CH + CH - 1, (c + 1) * CH))))
        self.act(t1, gc, AF.Exp)
        self.tt(bg, beta, t1, ALU.mult)
        for c in range(T // CH):
            cs = slice(c * CH, (c + 1) * CH)
            self.act(rowb.v((3, cs)), rowb.v((1, cs)), AF.Exp, scale=-1.0, bias=glast.v((slice(c, c + 1),)))
        self.act(glast.v((slice(4, 8),)), glast.v((slice(0, 4),)), AF.Exp)
        psc = self.ps((4, 4, NH))
        id6 = self.ident.v((slice(0, NH),), p=p6)
        for c in range(T // CH):
            cs = slice(c * CH, (c + 1) * CH)
            for qi, row in enumerate((1, 0, 2, 3)):
                self.mm(psc.v((c, qi)), [(rowb.v((row, cs)), id6)])
        self.copy(colb.v(), psc.v())
        pse = self.ps((NH, 4))
        for h in range(NH):
            self.mm(pse.v((h,)), [(self.sel6.v((slice(h * P, (h + 1) * P),)), glast.v((slice(4, 8),)))])
        self.copy(cegl.v(), pse.v(), eng="act")

    def gdn_conv(self, psx, tile_i, pad, cv, out):
        T = TM
        car = self.car_g.v((tile_i,))
        self.copy(pad.v((slice(0, 3),)), car, eng="dve")
        self.act(pad.v((slice(3, 3 + T),)), psx.v(), AF.Copy)
        self.copy(car, pad.v((slice(T, T + 3),)), eng="dve")
        c = lambda tap: self.cols.v((slice(CO_GCW + tap * 18 + tile_i, CO_GCW + tap * 18 + tile_i + 1),))
        cvv = cv.v((slice(0, T),))
        self.act(cvv, psx.v(), AF.Copy, scale=c(3))
        for tap in (2, 1, 0):
            self.stt(cvv, pad.v((slice(tap, tap + T),)), c(tap), cvv, ALU.mult, ALU.add)
        self.act(out.v((slice(0, T),)), cvv, AF.Silu)

    def gdn_B(self, h, TB, TAq, rowb, colb, cegl):
        T = TM
        NCk = T // CH
        sl = (slice(0, T),)
        mT = self.mT
        qn, kn, vs, zs = TAq
        Rg, Rb, Re, Dm, EU, EL, EUs, kbg, kd, vb = TB
        Nb = [EL, Rg]
        Pb = [EUs, Rb]
        Q = Dm
        selh = self.sel6.v((slice(h * P, (h + 1) * P),))
        for (row, dst, eng) in ((1, Rg, "act"), (0, Rb, "dve")):
            psr = self.ps()
            self.mm(psr.v(), [(selh, rowb.v((row,)))])
            self.copy(dst.v(sl), psr.v(), eng=eng)
        self.act(Re.v(sl), Rg.v(sl), AF.Exp)

        def v3(buf):
            vv = buf.v(sl)
            return View(vv.ap.rearrange("p (c f) -> p c f", c=NCk), vv.regs)

        def colq(qi):
            vv = colb.v((slice(None), qi, slice(h, h + 1)))
            return View(vv.ap.broadcast_to([P, NCk, CH]), vv.regs)

        def bcm(m):
            vv = m.v()
            return View(vv.ap.rearrange("p (o f) -> p o f", o=1).broadcast_to([P, NCk, CH]), vv.regs)
        self.tt(v3(Dm), v3(Rg), colq(0), ALU.subtract)
        self.stt(v3(EU), v3(Dm), 0.0, bcm(self.nmU), ALU.min, ALU.add)
        self.act(EU.v(sl), EU.v(sl), AF.Exp)
        self.stt(v3(EL), v3(Dm), 0.0, bcm(self.pmL), ALU.max, ALU.add)
        self.act(EL.v(sl), EL.v(sl), AF.Exp, scale=-1.0)
        self.tt(v3(EUs), v3(EU), bcm(self.mU01), ALU.mult)
        self.tt(EUs.v(sl), EUs.v(sl), Rb.v(sl), ALU.mult)
        self.tt(v3(EL), v3(EL), colq(1), ALU.mult)
        qd = Re
        self.tt(qd.v(sl), qn.v(sl), Re.v(sl), ALU.mult)
        pkk = self.ps()
        pqk = self.ps()
        for c in range(NCk):
            cs = (slice(c * CH, (c + 1) * CH),)
            self.mm(pkk.v(cs), [(kn.v(cs), kn.v(cs))])
            self.mm(pqk.v(cs), [(kn.v(cs), qn.v(cs))])
        N0, P0, attnT = EL, EUs, EU
        self.stt(P0.v(sl), pkk.v(), -1.0, EUs.v(sl), ALU.mult, ALU.mult)
        self.stt(N0.v(sl), pkk.v(), -1.0, EL.v(sl), ALU.mult, ALU.mult)
        self.tt(attnT.v(sl), pqk.v(), EU.v(sl), ALU.mult)
        self.tt(v3(Q), v3(P0), bcm(self.ident), ALU.add)
        ptk = self.ps()
        ptv = self.ps()
        for c in range(NCk):
            cs = (slice(c * CH, (c + 1) * CH),)
            self.tr(ptk.v(cs), kn.v(cs))
            self.tr(ptv.v(cs), vs.v(cs))
        ptk3 = View(ptk.v().ap.rearrange("p (c f) -> p c f", c=NCk), ptk.v().regs)
        ptv3 = View(ptv.v().ap.rearrange("p (c f) -> p c f", c=NCk), ptv.v().regs)
        self.tt(v3(kbg), ptk3, colq(2), ALU.mult)
        self.tt(v3(kd), ptk3, colq(3), ALU.mult)
        self.tt(v3(vb), ptv3, colq(1), ALU.mult)
        for k in range(1, 7):
            Np, Pp, Nn, Pn = Nb[(k - 1) % 2], Pb[(k - 1) % 2], Nb[k % 2], Pb[k % 2]
            pn = self.ps()
            for c in range(NCk):
                cs = (slice(c * CH, (c + 1) * CH),)
                self.mm(pn.v(cs), [(Pp.v(cs), Np.v(cs))])
            self.copy(Nn.v(sl), pn.v(), eng="act")
            if k < 6:
                pp = self.ps()
                for c in range(NCk):
                    cs = (slice(c * CH, (c + 1) * CH),)
                    self.mm(pp.v(cs), [(Np.v(cs), Pp.v(cs))])
                self.copy(Pn.v(sl), pp.v(), eng="dve")
            pq = self.ps()
            for c in range(NCk):
                cs = (slice(c * CH, (c + 1) * CH),)
                self.mm(pq.v(cs), [(Nn.v(cs), Q.v(cs))])
            self.tt(Q.v(sl), Q.v(sl), pq.v(), ALU.add)
        WT, U = Nb[0], Pb[0]
        pw = self.ps()
        pu = self.ps()
        for c in range(NCk):
            cs = (slice(c * CH, (c + 1) * CH),)
            self.mm(pw.v(cs), [(kbg.v(cs), Q.v(cs))])
            self.mm(pu.v(cs), [(Q.v(cs), vb.v(cs))])
        self.copy(WT.v(sl), pw.v(), eng="act")
        self.copy(U.v(sl), pu.v(), eng="dve")
        Sh = self.Sst.v((h,))
        vnew = Nb[1]
        pso = self.ps(bank=7)
        for c in range(NCk):
            cs = (slice(c * CH, (c + 1) * CH),)
            pv = self.ps((CH,))
            self.mm(pv.v(), [(WT.v(cs), Sh)])
            self.tt(vnew.v(cs), U.v(cs), pv.v(), ALU.subtract)
            self.mm(pso.v(cs), [(Sh, qd.v(cs)), (vnew.v(cs), attnT.v(cs))])
            pss_ = self.ps((CH,))
            self.mm(pss_.v(), [(kd.v(cs), vnew.v(cs))])
            self.stt(Sh, Sh, cegl.v((h, slice(c, c + 1))), pss_.v(), ALU.mult, ALU.add)
        osq, rs, on = Dm, Rb, kbg
        self.act(osq.v(sl), pso.v(), AF.Square)
        pss2 = self.ps()
        self.mm(pss2.v(), [(self.ones.v(), osq.v(sl))])
        self.rsqrt(rs.v(sl), pss2.v(), 1.0 / P, EPS)
        self.stt(on.v(sl), pso.v(), self.cols.v((slice(CO_GNW, CO_GNW + 1),)), rs.v(sl), ALU.mult, ALU.mult)
        self.tt(mT.v((4 + h,)), on.v(sl), zs.v(sl), ALU.mult)

    def f_pass(self, l):
        cfg = self.cfg
        S, NF, L = cfg.S, cfg.NF, cfg.depth
        self.sb_top = self.sb_persist
        xT = self.sb(F32, (ND, TF))
        hT = self.sb(BF16, (ND, TF))
        NWU = 4
        wup = [self.sb(BF16, (ND, 256)) for _ in range(NWU)]
        wdn = [self.sb(BF16, (2, D)) for _ in range(4)]
        actb = [self.sb(BF16, (4, TF)) for _ in range(2)]
        self.car_f = self.sb(F32, (NF, 2))
        self.memset(self.car_f.v(), 0.0, eng="pool")
        mark = self.sb_top
        otok = self.sb(F32, (D,))
        self.sb_top = mark
        gbuf = [self.sb(F32, (2 + TF,)) for _ in range(2)]
        tb = [self.sb(F32, (512,)) for _ in range(3)]
        wlu, wld = self.ffn_up[l], self.ffn_down[l]
        ws = (self.wsF, "wsF")
        last = (l == L - 1)
        nb = S // TF
        NCH = NF // 4
        for b in range(nb):
            first = False
            t0 = b * TF
            tsl = slice(t0, t0 + TF)
            self.dma(xT.v(), self.xsv(0, ND, t0, t0 + TF), "xTf")
            rstd = gbuf[0]
            self.rmsnorm(xT, TF, CO_N2, lambda j, hh: hT.v((j, slice(hh * 512, hh * 512 + 512))), [tb[0], tb[1]], rstd)
            self.urr = 0
            self.drr = 0

            def up_chunk(ci):
                ab = actb[ci % 2]
                for pr in range(2):
                    f0 = ci * 4 + pr * 2
                    ug = wup[self.urr % NWU]
                    self.urr += 1
                    self.fetch_unit(first, ug, ws, (f0 // 2) * 2, self.kunit_pieces(wlu, [(f0 * P, 256)], ug),
                                    f"wu{(self.urr - 1) % NWU}")
                    uv = wup[self.urr % NWU]
                    self.urr += 1
                    self.fetch_unit(first, uv, ws, (f0 // 2) * 2 + 1,
                                    self.kunit_pieces(wlu, [(cfg.d_ff + f0 * P, 256)], uv), f"wu{(self.urr - 1) % NWU}")
                    for i in range(2):
                        f = f0 + i
                        gb = gbuf[f % 2]
                        self.copy(gb.v((slice(0, 2),)), self.car_f.v((f,)), eng="pool")
                        for hh in range(TF // 512):
                            hs = slice(hh * 512, hh * 512 + 512)
                            psg = self.ps()
                            self.mm(psg.v(), [(ug.v((k, slice(i * P, i * P + P))), hT.v((k, hs))) for k in range(ND)])
                            psv = self.ps()
                            self.mm(psv.v(), [(uv.v((k, slice(i * P, i * P + P))), hT.v((k, hs))) for k in range(ND)])
                            self.act(gb.v((slice(2 + hh * 512, 2 + hh * 512 + 512),)), psg.v(), AF.Copy)
                            cw = lambda tap: self.cols.v((slice(CO_FCW + tap * NF + f, CO_FCW + tap * NF + f + 1),))
                            t = tb[0].v()
                            self.act(t, psg.v(), AF.Copy, scale=cw(2))
                            self.stt(t, gb.v((slice(1 + hh * 512, 1 + hh * 512 + 512),)), cw(1), t, ALU.mult, ALU.add)
                            self.stt(t, gb.v((slice(hh * 512, hh * 512 + 512),)), cw(0), t, ALU.mult, ALU.add, eng="pool")
                            self.gelu_tanh(t, t, tb[1].v(), tb[2].v())
                            self.tt(ab.v((pr * 2 + i, hs)), t, psv.v(), ALU.mult)
                        self.copy(self.car_f.v((f,)), gb.v((slice(TF, TF + 2),)), eng="pool")

            def down_chunk(ci):
                ab = actb[ci % 2]
                wd = []
                for pr in range(2):
                    dstb = wdn[self.drr % 4]
                    self.drr += 1
                    u = NF + ci * 2 + pr
                    pieces = []
                    for i in range(2):
                        f = ci * 4 + pr * 2 + i
                        srcv = self.dview("w", wld[f * P:(f + 1) * P, :], 0, 1)
                        pieces.append((srcv, dstb.v((i,)), lambda sb_: sb_.v()))
                    self.fetch_unit(first, dstb, ws, u, pieces, f"wd{(self.drr - 1) % 4}")
                    wd.append(dstb)
                for dti in range(ND):
                    for hh in range(TF // 512):
                        hs = slice(hh * 512, hh * 512 + 512)
                        pso = self.ps()
                        self.mm(pso.v(), [(wd[q // 2].v((q % 2, slice(dti * P, dti * P + P))), ab.v((q, hs)))
                                          for q in range(4)])
                        xv = xT.v((dti, hs))
                        self.tt(xv, xv, pso.v(), ALU.add)

            up_chunk(0)
            for ci in range(1, NCH):
                up_chunk(ci)
                down_chunk(ci - 1)
            down_chunk(NCH - 1)
            if not last:
                self.dma(self.xsv(0, ND, t0, t0 + TF), xT.v(), "xTfo")
            else:
                rstd = gbuf[0]
                self.rmsnorm(xT, TF, CO_NF, lambda j, hh: xT.v((j, slice(hh * 512, hh * 512 + 512))),
                             [tb[0], tb[1]], rstd)
                for tt_ in range(TF // P):
                    for g4 in range(4):
                        pst = self.ps()
                        for q in range(4):
                            self.tr(pst.v((slice(q * P, (q + 1) * P),)), xT.v((g4 * 4 + q, slice(tt_ * P, (tt_ + 1) * P))))
                        self.copy(otok.v((slice(g4 * 512, g4 * 512 + 512),)), pst.v(), eng=("act" if g4 % 2 else "dve"))
                    self.dma(self.dview("y", self.y_out[t0 + tt_ * P:t0 + (tt_ + 1) * P, :], t0 + tt_, t0 + tt_ + 1),
                             otok.v(), "otok")


def make_consts():
    c = np.zeros((P, 6 * P + NH * P + 64), np.float32)
    p = np.arange(P)[:, None]
    f = np.arange(P)[None, :]
    c[:, 0:P] = np.eye(P)
    c[:, P:2 * P] = 1.0
    c[:, 2 * P:3 * P] = np.where(f >= p, 0.0, NEG)
    c[:, 3 * P:4 * P] = np.where(p > f, 0.0, -NEG)
    c[:, 4 * P:5 * P] = np.where(f > p, 1.0, 0.0)
    for h in range(NH):
        c[h, 6 * P + h * P:6 * P + (h + 1) * P] = 1.0
    for g in range(4):
        win = 2 ** (g + 1)
        t = np.arange(16)
        c[:, 6 * P + NH * P + g * 16:6 * P + NH * P + (g + 1) * 16] = (1.0 / np.minimum(t + 1, win))[None, :]
    return c


def make_cols(inp, L, d_ff):
    NF = d_ff // P
    cols = np.zeros((L, P, n_cols(d_ff)), np.float32)
    col = lambda v: np.ascontiguousarray(np.asarray(v, np.float32).reshape(-1, P).T)
    for l in range(L):
        c = cols[l]
        c[:, CO_N1:CO_N1 + 16] = col(inp["norm1_w"][l])
        c[:, CO_N2:CO_N2 + 16] = col(inp["norm2_w"][l])
        c[:, CO_NF:CO_NF + 16] = col(inp["final_norm_w"])
        c[:, CO_PB:CO_PB + 4] = col(inp["pool_b"][l])
        c[:, CO_PS:CO_PS + 4] = col(inp["pool_scale"][l])
        gcw = np.asarray(inp["gdn_conv_w"][l], np.float32)
        for tap in range(4):
            t18 = col(gcw[tap])
            for h in range(NH):
                for qi in range(3):
                    c[:, CO_GCW + tap * 18 + 3 * h + qi] = t18[:, qi * NH + h]
        c[:, CO_GNW] = np.asarray(inp["gdn_norm_w"][l], np.float32)
        lcw = np.asarray(inp["lru_conv_w"][l], np.float32)
        for tap in range(4):
            c[:, CO_LCW + tap * NLB:CO_LCW + (tap + 1) * NLB] = col(lcw[tap])
        c[:, CO_LCB:CO_LCB + NLB] = col(inp["lru_conv_b"][l])
        c[:, CO_LBA:CO_LBA + NLB] = col(inp["lru_ba"][l])
        c[:, CO_LBX:CO_LBX + NLB] = col(inp["lru_bx"][l])
        c[:, CO_LLAM:CO_LLAM + NLB] = col(inp["lru_lambda"][l])
        fcw = np.asarray(inp["ffn_conv_w"][l], np.float32)
        for tap in range(3):
            c[:, CO_FCW + tap * NF:CO_FCW + (tap + 1) * NF] = col(fcw[tap])
    return cols


_NC_CACHE = {}
SPREAD = True


def run(inp, S, L, d_ff, n_cores):
    cfg = Cfg(S, L, d_ff)
    keyc = (S, L, d_ff)
    if keyc not in _NC_CACHE:
        _NC_CACHE[keyc] = K(cfg).build()
    nc = _NC_CACHE[keyc]
    f = lambda a: np.ascontiguousarray(np.asarray(a, np.float32))
    hrow = np.stack([f(inp["gdn_a_log"]), f(inp["gdn_dt_bias"])], axis=-1)
    shared = {
        "w_in": f(inp["w_in"]), "w_out": f(inp["w_out"]), "ffn_up": f(inp["ffn_up"]), "ffn_down": f(inp["ffn_down"]),
        "pool_w": f(inp["pool_w"]), "lru_wa": f(inp["lru_wa"]), "lru_wx": f(inp["lru_wx"]),
        "cols": make_cols(inp, L, d_ff), "hrow": np.ascontiguousarray(hrow), "consts": make_consts(),
    }
    x = f(inp["x"])
    if n_cores == 4 and SPREAD:
        slots = [0, 1, 4, 5]
        zero = {k: np.zeros_like(v) for k, v in shared.items()}
        zero["x"] = np.zeros_like(x[0])
        in_maps = [zero] * 8
        in_maps = list(in_maps)
        for i, sl_ in enumerate(slots):
            in_maps[sl_] = dict(shared, x=np.ascontiguousarray(x[i]))
        res = run_bass_kernel_spmd(nc, in_maps, core_ids=list(range(8)))
        return np.stack([np.asarray(res.results[sl_]["y"], np.float32) for sl_ in slots], axis=0)
    in_maps = [dict(shared, x=np.ascontiguousarray(x[i])) for i in range(n_cores)]
    res = run_bass_kernel_spmd(nc, in_maps, core_ids=list(range(n_cores)))
    return np.stack([np.asarray(res.results[i]["y"], np.float32) for i in range(n_cores)], axis=0)


def kernel(**inputs):
    x = np.asarray(inputs["x"])
    B, S, _ = x.shape
    L = np.asarray(inputs["w_in"]).shape[0]
    d_ff = np.asarray(inputs["ffn_down"]).shape[1]
    return run(inputs, S, L, d_ff, B)
```

```python
import numpy as np
from contextlib import ExitStack
import concourse.bass as bass
import concourse.mybir as mybir
from concourse.bass_utils import run_bass_kernel_spmd

F32 = mybir.dt.float32
BF16 = mybir.dt.bfloat16
AF = mybir.ActivationFunctionType
ALU = mybir.AluOpType
P = 128
EPS = 1e-6
NEG = -30000.0

D = 2048
ND = D // P
POOL_W = 512
GDN_W = 768
NH = 6
LRU_W = 768
NLB = 6
IN_COLS = POOL_W + 4 * GDN_W + 2 * NH + 2 * LRU_W
C_POOL, C_Q, C_K, C_V, C_Z = 0, 512, 1280, 2048, 2816
C_A, C_B, C_XR, C_GR = 3584, 3590, 3596, 4364
TM = 512
TF = 1024
CH = 128

CO_N1, CO_N2, CO_NF, CO_PB, CO_PS, CO_GCW, CO_GNW, CO_LCW, CO_LCB, CO_LBA, CO_LBX, CO_LLAM = (
    0, 16, 32, 48, 52, 56, 128, 129, 153, 159, 165, 171)
CO_FCW = 177


def n_cols(d_ff):
    return CO_FCW + 3 * (d_ff // P)


class View:
    __slots__ = ("ap", "regs")

    def __init__(self, ap, regs):
        self.ap = ap
        self.regs = regs


class Buf:
    def __init__(self, k, space, off, dtype, shape, np_=P):
        self.k, self.space, self.off, self.dtype, self.shape, self.np = k, space, off, dtype, tuple(shape), np_
        self.esz = 4 if dtype == F32 else 2
        n = 1
        for s in shape:
            n *= s
        self.n = n
        self.nbytes = n * self.esz
        if space == "sb":
            w0 = off // 4
            base = k.arena[0:np_, w0:w0 + (self.nbytes + 3) // 4]
            if dtype != F32:
                base = base.bitcast(dtype)
        else:
            bank = off // 2048
            w0 = (off % 2048) // 4
            base = k.psum[bank][0:np_, w0:w0 + (self.nbytes + 3) // 4]
            if dtype != F32:
                base = base.bitcast(dtype)
        if len(shape) == 2:
            base = base.rearrange("p (a b) -> p a b", a=shape[0])
        elif len(shape) == 3:
            base = base.rearrange("p (a b c) -> p a b c", a=shape[0], b=shape[1])
        self.base = base

    def __getitem__(self, idx):
        if not isinstance(idx, tuple):
            idx = (idx,)
        return self.v(idx)

    def v(self, idx=(), p=None):
        shape = self.shape
        idx = tuple(idx) + (slice(None),) * (len(shape) - len(idx))
        lo = 0
        hi = 0
        stride = self.n
        for s, i in zip(shape, idx):
            stride //= s
            if isinstance(i, int):
                a, b = i, i + 1
            else:
                a = 0 if i.start is None else i.start
                b = s if i.stop is None else i.stop
                assert i.step is None
            assert 0 <= a < b <= s, (shape, idx)
            lo += a * stride
            hi += (b - 1) * stride
        hi += 1
        p0, p1 = (0, self.np) if p is None else p
        ap = self.base[(slice(p0, p1),) + idx]
        return View(ap, [(self.space, self.off + lo * self.esz, self.off + hi * self.esz)])


class Op:
    __slots__ = ("eng", "fn", "deps", "seq", "key", "val", "prev_val", "i")


ENGS = ("pe", "act", "dve", "pool", "sp")
EPOCH = 12000
BUCKET = 512


class Rec:
    def __init__(self):
        self.ops = []
        self.by_eng = {e: [] for e in ENGS}
        self.wr = {}
        self.rd = {}
        self.dma_val = {}

    def _buckets(self, sp, lo, hi):
        return [(sp, b) for b in range(lo // BUCKET, (hi - 1) // BUCKET + 1)]

    def add(self, eng, fn, reads, writes, key=None, ndma=1):
        op = Op()
        op.eng, op.fn, op.key, op.i = eng, fn, key, len(self.ops)
        deps = set()
        rregs = [r for v in reads if v is not None and isinstance(v, View) for r in v.regs]
        wregs = [r for v in writes for r in v.regs]
        for (sp, lo, hi) in rregs:
            for bk in self._buckets(sp, lo, hi):
                for (a, b, o) in self.wr.get(bk, ()):
                    if a < hi and lo < b:
                        deps.add(o)
        for (sp, lo, hi) in wregs:
            for bk in self._buckets(sp, lo, hi):
                for (a, b, o) in self.wr.get(bk, ()):
                    if a < hi and lo < b:
                        deps.add(o)
                for (a, b, _e), o in self.rd.get(bk, {}).items():
                    if a < hi and lo < b:
                        deps.add(o)
        deps.discard(op.i)
        op.deps = deps
        if key is not None:
            pv = self.dma_val.get(key, 0)
            op.prev_val = pv
            op.val = pv + 16 * ndma
            self.dma_val[key] = op.val
            op.seq = None
        else:
            op.seq = len(self.by_eng[eng])
        for (sp, lo, hi) in wregs:
            for bk in self._buckets(sp, lo, hi):
                blo, bhi = bk[1] * BUCKET, (bk[1] + 1) * BUCKET
                l = self.wr.setdefault(bk, [])
                l[:] = [(a, b, o) for (a, b, o) in l if not (lo <= max(a, blo) and min(b, bhi) <= hi)]
                l.append((lo, hi, op.i))
                d = self.rd.get(bk)
                if d:
                    for kk in [kk for kk in d if lo <= max(kk[0], blo) and min(kk[1], bhi) <= hi]:
                        del d[kk]
        ek = eng if key is None else ("dma", op.i)
        for (sp, lo, hi) in rregs:
            for bk in self._buckets(sp, lo, hi):
                self.rd.setdefault(bk, {})[(lo, hi, ek)] = op.i
        self.ops.append(op)
        self.by_eng[eng].append(op)
        return op

    def emit(self, nc, stack):
        esem = {}
        for e in ("pe", "act", "dve", "pool"):
            n = len(self.by_eng[e])
            esem[e] = [stack.enter_context(nc.semaphore(f"s_{e}{i}")) for i in range(n // EPOCH + 1)]
        dsem = {k: stack.enter_context(nc.semaphore(f"d_{k}")) for k in self.dma_val}
        ops = self.ops
        block = stack.enter_context(nc.Block())

        def run(eng, e):
            waited = {}

            def wait(sem, val, tag):
                if waited.get(tag, 0) < val:
                    e.wait_ge(sem, val)
                    waited[tag] = val

            for op in self.by_eng[eng]:
                for di in sorted(op.deps):
                    d = ops[di]
                    if d.key is not None:
                        wait(dsem[d.key], d.val, ("d", d.key))
                    else:
                        if d.eng == "pe" and eng == "pe":
                            continue
                        ep = d.seq // EPOCH
                        wait(esem[d.eng][ep], d.seq % EPOCH + 1, (d.eng, ep))
                if op.key is not None:
                    if op.prev_val:
                        wait(dsem[op.key], op.prev_val, ("d", op.key))
                    op.fn(e, dsem[op.key])
                else:
                    ins = op.fn(e)
                    ins.then_inc(esem[eng][op.seq // EPOCH], 1)
            if eng == "sp":
                for k, v in self.dma_val.items():
                    wait(dsem[k], v, ("d", k))

        @block.tensor
        def _(e):
            run("pe", e)

        @block.scalar
        def _(e):
            run("act", e)

        @block.vector
        def _(e):
            run("dve", e)

        @block.gpsimd
        def _(e):
            run("pool", e)

        @block.sync
        def _(e):
            run("sp", e)


class Cfg:
    def __init__(self, S=4096, depth=2, d_ff=6144):
        self.S, self.depth, self.d_ff = S, depth, d_ff
        self.NF = d_ff // P
        assert S % TF == 0 and self.NF % 4 == 0


class K:
    def __init__(self, cfg):
        self.cfg = cfg
        self.nc = bass.Bass("TRN2", target_bir_lowering=False)
        self.rec = Rec()
        self.cur = None
        self.stage_sel = None
        self.banks = list(range(7))
        self.bank_i = 0

    def _emit(self, eng, fn, reads, writes, key=None):
        if self.cur is not None:
            self.cur.append((eng, fn, reads, writes, key))
        else:
            self.rec.add(eng, fn, reads, writes, key=key)

    def stream(self, fn, banks):
        assert self.cur is None
        save = (self.banks, self.bank_i)
        self.cur, self.banks, self.bank_i = [], banks, 0
        fn()
        out = self.cur
        self.cur = None
        self.banks, self.bank_i = save
        return out

    def _cost(self, a):
        eng, fn, reads, writes, key = a
        n = 1
        for d in writes[0].ap.shape[1:]:
            n *= d
        if key is not None:
            return 2.0 + n * 128 * 4 / 300e3
        if eng == "pe":
            c = getattr(fn, "cost", None)
            return c if c is not None else 0.2
        if eng == "dve":
            return 0.12 + n / 960.0
        if eng == "act":
            return 0.15 + n / 1100.0
        return 0.2 + n * 0.0035

    def merge(self, lists):
        lists = [l for l in lists if l]
        deps = []
        for l in lists:
            d = []
            wr, rd = [], []
            for i, a in enumerate(l):
                rr = [r for v in a[2] if isinstance(v, View) for r in v.regs]
                ww = [r for v in a[3] for r in v.regs]
                s_ = set()
                for (sp, lo, hi) in rr:
                    for (sp2, a2, b2, o) in wr:
                        if sp == sp2 and a2 < hi and lo < b2:
                            s_.add(o)
                for (sp, lo, hi) in ww:
                    for (sp2, a2, b2, o) in wr:
                        if sp == sp2 and a2 < hi and lo < b2:
                            s_.add(o)
                    for (sp2, a2, b2, o) in rd:
                        if sp == sp2 and a2 < hi and lo < b2:
                            s_.add(o)
                for (sp, lo, hi) in ww:
                    wr = [w for w in wr if not (w[0] == sp and lo <= w[1] and w[2] <= hi)]
                    rd = [w for w in rd if not (w[0] == sp and lo <= w[1] and w[2] <= hi)]
                    wr.append((sp, lo, hi, i))
                for (sp, lo, hi) in rr:
                    rd.append((sp, lo, hi, i))
                if len(rd) > 200:
                    rd = rd[-200:]
                d.append(s_)
            deps.append(d)
        pos = [0] * len(lists)
        fin = [[0.0] * len(l) for l in lists]
        costs = [[self._cost(a) for a in l] for l in lists]
        rem = []
        for cl in costs:
            r_, acc = [0.0] * (len(cl) + 1), 0.0
            for i in range(len(cl) - 1, -1, -1):
                acc += cl[i]
                r_[i] = acc
            rem.append(r_)
        efree = {e: 0.0 for e in ENGS}
        while True:
            cands = []
            for i, l in enumerate(lists):
                if pos[i] < len(l):
                    a = l[pos[i]]
                    rdy = 0.0
                    for o in deps[i][pos[i]]:
                        t = fin[i][o] + 0.3
                        if t > rdy:
                            rdy = t
                    cands.append((max(rdy, efree[a[0]]), i))
            if not cands:
                break
            tmin = min(c_[0] for c_ in cands)
            best, bi = None, -1
            for (st, i) in cands:
                if st <= tmin + 0.25:
                    if bi < 0 or rem[i][pos[i]] > rem[bi][pos[bi]]:
                        best, bi = st, i
            a = lists[bi][pos[bi]]
            c = costs[bi][pos[bi]]
            if a[4] is not None:
                efree[a[0]] = best + 0.05
            else:
                efree[a[0]] = best + c
            fin[bi][pos[bi]] = best + c
            pos[bi] += 1
            self.rec.add(a[0], a[1], a[2], a[3], key=a[4])

    def A(self, v):
        return v.ap if isinstance(v, View) else v

    def mm(self, out, pairs, start=True, stop=True):
        reads = [x for pr in pairs for x in pr]

        def fn(e):
            ins = None
            n = len(pairs)
            for i, (l, r) in enumerate(pairs):
                ins = e.matmul(out.ap, l.ap, r.ap, start=(start and i == 0), stop=(stop and i == n - 1))
            return ins
        cols = 1
        for d in pairs[0][1].ap.shape[1:]:
            cols *= d
        passes = 4 if pairs[0][0].ap.dtype == F32 else 1
        fn.cost = len(pairs) * (0.03 + cols * passes / 2400.0 * (0.6 if passes == 4 else 1.0))
        self._emit("pe", fn, reads + ([] if start else [out]), [out])

    def tr(self, out, in_):
        np_ = in_.ap.shape[0]
        idv = self.ident.v((slice(0, np_),), p=(0, np_))
        self._emit("pe", lambda e: e.transpose(out.ap, in_.ap, idv.ap), [in_, idv], [out])

    def act(self, out, in_, func, bias=None, scale=None, eng="act"):
        kw = {}
        if func == AF.Copy and (bias is not None or scale is not None):
            func = AF.Identity
        if bias is not None:
            kw["bias"] = self.A(bias)
        if scale is not None:
            kw["scale"] = self.A(scale)
        self._emit("act", lambda e: e.activation(out=out.ap, in_=in_.ap, func=func, **kw),
                     [in_, bias, scale], [out])

    def tt(self, out, in0, in1, op, eng="dve"):
        self._emit(eng, lambda e: e.tensor_tensor(out.ap, in0.ap, in1.ap, op), [in0, in1], [out])

    def ts(self, out, in0, s1, op0, s2=None, op1=None, eng="dve"):
        if op1 is None:
            fn = lambda e: e.tensor_scalar(out.ap, in0.ap, self.A(s1), None, op0)
        else:
            fn = lambda e: e.tensor_scalar(out.ap, in0.ap, self.A(s1), self.A(s2), op0, op1)
        self._emit(eng, fn, [in0, s1, s2], [out])

    def stt(self, out, in0, sc, in1, op0, op1, eng="dve"):
        eng = "dve"
        self._emit(eng, lambda e: e.scalar_tensor_tensor(out.ap, in0.ap, self.A(sc), in1.ap, op0, op1),
                     [in0, sc, in1], [out])

    def rsqrt(self, out, in_, mul, add):
        self.ts(out, in_, mul, ALU.mult, add, ALU.add)
        self.act(out, out, AF.Ln)
        self.act(out, out, AF.Exp, scale=-0.5)

    def scan(self, out, d0, d1, init, op0, op1, eng="dve"):
        eng = "dve"
        self._emit(eng, lambda e: e.tensor_tensor_scan(out.ap, d0.ap, d1.ap, self.A(init), op0, op1),
                     [d0, d1, init], [out])

    def copy(self, out, in_, eng="dve"):
        if eng == "act":
            self.act(out, in_, AF.Copy)
        else:
            self._emit(eng, lambda e: e.tensor_copy(out.ap, in_.ap), [in_], [out])

    def memset(self, out, val, eng="dve"):
        self._emit(eng, lambda e: e.memset(out.ap, val), [], [out])

    def dma(self, out, in_, key, eng="sp"):
        self._emit(eng, lambda e, sem: e.dma_start(out=out.ap, in_=in_.ap).then_inc(sem, 16),
                     [in_], [out], key=key)

    def xsv(self, d0, d1, t0, t1):
        if d1 - d0 == 1:
            ap = self.xs[:, d0, t0:t1]
        else:
            ap = self.xs[:, d0:d1, t0:t1]
        return View(ap, [("dr:xs%d" % d, t0, t1) for d in range(d0, d1)])

    def dview(self, name, ap, lo, hi):
        return View(ap, [("dr:" + name, lo, hi)])

    def sb(self, dtype, shape, np_=P):
        esz = 4 if dtype == F32 else 2
        n = int(np.prod(shape)) * esz
        n = (n + 31) // 32 * 32
        off = self.sb_top
        self.sb_top += n
        assert self.sb_top <= self.ARENA_BYTES, ("SBUF arena overflow", self.sb_top)
        return Buf(self, "sb", off, dtype, shape, np_)

    def ps(self, shape=(512,), np_=P, bank=None):
        if bank is None:
            bank = self.banks[self.bank_i % len(self.banks)]
            self.bank_i += 1
        return Buf(self, "ps", bank * 2048, F32, shape, np_)

    def build(self):
        cfg, nc = self.cfg, self.nc
        S, L, NF = cfg.S, cfg.depth, cfg.NF
        NCOL = n_cols(cfg.d_ff)
        dt = nc.dram_tensor
        self.x_in = dt("x", [S, D], F32, kind="ExternalInput").ap()
        self.w_in = dt("w_in", [L, D, IN_COLS], F32, kind="ExternalInput").ap()
        self.w_out = dt("w_out", [L, D, D], F32, kind="ExternalInput").ap()
        self.ffn_up = dt("ffn_up", [L, D, 2 * cfg.d_ff], F32, kind="ExternalInput").ap()
        self.ffn_down = dt("ffn_down", [L, cfg.d_ff, D], F32, kind="ExternalInput").ap()
        self.pool_w = dt("pool_w", [L, 4, P, P], F32, kind="ExternalInput").ap()
        self.lru_wa = dt("lru_wa", [L, NLB, P, P], F32, kind="ExternalInput").ap()
        self.lru_wx = dt("lru_wx", [L, NLB, P, P], F32, kind="ExternalInput").ap()
        self.cols_d = dt("cols", [L, P, NCOL], F32, kind="ExternalInput").ap()
        self.hrow_d = dt("hrow", [L, NH, 2], F32, kind="ExternalInput").ap()
        self.const_d = dt("consts", [P, 6 * P + NH * P + 64], F32, kind="ExternalInput").ap()
        self.y_out = dt("y", [S, D], F32, kind="ExternalOutput").ap()
        self.xs = dt("xs", [P, ND, S], F32).ap()
        self.n_units_M = 21 + 8
        self.n_units_F = NF // 2 * 2 + NF // 2
        self.wsM = dt("wsM", [self.n_units_M, P, 4096], BF16).ap()
        self.wsF = dt("wsF", [self.n_units_F, P, 4096], BF16).ap()

        self.ARENA_BYTES = 207 * 1024
        with ExitStack() as st:
            self.arena = st.enter_context(nc.sbuf_tensor("arena", [P, self.ARENA_BYTES // 4], F32))
            self.psum = [st.enter_context(nc.psum_tensor(f"psb{i}", [P, 512], F32)) for i in range(8)]
            self.sb_top = 0
            self.ident = self.sb(F32, (P,))
            self.ones = self.sb(F32, (P,))
            cd = self.const_d
            self.dma(self.ident.v(), self.dview("c", cd[:, 0:P], 0, 1), "c0")
            self.dma(self.ones.v(), self.dview("c", cd[:, P:2 * P], 0, 1), "c0")
            self.cols = self.sb(F32, (NCOL,))
            self.hrow = self.sb(F32, (2,), np_=NH)
            self.negA = self.sb(F32, (1,), np_=NH)
            self.lruc = self.sb(F32, (NLB,))
            self.pbs = self.sb(F32, (4,))
            self.sb_persist = self.sb_top
            for l in range(L):
                self.layer_setup(l)
                self.m_pass(l)
                self.f_pass(l)
            self.rec.emit(nc, st)
        return nc

    def layer_setup(self, l):
        self.sb_top = self.sb_persist
        t6 = self.sb(F32, (8,), np_=NH)
        tl = self.sb(F32, (NLB,))
        self.dma(self.cols.v(), self.dview("cols", self.cols_d[l], l, l + 1), "cols")
        self.dma(self.hrow.v(), self.dview("hrow", self.hrow_d[l], l, l + 1), "cols")
        self.act(t6.v((slice(0, 1),)), self.hrow.v((slice(0, 1),)), AF.Exp)
        self.ts(self.negA.v(), t6.v((slice(0, 1),)), -1.0, ALU.mult)
        lam = self.cols.v((slice(CO_LLAM, CO_LLAM + NLB),))
        self.act(tl.v(), lam, AF.Exp, scale=-1.0)
        self.act(tl.v(), tl.v(), AF.Ln, bias=1.0)
        self.ts(self.lruc.v(), tl.v(), -8.0, ALU.mult)
        self.tt(self.pbs.v(), self.cols.v((slice(CO_PB, CO_PB + 4),)), self.cols.v((slice(CO_PS, CO_PS + 4),)), ALU.mult)

    def fetch_unit(self, first, dst, ws, uidx, pieces, key):
        if first:
            for (src, dstv, stv) in pieces:
                if self.stage_sel is not None:
                    sidx = self.stage_sel
                else:
                    sidx = self.stage_rr
                    self.stage_rr = (self.stage_rr + 1) % len(self.stage)
                sv = stv(self.stage[sidx])
                self.dma(sv, src, f"stg{sidx}")
                self.copy(dstv, sv, eng=self.cast_engs[self.cast_rr % len(self.cast_engs)])
                self.cast_rr += 1
            self.dma(self.dview(ws[1], ws[0][uidx], uidx, uidx + 1), View(dst.base.rearrange("p a b -> p (a b)") if len(dst.shape) == 2 else dst.base, dst.v().regs), key + "o")
        else:
            self.dma(View(dst.base.rearrange("p a b -> p (a b)") if len(dst.shape) == 2 else dst.base, dst.v().regs),
                     self.dview(ws[1], ws[0][uidx], uidx, uidx + 1), key)

    def kunit_pieces(self, wl, cols, dst):
        pieces = []
        off = 0
        for (c0, w) in cols:
            for kh in range(2):
                src = wl[kh * 1024:(kh + 1) * 1024, c0:c0 + w].rearrange("(j p) c -> p j c", p=P)
                srcv = self.dview("w", src, 0, 1)
                dstv = dst.v((slice(kh * 8, kh * 8 + 8), slice(off, off + w)))
                pieces.append((srcv, dstv, (lambda w_: (lambda sb_: View(
                    sb_.base[:, 0:8 * w_].rearrange("p (j c) -> p j c", j=8), sb_.v().regs)))(w)))
            off += w
        return pieces

    def rmsnorm(self, xT, T, wcol0, out_fn, sq, rstd):
        for hh in range(T // 512):
            ts_ = slice(hh * 512, hh * 512 + 512)
            pss = self.ps()
            for j in range(ND):
                s = sq[j % 2].v((slice(0, 512),))
                self.act(s, xT.v((j, ts_)), AF.Square)
                self.mm(pss.v(), [(self.ones.v(), s)], start=(j == 0), stop=(j == ND - 1))
            r = rstd.v((ts_,))
            self.rsqrt(r, pss.v(), 1.0 / D, EPS)
            for j in range(ND):
                self.stt(out_fn(j, hh), xT.v((j, ts_)), self.cols.v((slice(wcol0 + j, wcol0 + j + 1),)), r,
                         ALU.mult, ALU.mult, eng=("dve" if j % 2 == 0 else "pool"))

    def m_pass(self, l):
        cfg = self.cfg
        S = cfg.S
        T = TM
        self.sb_top = self.sb_persist
        xT = self.sb(F32, (ND, TM))
        hT = self.sb(BF16, (ND, TM))
        mT = self.sb(BF16, (ND, TM))
        self.mT = mT
        self.stage = [self.sb(F32, (2048,)) for _ in range(2)]
        self.stage_rr = 0
        self.cast_engs = ["dve", "act"]
        self.cast_rr = 0
        wb = [self.sb(BF16, (ND, 256)) for _ in range(4)]
        wsmall = self.sb(F32, (16, P))
        self.wsmall = wsmall
        cd = self.const_d
        self.nmU = self.sb(F32, (P,))
        self.pmL = self.sb(F32, (P,))
        self.mU01 = self.sb(F32, (P,))
        self.sel6 = self.sb(F32, (NH * P,), np_=NH)
        self.poolrc = self.sb(F32, (4, 16))
        self.dma(self.nmU.v(), self.dview("c", cd[:, 2 * P:3 * P], 0, 1), "c0")
        self.dma(self.pmL.v(), self.dview("c", cd[:, 3 * P:4 * P], 0, 1), "c0")
        self.dma(self.mU01.v(), self.dview("c", cd[:, 4 * P:5 * P], 0, 1), "c0")
        self.dma(self.sel6.v(), self.dview("c", cd[0:NH, 6 * P:6 * P + NH * P], 0, 1), "c0")
        self.dma(self.poolrc.v(), self.dview("c", cd[:, 6 * P + NH * P:6 * P + NH * P + 64].rearrange(
            "p (a b) -> p a b", a=4), 0, 1), "c0")
        self.Sst = self.sb(F32, (NH, P))
        self.hst = self.sb(F32, (NLB,))
        self.car_g = self.sb(F32, (18, 3))
        self.car_l = self.sb(F32, (NLB, 3))
        self.car_p = self.sb(F32, (4, 16))
        for b_ in (self.Sst, self.hst, self.car_g, self.car_l, self.car_p):
            self.memset(b_.v(), 0.0, eng="pool")
        wt = lambda: self.sb(F32, (TM + 16,))
        TB = [wt() for _ in range(10)]
        TA = [wt() for _ in range(2)]
        TAd = [[wt() for _ in range(4)] for _ in range(2)]
        TC = [wt() for _ in range(8)]
        xtok = Buf(self, "sb", TA[0].off, F32, (D,))
        assert TAd[0][3].off + TAd[0][3].nbytes - TA[0].off >= D * 4
        rowb = self.sb(F32, (5, TM), np_=NH)
        cvo = [self.sb(BF16, (2048,)) for _ in range(2)]
        NF = cfg.NF
        wlu, wld = self.ffn_up[l], self.ffn_down[l]
        cv_pieces = []
        for pi in range(NF // 2):
            for gv in range(2):
                for kh in range(2):
                    c0 = gv * cfg.d_ff + pi * 256
                    src = wlu[kh * 1024:(kh + 1) * 1024, c0:c0 + 256].rearrange("(j p) c -> p j c", p=P)
                    cv_pieces.append((src, pi * 2 + gv, kh, True))
        for q in range(NF // 2):
            for i in range(2):
                f = q * 2 + i
                cv_pieces.append((wld[f * P:(f + 1) * P, :], NF + q, i, False))
        cv_pos = [0]
        NPc = len(cv_pieces)
        cv_seq = []
        for i in range(NPc + 2):
            if i < NPc:
                cv_seq.append(("in", i))
            if 1 <= i <= NPc:
                cv_seq.append(("cast", i - 1))
            if 2 <= i <= NPc + 1:
                cv_seq.append(("out", i - 2))
        colb = self.sb(F32, (4, 4, NH))
        cegl = self.sb(F32, (NH, 4))
        glast = self.sb(F32, (8,), np_=NH)
        for g in range(4):
            self.dma(wsmall.v((g,)), self.dview("pw", self.pool_w[l, g], 0, 1), "wsm")
        for j in range(NLB):
            self.dma(wsmall.v((4 + j,)), self.dview("pw", self.lru_wa[l, j], 0, 1), "wsm")
            self.dma(wsmall.v((10 + j,)), self.dview("pw", self.lru_wx[l, j], 0, 1), "wsm")
        wl_in, wl_out = self.w_in[l], self.w_out[l]
        ws = (self.wsM, "wsM")
        units = [[(C_POOL, 128), (C_POOL + 128, 128)], [(C_POOL + 256, 128), (C_POOL + 384, 128)], [(C_A, 12)]]
        for h in range(NH):
            units.append([(C_Q + h * P, P), (C_K + h * P, P)])
            units.append([(C_V + h * P, P), (C_Z + h * P, P)])
        for j in range(NLB):
            units.append([(C_XR + j * P, P), (C_GR + j * P, P)])
        nb = S // TM
        assert nb >= 2
        for b in range(nb):
            first = (b == 0)
            t0 = b * TM
            tsl = slice(t0, t0 + TM)

            def fetch(u, slot):
                if u < 21:
                    pcs = self.kunit_pieces(wl_in, units[u], wb[slot])
                else:
                    pcs = self.kunit_pieces(wl_out, [((u - 21) * 256, 256)], wb[slot])
                self.fetch_unit(first, wb[slot], ws, u, pcs, f"wb{slot}")

            def proj(slot, off, width):
                pso = self.ps((TM,), np_=width)
                self.mm(pso.v(), [(wb[slot].v((k, slice(off, off + width))), hT.v((k,))) for k in range(ND)])
                return pso
            def load_x(bb):
                tb0 = bb * TM
                if l == 0:
                    for tt_ in range(TM // P):
                        self.dma(xtok.v(), self.dview("x", self.x_in[tb0 + tt_ * P:tb0 + (tt_ + 1) * P, :], 0, 1), "xtok")
                        for g4 in range(4):
                            pst = self.ps((4, P))
                            for q in range(4):
                                self.tr(pst.v((q,)), xtok.v((slice((g4 * 4 + q) * P, (g4 * 4 + q + 1) * P),)))
                            self.copy(xT.v((slice(g4 * 4, g4 * 4 + 4), slice(tt_ * P, (tt_ + 1) * P))), pst.v(),
                                      eng=("act" if g4 % 2 else "dve"))
                    self.dma(self.xsv(0, ND, tb0, tb0 + TM), xT.v(), "xTo")
                else:
                    self.dma(xT.v(), self.xsv(0, ND, tb0, tb0 + TM), "xT")

            def norm_rows():
                self.rmsnorm(xT, TM, CO_N1, lambda j, hh: hT.v((j,)), [TB[0], TB[1]], TB[2])
                psa = proj(3, 0, NH)
                psb = proj(3, NH, NH)
                self.gdn_rows(psa, psb, rowb, colb, cegl, glast)

            if b == 0:
                fetch(2, 3)
                fetch(3, 0)
                fetch(4, 1)
                fetch(0, 2)
                load_x(0)
                norm_rows()

            def stream_A(h):
                self.stage_sel = 0
                pad, cv = TA
                qn, kn, vs, zs = TAd[h % 2]
                for (slot, off, ti, dst) in ((0, 0, 0, qn), (0, P, 1, kn), (1, 0, 2, vs)):
                    psx = proj(slot, off, P)
                    self.gdn_conv(psx, 3 * h + ti, pad, cv, dst)
                psz = proj(1, P, P)
                self.act(zs.v((slice(0, T),)), psz.v(), AF.Silu)
                if h + 1 < NH:
                    fetch(3 + 2 * (h + 1), 0)
                    fetch(4 + 2 * (h + 1), 1)
                sl = (slice(0, T),)
                sq, rn = pad, cv
                for (buf, scl) in ((kn, None), (qn, P ** -0.5)):
                    self.act(sq.v(sl), buf.v(sl), AF.Square)
                    pss = self.ps()
                    self.mm(pss.v(), [(self.ones.v(), sq.v(sl))])
                    self.rsqrt(rn.v(sl), pss.v(), 1.0, EPS)
                    if scl is None:
                        self.tt(buf.v(sl), buf.v(sl), rn.v(sl), ALU.mult)
                    else:
                        self.stt(buf.v(sl), buf.v(sl), scl, rn.v(sl), ALU.mult, ALU.mult)

            c_units = [0, 1, 15, 16, 17, 18, 19, 20]
            c_slot = lambda i: 2 + (i % 2)

            def stream_C(r):
                self.stage_sel = 1
                steps = [r] if r < 6 else [6, 7]
                for i in steps:
                    if i + 1 < len(c_units):
                        fetch(c_units[i + 1], c_slot(i + 1))
                    slot = c_slot(i)
                    if i < 2:
                        for q in range(2):
                            self.pool_group(first, i * 2 + q, proj(slot, q * P, P), TC)
                    else:
                        j = i - 2
                        psx = proj(slot, 0, P)
                        psg = proj(slot, P, P)
                        self.lru_block(b == 0, j, psx, psg, TC)
                if r == 6:
                    fetch(21, 0)
                    fetch(22, 1)

            def cv_op(kind, i):
                src, u, hf, is_up = cv_pieces[i]
                stg, ob = self.stage[i % 2], cvo[i % 2]
                if is_up:
                    sv = View(stg.base.rearrange("p (j c) -> p j c", j=8), stg.v().regs)
                    ov = View(ob.base.rearrange("p (j c) -> p j c", j=8), ob.v().regs)
                else:
                    sv, ov = stg.v(), ob.v()
                if kind == "in":
                    self.dma(sv, self.dview("w", src, 0, 1), f"cvi{i % 2}")
                elif kind == "cast":
                    self.copy(ov, sv, eng="pool")
                else:
                    self.dma(self.dview("wsF", self.wsF[u][:, hf * 2048:(hf + 1) * 2048], u, u + 1), ob.v(),
                             f"cvo{i % 2}")

            def stream_D(n):
                for _ in range(n):
                    if cv_pos[0] >= len(cv_seq):
                        return
                    kind, i = cv_seq[cv_pos[0]]
                    cv_pos[0] += 1
                    cv_op(kind, i)

            n_cv = -(-len(cv_seq) // (7 * (nb - 1)))
            for r in range(7):
                lists = []
                if b >= 1:
                    lists.append(self.stream(lambda: stream_D(n_cv), []))
                if r < NH:
                    lists.append(self.stream(lambda: stream_A(r), [0, 1]))
                if r >= 1:
                    lists.append(self.stream(lambda: self.gdn_B(r - 1, TB, TAd[(r - 1) % 2], rowb, colb, cegl), [2, 3, 4]))
                lists.append(self.stream(lambda: stream_C(r), [5, 6]))
                if r == 6 and b + 1 < nb:
                    lists.append(self.stream(lambda: load_x(b + 1), [0, 1]))
                self.merge(lists)
                self.stage_sel = None
            def out_proj():
                self.stage_sel = 0
                rot = TC[0:8]
                fetch(23, 2)

                def ld(dti):
                    self.dma(rot[dti % 8].v((slice(0, TM),)), self.xsv(dti, dti + 1, t0, t0 + TM), f"xr{dti % 8}")
                for dti in range(8):
                    ld(dti)
                for u in range(8):
                    slot = u % 3
                    for i in range(2):
                        dti = u * 2 + i
                        xt_ = rot[dti % 8].v((slice(0, TM),))
                        pso = self.ps()
                        self.mm(pso.v(), [(wb[slot].v((k, slice(i * P, i * P + P))), mT.v((k,))) for k in range(ND)])
                        self.tt(xt_, xt_, pso.v(), ALU.add)
                        self.dma(self.xsv(dti, dti + 1, t0, t0 + TM), xt_, f"xw{dti % 8}")
                        if dti + 8 < ND:
                            ld(dti + 8)
                    if u + 3 < 8:
                        fetch(21 + u + 3, slot)
                    elif b + 1 < nb:
                        fetch((3, 4, 0)[slot], slot)

            def next_head():
                self.stage_sel = 1
                fetch(2, 3)
                norm_rows()

            lists = [self.stream(out_proj, [0, 1, 2])]
            if b + 1 < nb:
                lists.append(self.stream(next_head, [3, 4, 5]))
            self.merge(lists)
            self.stage_sel = None
        assert cv_pos[0] == len(cv_seq)

    def pool_group(self, first, g, psu, TC):
        upad, la, lb, dd = TC[0], TC[1], TC[2], TC[3]
        mT, wsmall = self.mT, self.wsmall
        win = 2 ** (g + 1)
        T = TM
        self.copy(upad.v((slice(0, 16),)), self.car_p.v((g,)), eng="dve")
        self.act(upad.v((slice(16, 16 + T),)), psu.v(), AF.Copy)
        self.copy(self.car_p.v((g,)), upad.v((slice(T, T + 16),)), eng="dve")
        src = upad
        sh = 1
        bufs = [la, lb]
        for lev in range(g + 1):
            dst = bufs[lev % 2]
            self.tt(dst.v((slice(sh, 16 + T),)), src.v((slice(sh, 16 + T),)), src.v((slice(0, 16 + T - sh),)), ALU.add)
            src = dst
            sh *= 2
        self.stt(dd.v((slice(0, T),)), src.v((slice(16, 16 + T),)), 1.0 / win, upad.v((slice(16, 16 + T),)),
                 ALU.mult, ALU.subtract)
        if first:
            tmp = TC[4]
            self.tt(tmp.v((slice(0, 16),)), src.v((slice(16, 32),)), self.poolrc.v((g,)), ALU.mult)
            self.tt(dd.v((slice(0, 16),)), tmp.v((slice(0, 16),)), upad.v((slice(16, 32),)), ALU.subtract)
        psy = self.ps()
        self.mm(psy.v(), [(wsmall.v((g,)), dd.v((slice(0, T),)))])
        self.act(mT.v((g,)), psy.v(), AF.Identity, bias=self.pbs.v((slice(g, g + 1),)),
                 scale=self.cols.v((slice(CO_PS + g, CO_PS + g + 1),)))

    def gelu_tanh(self, out, x, t1, t2):
        self.act(t1, x, AF.Square)
        self.ts(t1, t1, 0.044715, ALU.mult, 1.0, ALU.add)
        self.tt(t1, t1, x, ALU.mult)
        self.act(t2, t1, AF.Sigmoid, scale=1.5957691216057308)
        self.tt(out, t2, x, ALU.mult)

    def lru_block(self, seq_start, j, psx, psg, TC):
        T = TM
        sl = (slice(0, T),)
        pad, xc, r, i_, a, gt, t1, t2 = TC
        th = pad
        mT, wsmall = self.mT, self.wsmall
        self.act(gt.v(sl), psg.v(), AF.Copy)
        self.copy(pad.v((slice(0, 3),)), self.car_l.v((j,)), eng="dve")
        self.act(pad.v((slice(3, 3 + T),)), psx.v(), AF.Copy)
        self.copy(self.car_l.v((j,)), pad.v((slice(T, T + 3),)), eng="dve")
        c = lambda tap: self.cols.v((slice(CO_LCW + tap * NLB + j, CO_LCW + tap * NLB + j + 1),))
        xcv = xc.v(sl)
        self.act(xcv, psx.v(), AF.Copy, scale=c(3), bias=self.cols.v((slice(CO_LCB + j, CO_LCB + j + 1),)))
        for tap in (2, 1, 0):
            self.stt(xcv, pad.v((slice(tap, tap + T),)), c(tap), xcv, ALU.mult, ALU.add)
        psr = self.ps()
        self.mm(psr.v(), [(wsmall.v((4 + j,)), xcv)])
        psi = self.ps()
        self.mm(psi.v(), [(wsmall.v((10 + j,)), xcv)])
        self.act(r.v(sl), psr.v(), AF.Sigmoid, bias=self.cols.v((slice(CO_LBA + j, CO_LBA + j + 1),)))
        self.act(i_.v(sl), psi.v(), AF.Sigmoid, bias=self.cols.v((slice(CO_LBX + j, CO_LBX + j + 1),)))
        lc = self.lruc.v((slice(j, j + 1),))
        self.act(a.v(sl), r.v(sl), AF.Exp, scale=lc)
        self.act(th.v(sl), r.v(sl), AF.Tanh, scale=lc)
        self.tt(t1.v(sl), a.v(sl), a.v(sl), ALU.mult)
        self.stt(t1.v(sl), t1.v(sl), 1.0, th.v(sl), ALU.add, ALU.mult)
        self.act(t1.v(sl), t1.v(sl), AF.Sqrt, scale=-1.0)
        if seq_start:
            self.memset(t1.v((slice(0, 1),)), 1.0)
        self.tt(t1.v(sl), t1.v(sl), i_.v(sl), ALU.mult)
        self.tt(t1.v(sl), t1.v(sl), xcv, ALU.mult)
        hv = r.v(sl)
        self.scan(hv, a.v(sl), t1.v(sl), self.hst.v((slice(j, j + 1),)), ALU.mult, ALU.add)
        self.copy(self.hst.v((slice(j, j + 1),)), r.v((slice(T - 1, T),)), eng="dve")
        self.gelu_tanh(gt.v(sl), gt.v(sl), t2.v(sl), i_.v(sl))
        self.tt(mT.v((10 + j,)), hv, gt.v(sl), ALU.mult)

    def gdn_rows(self, psa, psb, rowb, colb, cegl, glast):
        T = TM
        R = lambda i: rowb.v((i,))
        beta, gc, bg, egl, t1 = R(0), R(1), R(2), R(3), R(4)
        p6 = (0, NH)
        self.act(beta, psb.v(), AF.Sigmoid)
        self.act(t1, psa.v(), AF.Exp, bias=self.hrow.v((slice(1, 2),)))
        self.act(t1, t1, AF.Ln, bias=1.0)
        self.ts(t1, t1, self.negA.v(), ALU.mult)
        ones6 = self.ones.v((slice(0, CH),), p=p6)
        for c in range(T // CH):
            cs = slice(c * CH, (c + 1) * CH)
            self.scan(rowb.v((1, cs)), ones6, rowb.v((4, cs)), 0.0, ALU.mult, ALU.add)
            self.copy(glast.v((slice(c, c + 1),)), rowb.v((1, slice(c * CH + CH - 1, (c + 1) * CH))))
        self.act(t1, gc, AF.Exp)
        self.tt(bg, beta, t1, ALU.mult)
        for c in range(T // CH):
            cs = slice(c * CH, (c + 1) * CH)
            self.act(rowb.v((3, cs)), rowb.v((1, cs)), AF.Exp, scale=-1.0, bias=glast.v((slice(c, c + 1),)))
        self.act(glast.v((slice(4, 8),)), glast.v((slice(0, 4),)), AF.Exp)
        psc = self.ps((4, 4, NH))
        id6 = self.ident.v((slice(0, NH),), p=p6)
        for c in range(T // CH):
            cs = slice(c * CH, (c + 1) * CH)
            for qi, row in enumerate((1, 0, 2, 3)):
                self.mm(psc.v((c, qi)), [(rowb.v((row, cs)), id6)])
        self.copy(colb.v(), psc.v())
        pse = self.ps((NH, 4))
        for h in range(NH):
            self.mm(pse.v((h,)), [(self.sel6.v((slice(h * P, (h + 1) * P),)), glast.v((slice(4, 8),)))])
        self.copy(cegl.v(), pse.v(), eng="act")

    def gdn_conv(self, psx, tile_i, pad, cv, out):
        T = TM
        car = self.car_g.v((tile_i,))
        self.copy(pad.v((slice(0, 3),)), car, eng="dve")
        self.act(pad.v((slice(3, 3 + T),)), psx.v(), AF.Copy)
        self.copy(car, pad.v((slice(T, T + 3),)), eng="dve")
        c = lambda tap: self.cols.v((slice(CO_GCW + tap * 18 + tile_i, CO_GCW + tap * 18 + tile_i + 1),))
        cvv = cv.v((slice(0, T),))
        self.act(cvv, psx.v(), AF.Copy, scale=c(3))
        for tap in (2, 1, 0):
            self.stt(cvv, pad.v((slice(tap, tap + T),)), c(tap), cvv, ALU.mult, ALU.add)
        self.act(out.v((slice(0, T),)), cvv, AF.Silu)

    def gdn_B(self, h, TB, TAq, rowb, colb, cegl):
        T = TM
        NCk = T // CH
        sl = (slice(0, T),)
        mT = self.mT
        qn, kn, vs, zs = TAq
        Rg, Rb, Re, Dm, EU, EL, EUs, kbg, kd, vb = TB
        Nb = [EL, Rg]
        Pb = [EUs, Rb]
        Q = Dm
        selh = self.sel6.v((slice(h * P, (h + 1) * P),))
        for (row, dst, eng) in ((1, Rg, "act"), (0, Rb, "dve")):
            psr = self.ps()
            self.mm(psr.v(), [(selh, rowb.v((row,)))])
            self.copy(dst.v(sl), psr.v(), eng=eng)
        self.act(Re.v(sl), Rg.v(sl), AF.Exp)

        def v3(buf):
            vv = buf.v(sl)
            return View(vv.ap.rearrange("p (c f) -> p c f", c=NCk), vv.regs)

        def colq(qi):
            vv = colb.v((slice(None), qi, slice(h, h + 1)))
            return View(vv.ap.broadcast_to([P, NCk, CH]), vv.regs)

        def bcm(m):
            vv = m.v()
            return View(vv.ap.rearrange("p (o f) -> p o f", o=1).broadcast_to([P, NCk, CH]), vv.regs)
        self.tt(v3(Dm), v3(Rg), colq(0), ALU.subtract)
        self.stt(v3(EU), v3(Dm), 0.0, bcm(self.nmU), ALU.min, ALU.add)
        self.act(EU.v(sl), EU.v(sl), AF.Exp)
        self.stt(v3(EL), v3(Dm), 0.0, bcm(self.pmL), ALU.max, ALU.add)
        self.act(EL.v(sl), EL.v(sl), AF.Exp, scale=-1.0)
        self.tt(v3(EUs), v3(EU), bcm(self.mU01), ALU.mult)
        self.tt(EUs.v(sl), EUs.v(sl), Rb.v(sl), ALU.mult)
        self.tt(v3(EL), v3(EL), colq(1), ALU.mult)
        qd = Re
        self.tt(qd.v(sl), qn.v(sl), Re.v(sl), ALU.mult)
        pkk = self.ps()
        pqk = self.ps()
        for c in range(NCk):
            cs = (slice(c * CH, (c + 1) * CH),)
            self.mm(pkk.v(cs), [(kn.v(cs), kn.v(cs))])
            self.mm(pqk.v(cs), [(kn.v(cs), qn.v(cs))])
        N0, P0, attnT = EL, EUs, EU
        self.stt(P0.v(sl), pkk.v(), -1.0, EUs.v(sl), ALU.mult, ALU.mult)
        self.stt(N0.v(sl), pkk.v(), -1.0, EL.v(sl), ALU.mult, ALU.mult)
        self.tt(attnT.v(sl), pqk.v(), EU.v(sl), ALU.mult)
        self.tt(v3(Q), v3(P0), bcm(self.ident), ALU.add)
        ptk = self.ps()
        ptv = self.ps()
        for c in range(NCk):
            cs = (slice(c * CH, (c + 1) * CH),)
            self.tr(ptk.v(cs), kn.v(cs))
            self.tr(ptv.v(cs), vs.v(cs))
        ptk3 = View(ptk.v().ap.rearrange("p (c f) -> p c f", c=NCk), ptk.v().regs)
        ptv3 = View(ptv.v().ap.rearrange("p (c f) -> p c f", c=NCk), ptv.v().regs)
        self.tt(v3(kbg), ptk3, colq(2), ALU.mult)
        self.tt(v3(kd), ptk3, colq(3), ALU.mult)
        self.tt(v3(vb), ptv3, colq(1), ALU.mult)
        for k in range(1, 7):
            Np, Pp, Nn, Pn = Nb[(k - 1) % 2], Pb[(k - 1) % 2], Nb[k % 2], Pb[k % 2]
            pn = self.ps()
            for c in range(NCk):
                cs = (slice(c * CH, (c + 1) * CH),)
                self.mm(pn.v(cs), [(Pp.v(cs), Np.v(cs))])
            self.copy(Nn.v(sl), pn.v(), eng="act")
            if k < 6:
                pp = self.ps()
                for c in range(NCk):
                    cs = (slice(c * CH, (c + 1) * CH),)
                    self.mm(pp.v(cs), [(Np.v(cs), Pp.v(cs))])
                self.copy(Pn.v(sl), pp.v(), eng="dve")
            pq = self.ps()
            for c in range(NCk):
                cs = (slice(c * CH, (c + 1) * CH),)
                self.mm(pq.v(cs), [(Nn.v(cs), Q.v(cs))])
            self.tt(Q.v(sl), Q.v(sl), pq.v(), ALU.add)
        WT, U = Nb[0], Pb[0]
        pw = self.ps()
        pu = self.ps()
        for c in range(NCk):
            cs = (slice(c * CH, (c + 1) * CH),)
            self.mm(pw.v(cs), [(kbg.v(cs), Q.v(cs))])
            self.mm(pu.v(cs), [(Q.v(cs), vb.v(cs))])
        self.copy(WT.v(sl), pw.v(), eng="act")
        self.copy(U.v(sl), pu.v(), eng="dve")
        Sh = self.Sst.v((h,))
        vnew = Nb[1]
        pso = self.ps(bank=7)
        for c in range(NCk):
            cs = (slice(c * CH, (c + 1) * CH),)
            pv = self.ps((CH,))
            self.mm(pv.v(), [(WT.v(cs), Sh)])
            self.tt(vnew.v(cs), U.v(cs), pv.v(), ALU.subtract)
            self.mm(pso.v(cs), [(Sh, qd.v(cs)), (vnew.v(cs), attnT.v(cs))])
            pss_ = self.ps((CH,))
            self.mm(pss_.v(), [(kd.v(cs), vnew.v(cs))])
            self.stt(Sh, Sh, cegl.v((h, slice(c, c + 1))), pss_.v(), ALU.mult, ALU.add)
        osq, rs, on = Dm, Rb, kbg
        self.act(osq.v(sl), pso.v(), AF.Square)
        pss2 = self.ps()
        self.mm(pss2.v(), [(self.ones.v(), osq.v(sl))])
        self.rsqrt(rs.v(sl), pss2.v(), 1.0 / P, EPS)
        self.stt(on.v(sl), pso.v(), self.cols.v((slice(CO_GNW, CO_GNW + 1),)), rs.v(sl), ALU.mult, ALU.mult)
        self.tt(mT.v((4 + h,)), on.v(sl), zs.v(sl), ALU.mult)

    def f_pass(self, l):
        cfg = self.cfg
        S, NF, L = cfg.S, cfg.NF, cfg.depth
        self.sb_top = self.sb_persist
        xT = self.sb(F32, (ND, TF))
        hT = self.sb(BF16, (ND, TF))
        NWU = 4
        wup = [self.sb(BF16, (ND, 256)) for _ in range(NWU)]
        wdn = [self.sb(BF16, (2, D)) for _ in range(4)]
        actb = [self.sb(BF16, (4, TF)) for _ in range(2)]
        self.car_f = self.sb(F32, (NF, 2))
        self.memset(self.car_f.v(), 0.0, eng="pool")
        mark = self.sb_top
        otok = self.sb(F32, (D,))
        self.sb_top = mark
        gbuf = [self.sb(F32, (2 + TF,)) for _ in range(2)]
        tb = [self.sb(F32, (512,)) for _ in range(3)]
        wlu, wld = self.ffn_up[l], self.ffn_down[l]
        ws = (self.wsF, "wsF")
        last = (l == L - 1)
        nb = S // TF
        NCH = NF // 4
        for b in range(nb):
            first = False
            t0 = b * TF
            tsl = slice(t0, t0 + TF)
            self.dma(xT.v(), self.xsv(0, ND, t0, t0 + TF), "xTf")
            rstd = gbuf[0]
            self.rmsnorm(xT, TF, CO_N2, lambda j, hh: hT.v((j, slice(hh * 512, hh * 512 + 512))), [tb[0], tb[1]], rstd)
            self.urr = 0
            self.drr = 0

            def up_chunk(ci):
                ab = actb[ci % 2]
                for pr in range(2):
                    f0 = ci * 4 + pr * 2
                    ug = wup[self.urr % NWU]
                    self.urr += 1
                    self.fetch_unit(first, ug, ws, (f0 // 2) * 2, self.kunit_pieces(wlu, [(f0 * P, 256)], ug),
                                    f"wu{(self.urr - 1) % NWU}")
                    uv = wup[self.urr % NWU]
                    self.urr += 1
                    self.fetch_unit(first, uv, ws, (f0 // 2) * 2 + 1,
                                    self.kunit_pieces(wlu, [(cfg.d_ff + f0 * P, 256)], uv), f"wu{(self.urr - 1) % NWU}")
                    for i in range(2):
                        f = f0 + i
                        gb = gbuf[f % 2]
                        self.copy(gb.v((slice(0, 2),)), self.car_f.v((f,)), eng="pool")
                        for hh in range(TF // 512):
                            hs = slice(hh * 512, hh * 512 + 512)
                            psg = self.ps()
                            self.mm(psg.v(), [(ug.v((k, slice(i * P, i * P + P))), hT.v((k, hs))) for k in range(ND)])
                            psv = self.ps()
                            self.mm(psv.v(), [(uv.v((k, slice(i * P, i * P + P))), hT.v((k, hs))) for k in range(ND)])
                            self.act(gb.v((slice(2 + hh * 512, 2 + hh * 512 + 512),)), psg.v(), AF.Copy)
                            cw = lambda tap: self.cols.v((slice(CO_FCW + tap * NF + f, CO_FCW + tap * NF + f + 1),))
                            t = tb[0].v()
                            self.act(t, psg.v(), AF.Copy, scale=cw(2))
                            self.stt(t, gb.v((slice(1 + hh * 512, 1 + hh * 512 + 512),)), cw(1), t, ALU.mult, ALU.add)
                            self.stt(t, gb.v((slice(hh * 512, hh * 512 + 512),)), cw(0), t, ALU.mult, ALU.add, eng="pool")
                            self.gelu_tanh(t, t, tb[1].v(), tb[2].v())
                            self.tt(ab.v((pr * 2 + i, hs)), t, psv.v(), ALU.mult)
                        self.copy(self.car_f.v((f,)), gb.v((slice(TF, TF + 2),)), eng="pool")

            def down_chunk(ci):
                ab = actb[ci % 2]
                wd = []
                for pr in range(2):
                    dstb = wdn[self.drr % 4]
                    self.drr += 1
                    u = NF + ci * 2 + pr
                    pieces = []
                    for i in range(2):
                        f = ci * 4 + pr * 2 + i
                        srcv = self.dview("w", wld[f * P:(f + 1) * P, :], 0, 1)
                        pieces.append((srcv, dstb.v((i,)), lambda sb_: sb_.v()))
                    self.fetch_unit(first, dstb, ws, u, pieces, f"wd{(self.drr - 1) % 4}")
                    wd.append(dstb)
                for dti in range(ND):
                    for hh in range(TF // 512):
                        hs = slice(hh * 512, hh * 512 + 512)
                        pso = self.ps()
                        self.mm(pso.v(), [(wd[q // 2].v((q % 2, slice(dti * P, dti * P + P))), ab.v((q, hs)))
                                          for q in range(4)])
                        xv = xT.v((dti, hs))
                        self.tt(xv, xv, pso.v(), ALU.add)

            up_chunk(0)
            for ci in range(1, NCH):
                up_chunk(ci)
                down_chunk(ci - 1)
            down_chunk(NCH - 1)
            if not last:
                self.dma(self.xsv(0, ND, t0, t0 + TF), xT.v(), "xTfo")
            else:
                rstd = gbuf[0]
                self.rmsnorm(xT, TF, CO_NF, lambda j, hh: xT.v((j, slice(hh * 512, hh * 512 + 512))),
                             [tb[0], tb[1]], rstd)
                for tt_ in range(TF // P):
                    for g4 in range(4):
                        pst = self.ps()
                        for q in range(4):
                            self.tr(pst.v((slice(q * P, (q + 1) * P),)), xT.v((g4 * 4 + q, slice(tt_ * P, (tt_ + 1) * P))))
                        self.copy(otok.v((slice(g4 * 512, g4 * 512 + 512),)), pst.v(), eng=("act" if g4 % 2 else "dve"))
                    self.dma(self.dview("y", self.y_out[t0 + tt_ * P:t0 + (tt_ + 1) * P, :], t0 + tt_, t0 + tt_ + 1),
                             otok.v(), "otok")


def make_consts():
    c = np.zeros((P, 6 * P + NH * P + 64), np.float32)
    p = np.arange(P)[:, None]
    f = np.arange(P)[None, :]
    c[:, 0:P] = np.eye(P)
    c[:, P:2 * P] = 1.0
    c[:, 2 * P:3 * P] = np.where(f >= p, 0.0, NEG)
    c[:, 3 * P:4 * P] = np.where(p > f, 0.0, -NEG)
    c[:, 4 * P:5 * P] = np.where(f > p, 1.0, 0.0)
    for h in range(NH):
        c[h, 6 * P + h * P:6 * P + (h + 1) * P] = 1.0
    for g in range(4):
        win = 2 ** (g + 1)
        t = np.arange(16)
        c[:, 6 * P + NH * P + g * 16:6 * P + NH * P + (g + 1) * 16] = (1.0 / np.minimum(t + 1, win))[None, :]
    return c


def make_cols(inp, L, d_ff):
    NF = d_ff // P
    cols = np.zeros((L, P, n_cols(d_ff)), np.float32)
    col = lambda v: np.ascontiguousarray(np.asarray(v, np.float32).reshape(-1, P).T)
    for l in range(L):
        c = cols[l]
        c[:, CO_N1:CO_N1 + 16] = col(inp["norm1_w"][l])
        c[:, CO_N2:CO_N2 + 16] = col(inp["norm2_w"][l])
        c[:, CO_NF:CO_NF + 16] = col(inp["final_norm_w"])
        c[:, CO_PB:CO_PB + 4] = col(inp["pool_b"][l])
        c[:, CO_PS:CO_PS + 4] = col(inp["pool_scale"][l])
        gcw = np.asarray(inp["gdn_conv_w"][l], np.float32)
        for tap in range(4):
            t18 = col(gcw[tap])
            for h in range(NH):
                for qi in range(3):
                    c[:, CO_GCW + tap * 18 + 3 * h + qi] = t18[:, qi * NH + h]
        c[:, CO_GNW] = np.asarray(inp["gdn_norm_w"][l], np.float32)
        lcw = np.asarray(inp["lru_conv_w"][l], np.float32)
        for tap in range(4):
            c[:, CO_LCW + tap * NLB:CO_LCW + (tap + 1) * NLB] = col(lcw[tap])
        c[:, CO_LCB:CO_LCB + NLB] = col(inp["lru_conv_b"][l])
        c[:, CO_LBA:CO_LBA + NLB] = col(inp["lru_ba"][l])
        c[:, CO_LBX:CO_LBX + NLB] = col(inp["lru_bx"][l])
        c[:, CO_LLAM:CO_LLAM + NLB] = col(inp["lru_lambda"][l])
        fcw = np.asarray(inp["ffn_conv_w"][l], np.float32)
        for tap in range(3):
            c[:, CO_FCW + tap * NF:CO_FCW + (tap + 1) * NF] = col(fcw[tap])
    return cols


_NC_CACHE = {}
SPREAD = True


def run(inp, S, L, d_ff, n_cores):
    cfg = Cfg(S, L, d_ff)
    keyc = (S, L, d_ff)
    if keyc not in _NC_CACHE:
        _NC_CACHE[keyc] = K(cfg).build()
    nc = _NC_CACHE[keyc]
    f = lambda a: np.ascontiguousarray(np.asarray(a, np.float32))
    hrow = np.stack([f(inp["gdn_a_log"]), f(inp["gdn_dt_bias"])], axis=-1)
    shared = {
        "w_in": f(inp["w_in"]), "w_out": f(inp["w_out"]), "ffn_up": f(inp["ffn_up"]), "ffn_down": f(inp["ffn_down"]),
        "pool_w": f(inp["pool_w"]), "lru_wa": f(inp["lru_wa"]), "lru_wx": f(inp["lru_wx"]),
        "cols": make_cols(inp, L, d_ff), "hrow": np.ascontiguousarray(hrow), "consts": make_consts(),
    }
    x = f(inp["x"])
    if n_cores == 4 and SPREAD:
        slots = [0, 1, 4, 5]
        zero = {k: np.zeros_like(v) for k, v in shared.items()}
        zero["x"] = np.zeros_like(x[0])
        in_maps = [zero] * 8
        in_maps = list(in_maps)
        for i, sl_ in enumerate(slots):
            in_maps[sl_] = dict(shared, x=np.ascontiguousarray(x[i]))
        res = run_bass_kernel_spmd(nc, in_maps, core_ids=list(range(8)))
        return np.stack([np.asarray(res.results[sl_]["y"], np.float32) for sl_ in slots], axis=0)
    in_maps = [dict(shared, x=np.ascontiguousarray(x[i])) for i in range(n_cores)]
    res = run_bass_kernel_spmd(nc, in_maps, core_ids=list(range(n_cores)))
    return np.stack([np.asarray(res.results[i]["y"], np.float32) for i in range(n_cores)], axis=0)


def kernel(**inputs):
    x = np.asarray(inputs["x"])
    B, S, _ = x.shape
    L = np.asarray(inputs["w_in"]).shape[0]
    d_ff = np.asarray(inputs["ffn_down"]).shape[1]
    return run(inputs, S, L, d_ff, B)
```

```python
import numpy as np
from contextlib import ExitStack
import concourse.bass as bass
import concourse.mybir as mybir
from concourse.bass_utils import run_bass_kernel_spmd

F32 = mybir.dt.float32
BF16 = mybir.dt.bfloat16
AF = mybir.ActivationFunctionType
ALU = mybir.AluOpType
P = 128
EPS = 1e-6
NEG = -30000.0

D = 2048
ND = D // P
POOL_W = 512
GDN_W = 768
NH = 6
LRU_W = 768
NLB = 6
IN_COLS = POOL_W + 4 * GDN_W + 2 * NH + 2 * LRU_W
C_POOL, C_Q, C_K, C_V, C_Z = 0, 512, 1280, 2048, 2816
C_A, C_B, C_XR, C_GR = 3584, 3590, 3596, 4364
TM = 512
TF = 1024
CH = 128

CO_N1, CO_N2, CO_NF, CO_PB, CO_PS, CO_GCW, CO_GNW, CO_LCW, CO_LCB, CO_LBA, CO_LBX, CO_LLAM = (
    0, 16, 32, 48, 52, 56, 128, 129, 153, 159, 165, 171)
CO_FCW = 177


def n_cols(d_ff):
    return CO_FCW + 3 * (d_ff // P)


class View:
    __slots__ = ("ap", "regs")

    def __init__(self, ap, regs):
        self.ap = ap
        self.regs = regs


class Buf:
    def __init__(self, k, space, off, dtype, shape, np_=P):
        self.k, self.space, self.off, self.dtype, self.shape, self.np = k, space, off, dtype, tuple(shape), np_
        self.esz = 4 if dtype == F32 else 2
        n = 1
        for s in shape:
            n *= s
        self.n = n
        self.nbytes = n * self.esz
        if space == "sb":
            w0 = off // 4
            base = k.arena[0:np_, w0:w0 + (self.nbytes + 3) // 4]
            if dtype != F32:
                base = base.bitcast(dtype)
        else:
            bank = off // 2048
            w0 = (off % 2048) // 4
            base = k.psum[bank][0:np_, w0:w0 + (self.nbytes + 3) // 4]
            if dtype != F32:
                base = base.bitcast(dtype)
        if len(shape) == 2:
            base = base.rearrange("p (a b) -> p a b", a=shape[0])
        elif len(shape) == 3:
            base = base.rearrange("p (a b c) -> p a b c", a=shape[0], b=shape[1])
        self.base = base

    def __getitem__(self, idx):
        if not isinstance(idx, tuple):
            idx = (idx,)
        return self.v(idx)

    def v(self, idx=(), p=None):
        shape = self.shape
        idx = tuple(idx) + (slice(None),) * (len(shape) - len(idx))
        lo = 0
        hi = 0
        stride = self.n
        for s, i in zip(shape, idx):
            stride //= s
            if isinstance(i, int):
                a, b = i, i + 1
            else:
                a = 0 if i.start is None else i.start
                b = s if i.stop is None else i.stop
                assert i.step is None
            assert 0 <= a < b <= s, (shape, idx)
            lo += a * stride
            hi += (b - 1) * stride
        hi += 1
        p0, p1 = (0, self.np) if p is None else p
        ap = self.base[(slice(p0, p1),) + idx]
        return View(ap, [(self.space, self.off + lo * self.esz, self.off + hi * self.esz)])


class Op:
    __slots__ = ("eng", "fn", "deps", "seq", "key", "val", "prev_val", "i")


ENGS = ("pe", "act", "dve", "pool", "sp")
EPOCH = 12000
BUCKET = 512


class Rec:
    def __init__(self):
        self.ops = []
        self.by_eng = {e: [] for e in ENGS}
        self.wr = {}
        self.rd = {}
        self.dma_val = {}

    def _buckets(self, sp, lo, hi):
        return [(sp, b) for b in range(lo // BUCKET, (hi - 1) // BUCKET + 1)]

    def add(self, eng, fn, reads, writes, key=None, ndma=1):
        op = Op()
        op.eng, op.fn, op.key, op.i = eng, fn, key, len(self.ops)
        deps = set()
        rregs = [r for v in reads if v is not None and isinstance(v, View) for r in v.regs]
        wregs = [r for v in writes for r in v.regs]
        for (sp, lo, hi) in rregs:
            for bk in self._buckets(sp, lo, hi):
                for (a, b, o) in self.wr.get(bk, ()):
                    if a < hi and lo < b:
                        deps.add(o)
        for (sp, lo, hi) in wregs:
            for bk in self._buckets(sp, lo, hi):
                for (a, b, o) in self.wr.get(bk, ()):
                    if a < hi and lo < b:
                        deps.add(o)
                for (a, b, _e), o in self.rd.get(bk, {}).items():
                    if a < hi and lo < b:
                        deps.add(o)
        deps.discard(op.i)
        op.deps = deps
        if key is not None:
            pv = self.dma_val.get(key, 0)
            op.prev_val = pv
            op.val = pv + 16 * ndma
            self.dma_val[key] = op.val
            op.seq = None
        else:
            op.seq = len(self.by_eng[eng])
        for (sp, lo, hi) in wregs:
            for bk in self._buckets(sp, lo, hi):
                blo, bhi = bk[1] * BUCKET, (bk[1] + 1) * BUCKET
                l = self.wr.setdefault(bk, [])
                l[:] = [(a, b, o) for (a, b, o) in l if not (lo <= max(a, blo) and min(b, bhi) <= hi)]
                l.append((lo, hi, op.i))
                d = self.rd.get(bk)
                if d:
                    for kk in [kk for kk in d if lo <= max(kk[0], blo) and min(kk[1], bhi) <= hi]:
                        del d[kk]
        ek = eng if key is None else ("dma", op.i)
        for (sp, lo, hi) in rregs:
            for bk in self._buckets(sp, lo, hi):
                self.rd.setdefault(bk, {})[(lo, hi, ek)] = op.i
        self.ops.append(op)
        self.by_eng[eng].append(op)
        return op

    def emit(self, nc, stack):
        esem = {}
        for e in ("pe", "act", "dve", "pool"):
            n = len(self.by_eng[e])
            esem[e] = [stack.enter_context(nc.semaphore(f"s_{e}{i}")) for i in range(n // EPOCH + 1)]
        dsem = {k: stack.enter_context(nc.semaphore(f"d_{k}")) for k in self.dma_val}
        ops = self.ops
        block = stack.enter_context(nc.Block())

        def run(eng, e):
            waited = {}

            def wait(sem, val, tag):
                if waited.get(tag, 0) < val:
                    e.wait_ge(sem, val)
                    waited[tag] = val

            for op in self.by_eng[eng]:
                for di in sorted(op.deps):
                    d = ops[di]
                    if d.key is not None:
                        wait(dsem[d.key], d.val, ("d", d.key))
                    else:
                        if d.eng == "pe" and eng == "pe":
                            continue
                        ep = d.seq // EPOCH
                        wait(esem[d.eng][ep], d.seq % EPOCH + 1, (d.eng, ep))
                if op.key is not None:
                    if op.prev_val:
                        wait(dsem[op.key], op.prev_val, ("d", op.key))
                    op.fn(e, dsem[op.key])
                else:
                    ins = op.fn(e)
                    ins.then_inc(esem[eng][op.seq // EPOCH], 1)
            if eng == "sp":
                for k, v in self.dma_val.items():
                    wait(dsem[k], v, ("d", k))

        @block.tensor
        def _(e):
            run("pe", e)

        @block.scalar
        def _(e):
            run("act", e)

        @block.vector
        def _(e):
            run("dve", e)

        @block.gpsimd
        def _(e):
            run("pool", e)

        @block.sync
        def _(e):
            run("sp", e)


class Cfg:
    def __init__(self, S=4096, depth=2, d_ff=6144):
        self.S, self.depth, self.d_ff = S, depth, d_ff
        self.NF = d_ff // P
        assert S % TF == 0 and self.NF % 4 == 0


class K:
    def __init__(self, cfg):
        self.cfg = cfg
        self.nc = bass.Bass("TRN2", target_bir_lowering=False)
        self.rec = Rec()
        self.cur = None
        self.stage_sel = None
        self.banks = list(range(7))
        self.bank_i = 0

    def _emit(self, eng, fn, reads, writes, key=None):
        if self.cur is not None:
            self.cur.append((eng, fn, reads, writes, key))
        else:
            self.rec.add(eng, fn, reads, writes, key=key)

    def stream(self, fn, banks):
        assert self.cur is None
        save = (self.banks, self.bank_i)
        self.cur, self.banks, self.bank_i = [], banks, 0
        fn()
        out = self.cur
        self.cur = None
        self.banks, self.bank_i = save
        return out

    def _cost(self, a):
        eng, fn, reads, writes, key = a
        n = 1
        for d in writes[0].ap.shape[1:]:
            n *= d
        if key is not None:
            return 2.0 + n * 128 * 4 / 300e3
        if eng == "pe":
            c = getattr(fn, "cost", None)
            return c if c is not None else 0.2
        if eng == "dve":
            return 0.12 + n / 960.0
        if eng == "act":
            return 0.15 + n / 1100.0
        return 0.2 + n * 0.0035

    def merge(self, lists):
        lists = [l for l in lists if l]
        deps = []
        for l in lists:
            d = []
            wr, rd = [], []
            for i, a in enumerate(l):
                rr = [r for v in a[2] if isinstance(v, View) for r in v.regs]
                ww = [r for v in a[3] for r in v.regs]
                s_ = set()
                for (sp, lo, hi) in rr:
                    for (sp2, a2, b2, o) in wr:
                        if sp == sp2 and a2 < hi and lo < b2:
                            s_.add(o)
                for (sp, lo, hi) in ww:
                    for (sp2, a2, b2, o) in wr:
                        if sp == sp2 and a2 < hi and lo < b2:
                            s_.add(o)
                    for (sp2, a2, b2, o) in rd:
                        if sp == sp2 and a2 < hi and lo < b2:
                            s_.add(o)
                for (sp, lo, hi) in ww:
                    wr = [w for w in wr if not (w[0] == sp and lo <= w[1] and w[2] <= hi)]
                    rd = [w for w in rd if not (w[0] == sp and lo <= w[1] and w[2] <= hi)]
                    wr.append((sp, lo, hi, i))
                for (sp, lo, hi) in rr:
                    rd.append((sp, lo, hi, i))
                if len(rd) > 200:
                    rd = rd[-200:]
                d.append(s_)
            deps.append(d)
        pos = [0] * len(lists)
        fin = [[0.0] * len(l) for l in lists]
        efree = {e: 0.0 for e in ENGS}
        while True:
            best, bi = None, -1
            for i, l in enumerate(lists):
                if pos[i] < len(l):
                    a = l[pos[i]]
                    rdy = 0.0
                    for o in deps[i][pos[i]]:
                        t = fin[i][o] + 0.3
                        if t > rdy:
                            rdy = t
                    st = max(rdy, efree[a[0]])
                    if best is None or st < best - 1e-9:
                        best, bi = st, i
            if bi < 0:
                break
            a = lists[bi][pos[bi]]
            c = self._cost(a)
            if a[4] is not None:
                efree[a[0]] = best + 0.05
            else:
                efree[a[0]] = best + c
            fin[bi][pos[bi]] = best + c
            pos[bi] += 1
            self.rec.add(a[0], a[1], a[2], a[3], key=a[4])

    def A(self, v):
        return v.ap if isinstance(v, View) else v

    def mm(self, out, pairs, start=True, stop=True):
        reads = [x for pr in pairs for x in pr]

        def fn(e):
            ins = None
            n = len(pairs)
            for i, (l, r) in enumerate(pairs):
                ins = e.matmul(out.ap, l.ap, r.ap, start=(start and i == 0), stop=(stop and i == n - 1))
            return ins
        cols = 1
        for d in pairs[0][1].ap.shape[1:]:
            cols *= d
        passes = 4 if pairs[0][0].ap.dtype == F32 else 1
        fn.cost = len(pairs) * (0.03 + cols * passes / 2400.0 * (0.6 if passes == 4 else 1.0))
        self._emit("pe", fn, reads + ([] if start else [out]), [out])

    def tr(self, out, in_):
        np_ = in_.ap.shape[0]
        idv = self.ident.v((slice(0, np_),), p=(0, np_))
        self._emit("pe", lambda e: e.transpose(out.ap, in_.ap, idv.ap), [in_, idv], [out])

    def act(self, out, in_, func, bias=None, scale=None, eng="act"):
        kw = {}
        if func == AF.Copy and (bias is not None or scale is not None):
            func = AF.Identity
        if bias is not None:
            kw["bias"] = self.A(bias)
        if scale is not None:
            kw["scale"] = self.A(scale)
        self._emit("act", lambda e: e.activation(out=out.ap, in_=in_.ap, func=func, **kw),
                     [in_, bias, scale], [out])

    def tt(self, out, in0, in1, op, eng="dve"):
        self._emit(eng, lambda e: e.tensor_tensor(out.ap, in0.ap, in1.ap, op), [in0, in1], [out])

    def ts(self, out, in0, s1, op0, s2=None, op1=None, eng="dve"):
        if op1 is None:
            fn = lambda e: e.tensor_scalar(out.ap, in0.ap, self.A(s1), None, op0)
        else:
            fn = lambda e: e.tensor_scalar(out.ap, in0.ap, self.A(s1), self.A(s2), op0, op1)
        self._emit(eng, fn, [in0, s1, s2], [out])

    def stt(self, out, in0, sc, in1, op0, op1, eng="dve"):
        eng = "dve"
        self._emit(eng, lambda e: e.scalar_tensor_tensor(out.ap, in0.ap, self.A(sc), in1.ap, op0, op1),
                     [in0, sc, in1], [out])

    def rsqrt(self, out, in_, mul, add):
        self.ts(out, in_, mul, ALU.mult, add, ALU.add)
        self.act(out, out, AF.Ln)
        self.act(out, out, AF.Exp, scale=-0.5)

    def scan(self, out, d0, d1, init, op0, op1, eng="dve"):
        eng = "dve"
        self._emit(eng, lambda e: e.tensor_tensor_scan(out.ap, d0.ap, d1.ap, self.A(init), op0, op1),
                     [d0, d1, init], [out])

    def copy(self, out, in_, eng="dve"):
        if eng == "act":
            self.act(out, in_, AF.Copy)
        else:
            self._emit(eng, lambda e: e.tensor_copy(out.ap, in_.ap), [in_], [out])

    def memset(self, out, val, eng="dve"):
        self._emit(eng, lambda e: e.memset(out.ap, val), [], [out])

    def dma(self, out, in_, key, eng="sp"):
        self._emit(eng, lambda e, sem: e.dma_start(out=out.ap, in_=in_.ap).then_inc(sem, 16),
                     [in_], [out], key=key)

    def xsv(self, d0, d1, t0, t1):
        if d1 - d0 == 1:
            ap = self.xs[:, d0, t0:t1]
        else:
            ap = self.xs[:, d0:d1, t0:t1]
        return View(ap, [("dr:xs%d" % d, t0, t1) for d in range(d0, d1)])

    def dview(self, name, ap, lo, hi):
        return View(ap, [("dr:" + name, lo, hi)])

    def sb(self, dtype, shape, np_=P):
        esz = 4 if dtype == F32 else 2
        n = int(np.prod(shape)) * esz
        n = (n + 31) // 32 * 32
        off = self.sb_top
        self.sb_top += n
        assert self.sb_top <= self.ARENA_BYTES, ("SBUF arena overflow", self.sb_top)
        return Buf(self, "sb", off, dtype, shape, np_)

    def ps(self, shape=(512,), np_=P, bank=None):
        if bank is None:
            bank = self.banks[self.bank_i % len(self.banks)]
            self.bank_i += 1
        return Buf(self, "ps", bank * 2048, F32, shape, np_)

    def build(self):
        cfg, nc = self.cfg, self.nc
        S, L, NF = cfg.S, cfg.depth, cfg.NF
        NCOL = n_cols(cfg.d_ff)
        dt = nc.dram_tensor
        self.x_in = dt("x", [S, D], F32, kind="ExternalInput").ap()
        self.w_in = dt("w_in", [L, D, IN_COLS], F32, kind="ExternalInput").ap()
        self.w_out = dt("w_out", [L, D, D], F32, kind="ExternalInput").ap()
        self.ffn_up = dt("ffn_up", [L, D, 2 * cfg.d_ff], F32, kind="ExternalInput").ap()
        self.ffn_down = dt("ffn_down", [L, cfg.d_ff, D], F32, kind="ExternalInput").ap()
        self.pool_w = dt("pool_w", [L, 4, P, P], F32, kind="ExternalInput").ap()
        self.lru_wa = dt("lru_wa", [L, NLB, P, P], F32, kind="ExternalInput").ap()
        self.lru_wx = dt("lru_wx", [L, NLB, P, P], F32, kind="ExternalInput").ap()
        self.cols_d = dt("cols", [L, P, NCOL], F32, kind="ExternalInput").ap()
        self.hrow_d = dt("hrow", [L, NH, 2], F32, kind="ExternalInput").ap()
        self.const_d = dt("consts", [P, 6 * P + NH * P + 64], F32, kind="ExternalInput").ap()
        self.y_out = dt("y", [S, D], F32, kind="ExternalOutput").ap()
        self.xs = dt("xs", [P, ND, S], F32).ap()
        self.n_units_M = 21 + 8
        self.n_units_F = NF // 2 * 2 + NF // 2
        self.wsM = dt("wsM", [self.n_units_M, P, 4096], BF16).ap()
        self.wsF = dt("wsF", [self.n_units_F, P, 4096], BF16).ap()

        self.ARENA_BYTES = 207 * 1024
        with ExitStack() as st:
            self.arena = st.enter_context(nc.sbuf_tensor("arena", [P, self.ARENA_BYTES // 4], F32))
            self.psum = [st.enter_context(nc.psum_tensor(f"psb{i}", [P, 512], F32)) for i in range(8)]
            self.sb_top = 0
            self.ident = self.sb(F32, (P,))
            self.ones = self.sb(F32, (P,))
            cd = self.const_d
            self.dma(self.ident.v(), self.dview("c", cd[:, 0:P], 0, 1), "c0")
            self.dma(self.ones.v(), self.dview("c", cd[:, P:2 * P], 0, 1), "c0")
            self.cols = self.sb(F32, (NCOL,))
            self.hrow = self.sb(F32, (2,), np_=NH)
            self.negA = self.sb(F32, (1,), np_=NH)
            self.lruc = self.sb(F32, (NLB,))
            self.pbs = self.sb(F32, (4,))
            self.sb_persist = self.sb_top
            for l in range(L):
                self.layer_setup(l)
                self.m_pass(l)
                self.f_pass(l)
            self.rec.emit(nc, st)
        return nc

    def layer_setup(self, l):
        self.sb_top = self.sb_persist
        t6 = self.sb(F32, (8,), np_=NH)
        tl = self.sb(F32, (NLB,))
        self.dma(self.cols.v(), self.dview("cols", self.cols_d[l], l, l + 1), "cols")
        self.dma(self.hrow.v(), self.dview("hrow", self.hrow_d[l], l, l + 1), "cols")
        self.act(t6.v((slice(0, 1),)), self.hrow.v((slice(0, 1),)), AF.Exp)
        self.ts(self.negA.v(), t6.v((slice(0, 1),)), -1.0, ALU.mult)
        lam = self.cols.v((slice(CO_LLAM, CO_LLAM + NLB),))
        self.act(tl.v(), lam, AF.Exp, scale=-1.0)
        self.act(tl.v(), tl.v(), AF.Ln, bias=1.0)
        self.ts(self.lruc.v(), tl.v(), -8.0, ALU.mult)
        self.tt(self.pbs.v(), self.cols.v((slice(CO_PB, CO_PB + 4),)), self.cols.v((slice(CO_PS, CO_PS + 4),)), ALU.mult)

    def fetch_unit(self, first, dst, ws, uidx, pieces, key):
        if first:
            for (src, dstv, stv) in pieces:
                if self.stage_sel is not None:
                    sidx = self.stage_sel
                else:
                    sidx = self.stage_rr
                    self.stage_rr = (self.stage_rr + 1) % len(self.stage)
                sv = stv(self.stage[sidx])
                self.dma(sv, src, f"stg{sidx}")
                self.copy(dstv, sv, eng=self.cast_engs[self.cast_rr % len(self.cast_engs)])
                self.cast_rr += 1
            self.dma(self.dview(ws[1], ws[0][uidx], uidx, uidx + 1), View(dst.base.rearrange("p a b -> p (a b)") if len(dst.shape) == 2 else dst.base, dst.v().regs), key + "o")
        else:
            self.dma(View(dst.base.rearrange("p a b -> p (a b)") if len(dst.shape) == 2 else dst.base, dst.v().regs),
                     self.dview(ws[1], ws[0][uidx], uidx, uidx + 1), key)

    def kunit_pieces(self, wl, cols, dst):
        pieces = []
        off = 0
        for (c0, w) in cols:
            for kh in range(2):
                src = wl[kh * 1024:(kh + 1) * 1024, c0:c0 + w].rearrange("(j p) c -> p j c", p=P)
                srcv = self.dview("w", src, 0, 1)
                dstv = dst.v((slice(kh * 8, kh * 8 + 8), slice(off, off + w)))
                pieces.append((srcv, dstv, (lambda w_: (lambda sb_: View(
                    sb_.base[:, 0:8 * w_].rearrange("p (j c) -> p j c", j=8), sb_.v().regs)))(w)))
            off += w
        return pieces

    def rmsnorm(self, xT, T, wcol0, out_fn, sq, rstd):
        for hh in range(T // 512):
            ts_ = slice(hh * 512, hh * 512 + 512)
            pss = self.ps()
            for j in range(ND):
                s = sq[j % 2].v((slice(0, 512),))
                self.act(s, xT.v((j, ts_)), AF.Square)
                self.mm(pss.v(), [(self.ones.v(), s)], start=(j == 0), stop=(j == ND - 1))
            r = rstd.v((ts_,))
            self.rsqrt(r, pss.v(), 1.0 / D, EPS)
            for j in range(ND):
                self.stt(out_fn(j, hh), xT.v((j, ts_)), self.cols.v((slice(wcol0 + j, wcol0 + j + 1),)), r,
                         ALU.mult, ALU.mult, eng=("dve" if j % 2 == 0 else "pool"))

    def m_pass(self, l):
        cfg = self.cfg
        S = cfg.S
        T = TM
        self.sb_top = self.sb_persist
        xT = self.sb(F32, (ND, TM))
        hT = self.sb(BF16, (ND, TM))
        mT = self.sb(BF16, (ND, TM))
        self.mT = mT
        self.stage = [self.sb(F32, (2048,)) for _ in range(2)]
        self.stage_rr = 0
        self.cast_engs = ["dve", "act"]
        self.cast_rr = 0
        wb = [self.sb(BF16, (ND, 256)) for _ in range(4)]
        wsmall = self.sb(F32, (16, P))
        self.wsmall = wsmall
        cd = self.const_d
        self.nmU = self.sb(F32, (P,))
        self.pmL = self.sb(F32, (P,))
        self.mU01 = self.sb(F32, (P,))
        self.sel6 = self.sb(F32, (NH * P,), np_=NH)
        self.poolrc = self.sb(F32, (4, 16))
        self.dma(self.nmU.v(), self.dview("c", cd[:, 2 * P:3 * P], 0, 1), "c0")
        self.dma(self.pmL.v(), self.dview("c", cd[:, 3 * P:4 * P], 0, 1), "c0")
        self.dma(self.mU01.v(), self.dview("c", cd[:, 4 * P:5 * P], 0, 1), "c0")
        self.dma(self.sel6.v(), self.dview("c", cd[0:NH, 6 * P:6 * P + NH * P], 0, 1), "c0")
        self.dma(self.poolrc.v(), self.dview("c", cd[:, 6 * P + NH * P:6 * P + NH * P + 64].rearrange(
            "p (a b) -> p a b", a=4), 0, 1), "c0")
        self.Sst = self.sb(F32, (NH, P))
        self.hst = self.sb(F32, (NLB,))
        self.car_g = self.sb(F32, (18, 3))
        self.car_l = self.sb(F32, (NLB, 3))
        self.car_p = self.sb(F32, (4, 16))
        for b_ in (self.Sst, self.hst, self.car_g, self.car_l, self.car_p):
            self.memset(b_.v(), 0.0, eng="pool")
        wt = lambda: self.sb(F32, (TM + 16,))
        TB = [wt() for _ in range(10)]
        TA = [wt() for _ in range(2)]
        TAd = [[wt() for _ in range(4)] for _ in range(2)]
        TC = [wt() for _ in range(8)]
        xtok = Buf(self, "sb", TA[0].off, F32, (D,))
        assert TAd[0][3].off + TAd[0][3].nbytes - TA[0].off >= D * 4
        rowb = self.sb(F32, (5, TM), np_=NH)
        cvo = [self.sb(BF16, (2048,)) for _ in range(2)]
        NF = cfg.NF
        wlu, wld = self.ffn_up[l], self.ffn_down[l]
        cv_pieces = []
        for pi in range(NF // 2):
            for gv in range(2):
                for kh in range(2):
                    c0 = gv * cfg.d_ff + pi * 256
                    src = wlu[kh * 1024:(kh + 1) * 1024, c0:c0 + 256].rearrange("(j p) c -> p j c", p=P)
                    cv_pieces.append((src, pi * 2 + gv, kh, True))
        for q in range(NF // 2):
            for i in range(2):
                f = q * 2 + i
                cv_pieces.append((wld[f * P:(f + 1) * P, :], NF + q, i, False))
        cv_pos = [0]
        NPc = len(cv_pieces)
        cv_seq = []
        for i in range(NPc + 2):
            if i < NPc:
                cv_seq.append(("in", i))
            if 1 <= i <= NPc:
                cv_seq.append(("cast", i - 1))
            if 2 <= i <= NPc + 1:
                cv_seq.append(("out", i - 2))
        colb = self.sb(F32, (4, 4, NH))
        cegl = self.sb(F32, (NH, 4))
        glast = self.sb(F32, (8,), np_=NH)
        for g in range(4):
            self.dma(wsmall.v((g,)), self.dview("pw", self.pool_w[l, g], 0, 1), "wsm")
        for j in range(NLB):
            self.dma(wsmall.v((4 + j,)), self.dview("pw", self.lru_wa[l, j], 0, 1), "wsm")
            self.dma(wsmall.v((10 + j,)), self.dview("pw", self.lru_wx[l, j], 0, 1), "wsm")
        wl_in, wl_out = self.w_in[l], self.w_out[l]
        ws = (self.wsM, "wsM")
        units = [[(C_POOL, 128), (C_POOL + 128, 128)], [(C_POOL + 256, 128), (C_POOL + 384, 128)], [(C_A, 12)]]
        for h in range(NH):
            units.append([(C_Q + h * P, P), (C_K + h * P, P)])
            units.append([(C_V + h * P, P), (C_Z + h * P, P)])
        for j in range(NLB):
            units.append([(C_XR + j * P, P), (C_GR + j * P, P)])
        nb = S // TM
        assert nb >= 2
        for b in range(nb):
            first = (b == 0)
            t0 = b * TM
            tsl = slice(t0, t0 + TM)

            def fetch(u, slot):
                if u < 21:
                    pcs = self.kunit_pieces(wl_in, units[u], wb[slot])
                else:
                    pcs = self.kunit_pieces(wl_out, [((u - 21) * 256, 256)], wb[slot])
                self.fetch_unit(first, wb[slot], ws, u, pcs, f"wb{slot}")

            def proj(slot, off, width):
                pso = self.ps((TM,), np_=width)
                self.mm(pso.v(), [(wb[slot].v((k, slice(off, off + width))), hT.v((k,))) for k in range(ND)])
                return pso
            def load_x(bb):
                tb0 = bb * TM
                if l == 0:
                    for tt_ in range(TM // P):
                        self.dma(xtok.v(), self.dview("x", self.x_in[tb0 + tt_ * P:tb0 + (tt_ + 1) * P, :], 0, 1), "xtok")
                        for g4 in range(4):
                            pst = self.ps((4, P))
                            for q in range(4):
                                self.tr(pst.v((q,)), xtok.v((slice((g4 * 4 + q) * P, (g4 * 4 + q + 1) * P),)))
                            self.copy(xT.v((slice(g4 * 4, g4 * 4 + 4), slice(tt_ * P, (tt_ + 1) * P))), pst.v(),
                                      eng=("act" if g4 % 2 else "dve"))
                    self.dma(self.xsv(0, ND, tb0, tb0 + TM), xT.v(), "xTo")
                else:
                    self.dma(xT.v(), self.xsv(0, ND, tb0, tb0 + TM), "xT")

            def norm_rows():
                self.rmsnorm(xT, TM, CO_N1, lambda j, hh: hT.v((j,)), [TB[0], TB[1]], TB[2])
                psa = proj(3, 0, NH)
                psb = proj(3, NH, NH)
                self.gdn_rows(psa, psb, rowb, colb, cegl, glast)

            if b == 0:
                fetch(2, 3)
                fetch(3, 0)
                fetch(4, 1)
                fetch(0, 2)
                load_x(0)
                norm_rows()

            def stream_A(h):
                self.stage_sel = 0
                pad, cv = TA
                qn, kn, vs, zs = TAd[h % 2]
                for (slot, off, ti, dst) in ((0, 0, 0, qn), (0, P, 1, kn), (1, 0, 2, vs)):
                    psx = proj(slot, off, P)
                    self.gdn_conv(psx, 3 * h + ti, pad, cv, dst)
                psz = proj(1, P, P)
                self.act(zs.v((slice(0, T),)), psz.v(), AF.Silu)
                if h + 1 < NH:
                    fetch(3 + 2 * (h + 1), 0)
                    fetch(4 + 2 * (h + 1), 1)
                sl = (slice(0, T),)
                sq, rn = pad, cv
                for (buf, scl) in ((kn, None), (qn, P ** -0.5)):
                    self.act(sq.v(sl), buf.v(sl), AF.Square)
                    pss = self.ps()
                    self.mm(pss.v(), [(self.ones.v(), sq.v(sl))])
                    self.rsqrt(rn.v(sl), pss.v(), 1.0, EPS)
                    if scl is None:
                        self.tt(buf.v(sl), buf.v(sl), rn.v(sl), ALU.mult)
                    else:
                        self.stt(buf.v(sl), buf.v(sl), scl, rn.v(sl), ALU.mult, ALU.mult)

            c_units = [0, 1, 15, 16, 17, 18, 19, 20]
            c_slot = lambda i: 2 + (i % 2)

            def stream_C(r):
                self.stage_sel = 1
                steps = [0, 1] if r == 0 else [r + 1]
                for i in steps:
                    if i + 1 < len(c_units):
                        fetch(c_units[i + 1], c_slot(i + 1))
                    slot = c_slot(i)
                    if i < 2:
                        for q in range(2):
                            self.pool_group(first, i * 2 + q, proj(slot, q * P, P), TC)
                    else:
                        j = i - 2
                        psx = proj(slot, 0, P)
                        psg = proj(slot, P, P)
                        self.lru_block(b == 0, j, psx, psg, TC)
                if r == 6:
                    fetch(21, 0)
                    fetch(22, 1)

            def cv_op(kind, i):
                src, u, hf, is_up = cv_pieces[i]
                stg, ob = self.stage[i % 2], cvo[i % 2]
                if is_up:
                    sv = View(stg.base.rearrange("p (j c) -> p j c", j=8), stg.v().regs)
                    ov = View(ob.base.rearrange("p (j c) -> p j c", j=8), ob.v().regs)
                else:
                    sv, ov = stg.v(), ob.v()
                if kind == "in":
                    self.dma(sv, self.dview("w", src, 0, 1), f"cvi{i % 2}")
                elif kind == "cast":
                    self.copy(ov, sv, eng="pool")
                else:
                    self.dma(self.dview("wsF", self.wsF[u][:, hf * 2048:(hf + 1) * 2048], u, u + 1), ob.v(),
                             f"cvo{i % 2}")

            def stream_D(n):
                for _ in range(n):
                    if cv_pos[0] >= len(cv_seq):
                        return
                    kind, i = cv_seq[cv_pos[0]]
                    cv_pos[0] += 1
                    cv_op(kind, i)

            n_cv = -(-len(cv_seq) // (7 * (nb - 1)))
            for r in range(7):
                lists = []
                if b >= 1:
                    lists.append(self.stream(lambda: stream_D(n_cv), []))
                if r < NH:
                    lists.append(self.stream(lambda: stream_A(r), [0, 1]))
                if r >= 1:
                    lists.append(self.stream(lambda: self.gdn_B(r - 1, TB, TAd[(r - 1) % 2], rowb, colb, cegl), [2, 3, 4]))
                lists.append(self.stream(lambda: stream_C(r), [5, 6]))
                if r == 6 and b + 1 < nb:
                    lists.append(self.stream(lambda: load_x(b + 1), [0, 1]))
                self.merge(lists)
                self.stage_sel = None
            def out_proj():
                self.stage_sel = 0
                rot = TC[0:8]
                fetch(23, 2)

                def ld(dti):
                    self.dma(rot[dti % 8].v((slice(0, TM),)), self.xsv(dti, dti + 1, t0, t0 + TM), f"xr{dti % 8}")
                for dti in range(8):
                    ld(dti)
                for u in range(8):
                    slot = u % 3
                    for i in range(2):
                        dti = u * 2 + i
                        xt_ = rot[dti % 8].v((slice(0, TM),))
                        pso = self.ps()
                        self.mm(pso.v(), [(wb[slot].v((k, slice(i * P, i * P + P))), mT.v((k,))) for k in range(ND)])
                        self.tt(xt_, xt_, pso.v(), ALU.add)
                        self.dma(self.xsv(dti, dti + 1, t0, t0 + TM), xt_, f"xw{dti % 8}")
                        if dti + 8 < ND:
                            ld(dti + 8)
                    if u + 3 < 8:
                        fetch(21 + u + 3, slot)
                    elif b + 1 < nb:
                        fetch((3, 4, 0)[slot], slot)

            def next_head():
                self.stage_sel = 1
                fetch(2, 3)
                norm_rows()

            lists = [self.stream(out_proj, [0, 1, 2])]
            if b + 1 < nb:
                lists.append(self.stream(next_head, [3, 4, 5]))
            self.merge(lists)
            self.stage_sel = None
        assert cv_pos[0] == len(cv_seq)

    def pool_group(self, first, g, psu, TC):
        upad, la, lb, dd = TC[0], TC[1], TC[2], TC[3]
        mT, wsmall = self.mT, self.wsmall
        win = 2 ** (g + 1)
        T = TM
        self.copy(upad.v((slice(0, 16),)), self.car_p.v((g,)), eng="dve")
        self.act(upad.v((slice(16, 16 + T),)), psu.v(), AF.Copy)
        self.copy(self.car_p.v((g,)), upad.v((slice(T, T + 16),)), eng="dve")
        src = upad
        sh = 1
        bufs = [la, lb]
        for lev in range(g + 1):
            dst = bufs[lev % 2]
            self.tt(dst.v((slice(sh, 16 + T),)), src.v((slice(sh, 16 + T),)), src.v((slice(0, 16 + T - sh),)), ALU.add)
            src = dst
            sh *= 2
        self.stt(dd.v((slice(0, T),)), src.v((slice(16, 16 + T),)), 1.0 / win, upad.v((slice(16, 16 + T),)),
                 ALU.mult, ALU.subtract)
        if first:
            tmp = TC[4]
            self.tt(tmp.v((slice(0, 16),)), src.v((slice(16, 32),)), self.poolrc.v((g,)), ALU.mult)
            self.tt(dd.v((slice(0, 16),)), tmp.v((slice(0, 16),)), upad.v((slice(16, 32),)), ALU.subtract)
        psy = self.ps()
        self.mm(psy.v(), [(wsmall.v((g,)), dd.v((slice(0, T),)))])
        self.act(mT.v((g,)), psy.v(), AF.Identity, bias=self.pbs.v((slice(g, g + 1),)),
                 scale=self.cols.v((slice(CO_PS + g, CO_PS + g + 1),)))

    def gelu_tanh(self, out, x, t1, t2):
        self.act(t1, x, AF.Square)
        self.ts(t1, t1, 0.044715, ALU.mult, 1.0, ALU.add)
        self.tt(t1, t1, x, ALU.mult)
        self.act(t2, t1, AF.Sigmoid, scale=1.5957691216057308)
        self.tt(out, t2, x, ALU.mult)

    def lru_block(self, seq_start, j, psx, psg, TC):
        T = TM
        sl = (slice(0, T),)
        pad, xc, r, i_, a, gt, t1, t2 = TC
        th = pad
        mT, wsmall = self.mT, self.wsmall
        self.act(gt.v(sl), psg.v(), AF.Copy)
        self.copy(pad.v((slice(0, 3),)), self.car_l.v((j,)), eng="dve")
        self.act(pad.v((slice(3, 3 + T),)), psx.v(), AF.Copy)
        self.copy(self.car_l.v((j,)), pad.v((slice(T, T + 3),)), eng="dve")
        c = lambda tap: self.cols.v((slice(CO_LCW + tap * NLB + j, CO_LCW + tap * NLB + j + 1),))
        xcv = xc.v(sl)
        self.act(xcv, psx.v(), AF.Copy, scale=c(3), bias=self.cols.v((slice(CO_LCB + j, CO_LCB + j + 1),)))
        for tap in (2, 1, 0):
            self.stt(xcv, pad.v((slice(tap, tap + T),)), c(tap), xcv, ALU.mult, ALU.add)
        psr = self.ps()
        self.mm(psr.v(), [(wsmall.v((4 + j,)), xcv)])
        psi = self.ps()
        self.mm(psi.v(), [(wsmall.v((10 + j,)), xcv)])
        self.act(r.v(sl), psr.v(), AF.Sigmoid, bias=self.cols.v((slice(CO_LBA + j, CO_LBA + j + 1),)))
        self.act(i_.v(sl), psi.v(), AF.Sigmoid, bias=self.cols.v((slice(CO_LBX + j, CO_LBX + j + 1),)))
        lc = self.lruc.v((slice(j, j + 1),))
        self.act(a.v(sl), r.v(sl), AF.Exp, scale=lc)
        self.act(th.v(sl), r.v(sl), AF.Tanh, scale=lc)
        self.tt(t1.v(sl), a.v(sl), a.v(sl), ALU.mult)
        self.stt(t1.v(sl), t1.v(sl), 1.0, th.v(sl), ALU.add, ALU.mult)
        self.act(t1.v(sl), t1.v(sl), AF.Sqrt, scale=-1.0)
        if seq_start:
            self.memset(t1.v((slice(0, 1),)), 1.0)
        self.tt(t1.v(sl), t1.v(sl), i_.v(sl), ALU.mult)
        self.tt(t1.v(sl), t1.v(sl), xcv, ALU.mult)
        hv = r.v(sl)
        self.scan(hv, a.v(sl), t1.v(sl), self.hst.v((slice(j, j + 1),)), ALU.mult, ALU.add)
        self.copy(self.hst.v((slice(j, j + 1),)), r.v((slice(T - 1, T),)), eng="dve")
        self.gelu_tanh(gt.v(sl), gt.v(sl), t2.v(sl), i_.v(sl))
        self.tt(mT.v((10 + j,)), hv, gt.v(sl), ALU.mult)

    def gdn_rows(self, psa, psb, rowb, colb, cegl, glast):
        T = TM
        R = lambda i: rowb.v((i,))
        beta, gc, bg, egl, t1 = R(0), R(1), R(2), R(3), R(4)
        p6 = (0, NH)
        self.act(beta, psb.v(), AF.Sigmoid)
        self.act(t1, psa.v(), AF.Exp, bias=self.hrow.v((slice(1, 2),)))
        self.act(t1, t1, AF.Ln, bias=1.0)
        self.ts(t1, t1, self.negA.v(), ALU.mult)
        ones6 = self.ones.v((slice(0, CH),), p=p6)
        for c in range(T // CH):
            cs = slice(c * CH, (c + 1) * CH)
            self.scan(rowb.v((1, cs)), ones6, rowb.v((4, cs)), 0.0, ALU.mult, ALU.add)
            self.copy(glast.v((slice(c, c + 1),)), rowb.v((1, slice(c * CH + CH - 1, (c + 1) * CH))))
        self.act(t1, gc, AF.Exp)
        self.tt(bg, beta, t1, ALU.mult)
        for c in range(T // CH):
            cs = slice(c * CH, (c + 1) * CH)
            self.act(rowb.v((3, cs)), rowb.v((1, cs)), AF.Exp, scale=-1.0, bias=glast.v((slice(c, c + 1),)))
        self.act(glast.v((slice(4, 8),)), glast.v((slice(0, 4),)), AF.Exp)
        psc = self.ps((4, 4, NH))
        id6 = self.ident.v((slice(0, NH),), p=p6)
        for c in range(T // CH):
            cs = slice(c * CH, (c + 1) * CH)
            for qi, row in enumerate((1, 0, 2, 3)):
                self.mm(psc.v((c, qi)), [(rowb.v((row, cs)), id6)])
        self.copy(colb.v(), psc.v())
        pse = self.ps((NH, 4))
        for h in range(NH):
            self.mm(pse.v((h,)), [(self.sel6.v((slice(h * P, (h + 1) * P),)), glast.v((slice(4, 8),)))])
        self.copy(cegl.v(), pse.v(), eng="act")

    def gdn_conv(self, psx, tile_i, pad, cv, out):
        T = TM
        car = self.car_g.v((tile_i,))
        self.copy(pad.v((slice(0, 3),)), car, eng="dve")
        self.act(pad.v((slice(3, 3 + T),)), psx.v(), AF.Copy)
        self.copy(car, pad.v((slice(T, T + 3),)), eng="dve")
        c = lambda tap: self.cols.v((slice(CO_GCW + tap * 18 + tile_i, CO_GCW + tap * 18 + tile_i + 1),))
        cvv = cv.v((slice(0, T),))
        self.act(cvv, psx.v(), AF.Copy, scale=c(3))
        for tap in (2, 1, 0):
            self.stt(cvv, pad.v((slice(tap, tap + T),)), c(tap), cvv, ALU.mult, ALU.add)
        self.act(out.v((slice(0, T),)), cvv, AF.Silu)

    def gdn_B(self, h, TB, TAq, rowb, colb, cegl):
        T = TM
        NCk = T // CH
        sl = (slice(0, T),)
        mT = self.mT
        qn, kn, vs, zs = TAq
        Rg, Rb, Re, Dm, EU, EL, EUs, kbg, kd, vb = TB
        Nb = [EL, Rg]
        Pb = [EUs, Rb]
        Q = Dm
        selh = self.sel6.v((slice(h * P, (h + 1) * P),))
        for (row, dst, eng) in ((1, Rg, "act"), (0, Rb, "dve")):
            psr = self.ps()
            self.mm(psr.v(), [(selh, rowb.v((row,)))])
            self.copy(dst.v(sl), psr.v(), eng=eng)
        self.act(Re.v(sl), Rg.v(sl), AF.Exp)

        def v3(buf):
            vv = buf.v(sl)
            return View(vv.ap.rearrange("p (c f) -> p c f", c=NCk), vv.regs)

        def colq(qi):
            vv = colb.v((slice(None), qi, slice(h, h + 1)))
            return View(vv.ap.broadcast_to([P, NCk, CH]), vv.regs)

        def bcm(m):
            vv = m.v()
            return View(vv.ap.rearrange("p (o f) -> p o f", o=1).broadcast_to([P, NCk, CH]), vv.regs)
        self.tt(v3(Dm), v3(Rg), colq(0), ALU.subtract)
        self.stt(v3(EU), v3(Dm), 0.0, bcm(self.nmU), ALU.min, ALU.add)
        self.act(EU.v(sl), EU.v(sl), AF.Exp)
        self.stt(v3(EL), v3(Dm), 0.0, bcm(self.pmL), ALU.max, ALU.add)
        self.act(EL.v(sl), EL.v(sl), AF.Exp, scale=-1.0)
        self.tt(v3(EUs), v3(EU), bcm(self.mU01), ALU.mult)
        self.tt(EUs.v(sl), EUs.v(sl), Rb.v(sl), ALU.mult)
        self.tt(v3(EL), v3(EL), colq(1), ALU.mult)
        qd = Re
        self.tt(qd.v(sl), qn.v(sl), Re.v(sl), ALU.mult)
        pkk = self.ps()
        pqk = self.ps()
        for c in range(NCk):
            cs = (slice(c * CH, (c + 1) * CH),)
            self.mm(pkk.v(cs), [(kn.v(cs), kn.v(cs))])
            self.mm(pqk.v(cs), [(kn.v(cs), qn.v(cs))])
        N0, P0, attnT = EL, EUs, EU
        self.stt(P0.v(sl), pkk.v(), -1.0, EUs.v(sl), ALU.mult, ALU.mult)
        self.stt(N0.v(sl), pkk.v(), -1.0, EL.v(sl), ALU.mult, ALU.mult)
        self.tt(attnT.v(sl), pqk.v(), EU.v(sl), ALU.mult)
        self.tt(v3(Q), v3(P0), bcm(self.ident), ALU.add)
        ptk = self.ps()
        ptv = self.ps()
        for c in range(NCk):
            cs = (slice(c * CH, (c + 1) * CH),)
            self.tr(ptk.v(cs), kn.v(cs))
            self.tr(ptv.v(cs), vs.v(cs))
        ptk3 = View(ptk.v().ap.rearrange("p (c f) -> p c f", c=NCk), ptk.v().regs)
        ptv3 = View(ptv.v().ap.rearrange("p (c f) -> p c f", c=NCk), ptv.v().regs)
        self.tt(v3(kbg), ptk3, colq(2), ALU.mult)
        self.tt(v3(kd), ptk3, colq(3), ALU.mult)
        self.tt(v3(vb), ptv3, colq(1), ALU.mult)
        for k in range(1, 7):
            Np, Pp, Nn, Pn = Nb[(k - 1) % 2], Pb[(k - 1) % 2], Nb[k % 2], Pb[k % 2]
            pn = self.ps()
            for c in range(NCk):
                cs = (slice(c * CH, (c + 1) * CH),)
                self.mm(pn.v(cs), [(Pp.v(cs), Np.v(cs))])
            self.copy(Nn.v(sl), pn.v(), eng="act")
            if k < 6:
                pp = self.ps()
                for c in range(NCk):
                    cs = (slice(c * CH, (c + 1) * CH),)
                    self.mm(pp.v(cs), [(Np.v(cs), Pp.v(cs))])
                self.copy(Pn.v(sl), pp.v(), eng="dve")
            pq = self.ps()
            for c in range(NCk):
                cs = (slice(c * CH, (c + 1) * CH),)
                self.mm(pq.v(cs), [(Nn.v(cs), Q.v(cs))])
            self.tt(Q.v(sl), Q.v(sl), pq.v(), ALU.add)
        WT, U = Nb[0], Pb[0]
        pw = self.ps()
        pu = self.ps()
        for c in range(NCk):
            cs = (slice(c * CH, (c + 1) * CH),)
            self.mm(pw.v(cs), [(kbg.v(cs), Q.v(cs))])
            self.mm(pu.v(cs), [(Q.v(cs), vb.v(cs))])
        self.copy(WT.v(sl), pw.v(), eng="act")
        self.copy(U.v(sl), pu.v(), eng="dve")
        Sh = self.Sst.v((h,))
        vnew = Nb[1]
        pso = self.ps(bank=7)
        for c in range(NCk):
            cs = (slice(c * CH, (c + 1) * CH),)
            pv = self.ps((CH,))
            self.mm(pv.v(), [(WT.v(cs), Sh)])
            self.tt(vnew.v(cs), U.v(cs), pv.v(), ALU.subtract)
            self.mm(pso.v(cs), [(Sh, qd.v(cs)), (vnew.v(cs), attnT.v(cs))])
            pss_ = self.ps((CH,))
            self.mm(pss_.v(), [(kd.v(cs), vnew.v(cs))])
            self.stt(Sh, Sh, cegl.v((h, slice(c, c + 1))), pss_.v(), ALU.mult, ALU.add)
        osq, rs, on = Dm, Rb, kbg
        self.act(osq.v(sl), pso.v(), AF.Square)
        pss2 = self.ps()
        self.mm(pss2.v(), [(self.ones.v(), osq.v(sl))])
        self.rsqrt(rs.v(sl), pss2.v(), 1.0 / P, EPS)
        self.stt(on.v(sl), pso.v(), self.cols.v((slice(CO_GNW, CO_GNW + 1),)), rs.v(sl), ALU.mult, ALU.mult)
        self.tt(mT.v((4 + h,)), on.v(sl), zs.v(sl), ALU.mult)

    def f_pass(self, l):
        cfg = self.cfg
        S, NF, L = cfg.S, cfg.NF, cfg.depth
        self.sb_top = self.sb_persist
        xT = self.sb(F32, (ND, TF))
        hT = self.sb(BF16, (ND, TF))
        NWU = 4
        wup = [self.sb(BF16, (ND, 256)) for _ in range(NWU)]
        wdn = [self.sb(BF16, (2, D)) for _ in range(4)]
        actb = [self.sb(BF16, (4, TF)) for _ in range(2)]
        self.car_f = self.sb(F32, (NF, 2))
        self.memset(self.car_f.v(), 0.0, eng="pool")
        mark = self.sb_top
        otok = self.sb(F32, (D,))
        self.sb_top = mark
        gbuf = [self.sb(F32, (2 + TF,)) for _ in range(2)]
        tb = [self.sb(F32, (512,)) for _ in range(3)]
        wlu, wld = self.ffn_up[l], self.ffn_down[l]
        ws = (self.wsF, "wsF")
        last = (l == L - 1)
        nb = S // TF
        NCH = NF // 4
        for b in range(nb):
            first = False
            t0 = b * TF
            tsl = slice(t0, t0 + TF)
            self.dma(xT.v(), self.xsv(0, ND, t0, t0 + TF), "xTf")
            rstd = gbuf[0]
            self.rmsnorm(xT, TF, CO_N2, lambda j, hh: hT.v((j, slice(hh * 512, hh * 512 + 512))), [tb[0], tb[1]], rstd)
            self.urr = 0
            self.drr = 0

            def up_chunk(ci):
                ab = actb[ci % 2]
                for pr in range(2):
                    f0 = ci * 4 + pr * 2
                    ug = wup[self.urr % NWU]
                    self.urr += 1
                    self.fetch_unit(first, ug, ws, (f0 // 2) * 2, self.kunit_pieces(wlu, [(f0 * P, 256)], ug),
                                    f"wu{(self.urr - 1) % NWU}")
                    uv = wup[self.urr % NWU]
                    self.urr += 1
                    self.fetch_unit(first, uv, ws, (f0 // 2) * 2 + 1,
                                    self.kunit_pieces(wlu, [(cfg.d_ff + f0 * P, 256)], uv), f"wu{(self.urr - 1) % NWU}")
                    for i in range(2):
                        f = f0 + i
                        gb = gbuf[f % 2]
                        self.copy(gb.v((slice(0, 2),)), self.car_f.v((f,)), eng="pool")
                        for hh in range(TF // 512):
                            hs = slice(hh * 512, hh * 512 + 512)
                            psg = self.ps()
                            self.mm(psg.v(), [(ug.v((k, slice(i * P, i * P + P))), hT.v((k, hs))) for k in range(ND)])
                            psv = self.ps()
                            self.mm(psv.v(), [(uv.v((k, slice(i * P, i * P + P))), hT.v((k, hs))) for k in range(ND)])
                            self.act(gb.v((slice(2 + hh * 512, 2 + hh * 512 + 512),)), psg.v(), AF.Copy)
                            cw = lambda tap: self.cols.v((slice(CO_FCW + tap * NF + f, CO_FCW + tap * NF + f + 1),))
                            t = tb[0].v()
                            self.act(t, psg.v(), AF.Copy, scale=cw(2))
                            self.stt(t, gb.v((slice(1 + hh * 512, 1 + hh * 512 + 512),)), cw(1), t, ALU.mult, ALU.add)
                            self.stt(t, gb.v((slice(hh * 512, hh * 512 + 512),)), cw(0), t, ALU.mult, ALU.add, eng="pool")
                            self.gelu_tanh(t, t, tb[1].v(), tb[2].v())
                            self.tt(ab.v((pr * 2 + i, hs)), t, psv.v(), ALU.mult)
                        self.copy(self.car_f.v((f,)), gb.v((slice(TF, TF + 2),)), eng="pool")

            def down_chunk(ci):
                ab = actb[ci % 2]
                wd = []
                for pr in range(2):
                    dstb = wdn[self.drr % 4]
                    self.drr += 1
                    u = NF + ci * 2 + pr
                    pieces = []
                    for i in range(2):
                        f = ci * 4 + pr * 2 + i
                        srcv = self.dview("w", wld[f * P:(f + 1) * P, :], 0, 1)
                        pieces.append((srcv, dstb.v((i,)), lambda sb_: sb_.v()))
                    self.fetch_unit(first, dstb, ws, u, pieces, f"wd{(self.drr - 1) % 4}")
                    wd.append(dstb)
                for dti in range(ND):
                    for hh in range(TF // 512):
                        hs = slice(hh * 512, hh * 512 + 512)
                        pso = self.ps()
                        self.mm(pso.v(), [(wd[q // 2].v((q % 2, slice(dti * P, dti * P + P))), ab.v((q, hs)))
                                          for q in range(4)])
                        xv = xT.v((dti, hs))
                        self.tt(xv, xv, pso.v(), ALU.add)

            up_chunk(0)
            for ci in range(1, NCH):
                up_chunk(ci)
                down_chunk(ci - 1)
            down_chunk(NCH - 1)
            if not last:
                self.dma(self.xsv(0, ND, t0, t0 + TF), xT.v(), "xTfo")
            else:
                rstd = gbuf[0]
                self.rmsnorm(xT, TF, CO_NF, lambda j, hh: xT.v((j, slice(hh * 512, hh * 512 + 512))),
                             [tb[0], tb[1]], rstd)
                for tt_ in range(TF // P):
                    for g4 in range(4):
                        pst = self.ps()
                        for q in range(4):
                            self.tr(pst.v((slice(q * P, (q + 1) * P),)), xT.v((g4 * 4 + q, slice(tt_ * P, (tt_ + 1) * P))))
                        self.copy(otok.v((slice(g4 * 512, g4 * 512 + 512),)), pst.v(), eng=("act" if g4 % 2 else "dve"))
                    self.dma(self.dview("y", self.y_out[t0 + tt_ * P:t0 + (tt_ + 1) * P, :], t0 + tt_, t0 + tt_ + 1),
                             otok.v(), "otok")


def make_consts():
    c = np.zeros((P, 6 * P + NH * P + 64), np.float32)
    p = np.arange(P)[:, None]
    f = np.arange(P)[None, :]
    c[:, 0:P] = np.eye(P)
    c[:, P:2 * P] = 1.0
    c[:, 2 * P:3 * P] = np.where(f >= p, 0.0, NEG)
    c[:, 3 * P:4 * P] = np.where(p > f, 0.0, -NEG)
    c[:, 4 * P:5 * P] = np.where(f > p, 1.0, 0.0)
    for h in range(NH):
        c[h, 6 * P + h * P:6 * P + (h + 1) * P] = 1.0
    for g in range(4):
        win = 2 ** (g + 1)
        t = np.arange(16)
        c[:, 6 * P + NH * P + g * 16:6 * P + NH * P + (g + 1) * 16] = (1.0 / np.minimum(t + 1, win))[None, :]
    return c


def make_cols(inp, L, d_ff):
    NF = d_ff // P
    cols = np.zeros((L, P, n_cols(d_ff)), np.float32)
    col = lambda v: np.ascontiguousarray(np.asarray(v, np.float32).reshape(-1, P).T)
    for l in range(L):
        c = cols[l]
        c[:, CO_N1:CO_N1 + 16] = col(inp["norm1_w"][l])
        c[:, CO_N2:CO_N2 + 16] = col(inp["norm2_w"][l])
        c[:, CO_NF:CO_NF + 16] = col(inp["final_norm_w"])
        c[:, CO_PB:CO_PB + 4] = col(inp["pool_b"][l])
        c[:, CO_PS:CO_PS + 4] = col(inp["pool_scale"][l])
        gcw = np.asarray(inp["gdn_conv_w"][l], np.float32)
        for tap in range(4):
            t18 = col(gcw[tap])
            for h in range(NH):
                for qi in range(3):
                    c[:, CO_GCW + tap * 18 + 3 * h + qi] = t18[:, qi * NH + h]
        c[:, CO_GNW] = np.asarray(inp["gdn_norm_w"][l], np.float32)
        lcw = np.asarray(inp["lru_conv_w"][l], np.float32)
        for tap in range(4):
            c[:, CO_LCW + tap * NLB:CO_LCW + (tap + 1) * NLB] = col(lcw[tap])
        c[:, CO_LCB:CO_LCB + NLB] = col(inp["lru_conv_b"][l])
        c[:, CO_LBA:CO_LBA + NLB] = col(inp["lru_ba"][l])
        c[:, CO_LBX:CO_LBX + NLB] = col(inp["lru_bx"][l])
        c[:, CO_LLAM:CO_LLAM + NLB] = col(inp["lru_lambda"][l])
        fcw = np.asarray(inp["ffn_conv_w"][l], np.float32)
        for tap in range(3):
            c[:, CO_FCW + tap * NF:CO_FCW + (tap + 1) * NF] = col(fcw[tap])
    return cols


_NC_CACHE = {}
SPREAD = True


def run(inp, S, L, d_ff, n_cores):
    cfg = Cfg(S, L, d_ff)
    keyc = (S, L, d_ff)
    if keyc not in _NC_CACHE:
        _NC_CACHE[keyc] = K(cfg).build()
    nc = _NC_CACHE[keyc]
    f = lambda a: np.ascontiguousarray(np.asarray(a, np.float32))
    hrow = np.stack([f(inp["gdn_a_log"]), f(inp["gdn_dt_bias"])], axis=-1)
    shared = {
        "w_in": f(inp["w_in"]), "w_out": f(inp["w_out"]), "ffn_up": f(inp["ffn_up"]), "ffn_down": f(inp["ffn_down"]),
        "pool_w": f(inp["pool_w"]), "lru_wa": f(inp["lru_wa"]), "lru_wx": f(inp["lru_wx"]),
        "cols": make_cols(inp, L, d_ff), "hrow": np.ascontiguousarray(hrow), "consts": make_consts(),
    }
    x = f(inp["x"])
    if n_cores == 4 and SPREAD:
        slots = [0, 1, 4, 5]
        zero = {k: np.zeros_like(v) for k, v in shared.items()}
        zero["x"] = np.zeros_like(x[0])
        in_maps = [zero] * 8
        in_maps = list(in_maps)
        for i, sl_ in enumerate(slots):
            in_maps[sl_] = dict(shared, x=np.ascontiguousarray(x[i]))
        res = run_bass_kernel_spmd(nc, in_maps, core_ids=list(range(8)))
        return np.stack([np.asarray(res.results[sl_]["y"], np.float32) for sl_ in slots], axis=0)
    in_maps = [dict(shared, x=np.ascontiguousarray(x[i])) for i in range(n_cores)]
    res = run_bass_kernel_spmd(nc, in_maps, core_ids=list(range(n_cores)))
    return np.stack([np.asarray(res.results[i]["y"], np.float32) for i in range(n_cores)], axis=0)


def kernel(**inputs):
    x = np.asarray(inputs["x"])
    B, S, _ = x.shape
    L = np.asarray(inputs["w_in"]).shape[0]
    d_ff = np.asarray(inputs["ffn_down"]).shape[1]
    return run(inputs, S, L, d_ff, B)
```

```python
import numpy as np
from contextlib import ExitStack
import concourse.bass as bass
import concourse.mybir as mybir
from concourse.bass_utils import run_bass_kernel_spmd

F32 = mybir.dt.float32
BF16 = mybir.dt.bfloat16
AF = mybir.ActivationFunctionType
ALU = mybir.AluOpType
P = 128
EPS = 1e-6
NEG = -30000.0

D = 2048
ND = D // P
POOL_W = 512
GDN_W = 768
NH = 6
LRU_W = 768
NLB = 6
IN_COLS = POOL_W + 4 * GDN_W + 2 * NH + 2 * LRU_W
C_POOL, C_Q, C_K, C_V, C_Z = 0, 512, 1280, 2048, 2816
C_A, C_B, C_XR, C_GR = 3584, 3590, 3596, 4364
TM = 512
TF = 1024
CH = 128

CO_N1, CO_N2, CO_NF, CO_PB, CO_PS, CO_GCW, CO_GNW, CO_LCW, CO_LCB, CO_LBA, CO_LBX, CO_LLAM = (
    0, 16, 32, 48, 52, 56, 128, 129, 153, 159, 165, 171)
CO_FCW = 177


def n_cols(d_ff):
    return CO_FCW + 3 * (d_ff // P)


class View:
    __slots__ = ("ap", "regs")

    def __init__(self, ap, regs):
        self.ap = ap
        self.regs = regs


class Buf:
    def __init__(self, k, space, off, dtype, shape, np_=P):
        self.k, self.space, self.off, self.dtype, self.shape, self.np = k, space, off, dtype, tuple(shape), np_
        self.esz = 4 if dtype == F32 else 2
        n = 1
        for s in shape:
            n *= s
        self.n = n
        self.nbytes = n * self.esz
        if space == "sb":
            w0 = off // 4
            base = k.arena[0:np_, w0:w0 + (self.nbytes + 3) // 4]
            if dtype != F32:
                base = base.bitcast(dtype)
        else:
            bank = off // 2048
            w0 = (off % 2048) // 4
            base = k.psum[bank][0:np_, w0:w0 + (self.nbytes + 3) // 4]
            if dtype != F32:
                base = base.bitcast(dtype)
        if len(shape) == 2:
            base = base.rearrange("p (a b) -> p a b", a=shape[0])
        elif len(shape) == 3:
            base = base.rearrange("p (a b c) -> p a b c", a=shape[0], b=shape[1])
        self.base = base

    def __getitem__(self, idx):
        if not isinstance(idx, tuple):
            idx = (idx,)
        return self.v(idx)

    def v(self, idx=(), p=None):
        shape = self.shape
        idx = tuple(idx) + (slice(None),) * (len(shape) - len(idx))
        lo = 0
        hi = 0
        stride = self.n
        for s, i in zip(shape, idx):
            stride //= s
            if isinstance(i, int):
                a, b = i, i + 1
            else:
                a = 0 if i.start is None else i.start
                b = s if i.stop is None else i.stop
                assert i.step is None
            assert 0 <= a < b <= s, (shape, idx)
            lo += a * stride
            hi += (b - 1) * stride
        hi += 1
        p0, p1 = (0, self.np) if p is None else p
        ap = self.base[(slice(p0, p1),) + idx]
        return View(ap, [(self.space, self.off + lo * self.esz, self.off + hi * self.esz)])


class Op:
    __slots__ = ("eng", "fn", "deps", "seq", "key", "val", "prev_val", "i")


ENGS = ("pe", "act", "dve", "pool", "sp")
EPOCH = 12000
BUCKET = 512


class Rec:
    def __init__(self):
        self.ops = []
        self.by_eng = {e: [] for e in ENGS}
        self.wr = {}
        self.rd = {}
        self.dma_val = {}

    def _buckets(self, sp, lo, hi):
        return [(sp, b) for b in range(lo // BUCKET, (hi - 1) // BUCKET + 1)]

    def add(self, eng, fn, reads, writes, key=None, ndma=1):
        op = Op()
        op.eng, op.fn, op.key, op.i = eng, fn, key, len(self.ops)
        deps = set()
        rregs = [r for v in reads if v is not None and isinstance(v, View) for r in v.regs]
        wregs = [r for v in writes for r in v.regs]
        for (sp, lo, hi) in rregs:
            for bk in self._buckets(sp, lo, hi):
                for (a, b, o) in self.wr.get(bk, ()):
                    if a < hi and lo < b:
                        deps.add(o)
        for (sp, lo, hi) in wregs:
            for bk in self._buckets(sp, lo, hi):
                for (a, b, o) in self.wr.get(bk, ()):
                    if a < hi and lo < b:
                        deps.add(o)
                for (a, b, _e), o in self.rd.get(bk, {}).items():
                    if a < hi and lo < b:
                        deps.add(o)
        deps.discard(op.i)
        op.deps = deps
        if key is not None:
            pv = self.dma_val.get(key, 0)
            op.prev_val = pv
            op.val = pv + 16 * ndma
            self.dma_val[key] = op.val
            op.seq = None
        else:
            op.seq = len(self.by_eng[eng])
        for (sp, lo, hi) in wregs:
            for bk in self._buckets(sp, lo, hi):
                blo, bhi = bk[1] * BUCKET, (bk[1] + 1) * BUCKET
                l = self.wr.setdefault(bk, [])
                l[:] = [(a, b, o) for (a, b, o) in l if not (lo <= max(a, blo) and min(b, bhi) <= hi)]
                l.append((lo, hi, op.i))
                d = self.rd.get(bk)
                if d:
                    for kk in [kk for kk in d if lo <= max(kk[0], blo) and min(kk[1], bhi) <= hi]:
                        del d[kk]
        ek = eng if key is None else ("dma", op.i)
        for (sp, lo, hi) in rregs:
            for bk in self._buckets(sp, lo, hi):
                self.rd.setdefault(bk, {})[(lo, hi, ek)] = op.i
        self.ops.append(op)
        self.by_eng[eng].append(op)
        return op

    def emit(self, nc, stack):
        esem = {}
        for e in ("pe", "act", "dve", "pool"):
            n = len(self.by_eng[e])
            esem[e] = [stack.enter_context(nc.semaphore(f"s_{e}{i}")) for i in range(n // EPOCH + 1)]
        dsem = {k: stack.enter_context(nc.semaphore(f"d_{k}")) for k in self.dma_val}
        ops = self.ops
        block = stack.enter_context(nc.Block())

        def run(eng, e):
            waited = {}

            def wait(sem, val, tag):
                if waited.get(tag, 0) < val:
                    e.wait_ge(sem, val)
                    waited[tag] = val

            for op in self.by_eng[eng]:
                for di in sorted(op.deps):
                    d = ops[di]
                    if d.key is not None:
                        wait(dsem[d.key], d.val, ("d", d.key))
                    else:
                        if d.eng == "pe" and eng == "pe":
                            continue
                        ep = d.seq // EPOCH
                        wait(esem[d.eng][ep], d.seq % EPOCH + 1, (d.eng, ep))
                if op.key is not None:
                    if op.prev_val:
                        wait(dsem[op.key], op.prev_val, ("d", op.key))
                    op.fn(e, dsem[op.key])
                else:
                    ins = op.fn(e)
                    ins.then_inc(esem[eng][op.seq // EPOCH], 1)
            if eng == "sp":
                for k, v in self.dma_val.items():
                    wait(dsem[k], v, ("d", k))

        @block.tensor
        def _(e):
            run("pe", e)

        @block.scalar
        def _(e):
            run("act", e)

        @block.vector
        def _(e):
            run("dve", e)

        @block.gpsimd
        def _(e):
            run("pool", e)

        @block.sync
        def _(e):
            run("sp", e)


class Cfg:
    def __init__(self, S=4096, depth=2, d_ff=6144):
        self.S, self.depth, self.d_ff = S, depth, d_ff
        self.NF = d_ff // P
        assert S % TF == 0 and self.NF % 4 == 0


class K:
    def __init__(self, cfg):
        self.cfg = cfg
        self.nc = bass.Bass("TRN2", target_bir_lowering=False)
        self.rec = Rec()
        self.cur = None
        self.stage_sel = None
        self.banks = list(range(7))
        self.bank_i = 0

    def _emit(self, eng, fn, reads, writes, key=None):
        if self.cur is not None:
            self.cur.append((eng, fn, reads, writes, key))
        else:
            self.rec.add(eng, fn, reads, writes, key=key)

    def stream(self, fn, banks):
        assert self.cur is None
        save = (self.banks, self.bank_i)
        self.cur, self.banks, self.bank_i = [], banks, 0
        fn()
        out = self.cur
        self.cur = None
        self.banks, self.bank_i = save
        return out

    def _cost(self, a):
        eng, fn, reads, writes, key = a
        n = 1
        for d in writes[0].ap.shape[1:]:
            n *= d
        if key is not None:
            return 2.0 + n * 128 * 4 / 300e3
        if eng == "pe":
            c = getattr(fn, "cost", None)
            return c if c is not None else 0.2
        if eng == "dve":
            return 0.12 + n / 960.0
        if eng == "act":
            return 0.15 + n / 1100.0
        return 0.2 + n * 0.0035

    def merge(self, lists):
        lists = [l for l in lists if l]
        deps = []
        for l in lists:
            d = []
            wr, rd = [], []
            for i, a in enumerate(l):
                rr = [r for v in a[2] if isinstance(v, View) for r in v.regs]
                ww = [r for v in a[3] for r in v.regs]
                s_ = set()
                for (sp, lo, hi) in rr:
                    for (sp2, a2, b2, o) in wr:
                        if sp == sp2 and a2 < hi and lo < b2:
                            s_.add(o)
                for (sp, lo, hi) in ww:
                    for (sp2, a2, b2, o) in wr:
                        if sp == sp2 and a2 < hi and lo < b2:
                            s_.add(o)
                    for (sp2, a2, b2, o) in rd:
                        if sp == sp2 and a2 < hi and lo < b2:
                            s_.add(o)
                for (sp, lo, hi) in ww:
                    wr = [w for w in wr if not (w[0] == sp and lo <= w[1] and w[2] <= hi)]
                    rd = [w for w in rd if not (w[0] == sp and lo <= w[1] and w[2] <= hi)]
                    wr.append((sp, lo, hi, i))
                for (sp, lo, hi) in rr:
                    rd.append((sp, lo, hi, i))
                if len(rd) > 200:
                    rd = rd[-200:]
                d.append(s_)
            deps.append(d)
        pos = [0] * len(lists)
        fin = [[0.0] * len(l) for l in lists]
        efree = {e: 0.0 for e in ENGS}
        while True:
            best, bi = None, -1
            for i, l in enumerate(lists):
                if pos[i] < len(l):
                    a = l[pos[i]]
                    rdy = 0.0
                    for o in deps[i][pos[i]]:
                        t = fin[i][o] + 0.3
                        if t > rdy:
                            rdy = t
                    st = max(rdy, efree[a[0]])
                    if best is None or st < best - 1e-9:
                        best, bi = st, i
            if bi < 0:
                break
            a = lists[bi][pos[bi]]
            c = self._cost(a)
            if a[4] is not None:
                efree[a[0]] = best + 0.05
            else:
                efree[a[0]] = best + c
            fin[bi][pos[bi]] = best + c
            pos[bi] += 1
            self.rec.add(a[0], a[1], a[2], a[3], key=a[4])

    def A(self, v):
        return v.ap if isinstance(v, View) else v

    def mm(self, out, pairs, start=True, stop=True):
        reads = [x for pr in pairs for x in pr]

        def fn(e):
            ins = None
            n = len(pairs)
            for i, (l, r) in enumerate(pairs):
                ins = e.matmul(out.ap, l.ap, r.ap, start=(start and i == 0), stop=(stop and i == n - 1))
            return ins
        cols = 1
        for d in pairs[0][1].ap.shape[1:]:
            cols *= d
        passes = 4 if pairs[0][0].ap.dtype == F32 else 1
        fn.cost = len(pairs) * (0.03 + cols * passes / 2400.0 * (0.6 if passes == 4 else 1.0))
        self._emit("pe", fn, reads + ([] if start else [out]), [out])

    def tr(self, out, in_):
        np_ = in_.ap.shape[0]
        idv = self.ident.v((slice(0, np_),), p=(0, np_))
        self._emit("pe", lambda e: e.transpose(out.ap, in_.ap, idv.ap), [in_, idv], [out])

    def act(self, out, in_, func, bias=None, scale=None, eng="act"):
        kw = {}
        if func == AF.Copy and (bias is not None or scale is not None):
            func = AF.Identity
        if bias is not None:
            kw["bias"] = self.A(bias)
        if scale is not None:
            kw["scale"] = self.A(scale)
        self._emit("act", lambda e: e.activation(out=out.ap, in_=in_.ap, func=func, **kw),
                     [in_, bias, scale], [out])

    def tt(self, out, in0, in1, op, eng="dve"):
        self._emit(eng, lambda e: e.tensor_tensor(out.ap, in0.ap, in1.ap, op), [in0, in1], [out])

    def ts(self, out, in0, s1, op0, s2=None, op1=None, eng="dve"):
        if op1 is None:
            fn = lambda e: e.tensor_scalar(out.ap, in0.ap, self.A(s1), None, op0)
        else:
            fn = lambda e: e.tensor_scalar(out.ap, in0.ap, self.A(s1), self.A(s2), op0, op1)
        self._emit(eng, fn, [in0, s1, s2], [out])

    def stt(self, out, in0, sc, in1, op0, op1, eng="dve"):
        eng = "dve"
        self._emit(eng, lambda e: e.scalar_tensor_tensor(out.ap, in0.ap, self.A(sc), in1.ap, op0, op1),
                     [in0, sc, in1], [out])

    def rsqrt(self, out, in_, mul, add):
        self.ts(out, in_, mul, ALU.mult, add, ALU.add)
        self.act(out, out, AF.Ln)
        self.act(out, out, AF.Exp, scale=-0.5)

    def scan(self, out, d0, d1, init, op0, op1, eng="dve"):
        eng = "dve"
        self._emit(eng, lambda e: e.tensor_tensor_scan(out.ap, d0.ap, d1.ap, self.A(init), op0, op1),
                     [d0, d1, init], [out])

    def copy(self, out, in_, eng="dve"):
        if eng == "act":
            self.act(out, in_, AF.Copy)
        else:
            self._emit(eng, lambda e: e.tensor_copy(out.ap, in_.ap), [in_], [out])

    def memset(self, out, val, eng="dve"):
        self._emit(eng, lambda e: e.memset(out.ap, val), [], [out])

    def dma(self, out, in_, key, eng="sp"):
        self._emit(eng, lambda e, sem: e.dma_start(out=out.ap, in_=in_.ap).then_inc(sem, 16),
                     [in_], [out], key=key)

    def xsv(self, d0, d1, t0, t1):
        if d1 - d0 == 1:
            ap = self.xs[:, d0, t0:t1]
        else:
            ap = self.xs[:, d0:d1, t0:t1]
        return View(ap, [("dr:xs%d" % d, t0, t1) for d in range(d0, d1)])

    def dview(self, name, ap, lo, hi):
        return View(ap, [("dr:" + name, lo, hi)])

    def sb(self, dtype, shape, np_=P):
        esz = 4 if dtype == F32 else 2
        n = int(np.prod(shape)) * esz
        n = (n + 31) // 32 * 32
        off = self.sb_top
        self.sb_top += n
        assert self.sb_top <= self.ARENA_BYTES, ("SBUF arena overflow", self.sb_top)
        return Buf(self, "sb", off, dtype, shape, np_)

    def ps(self, shape=(512,), np_=P, bank=None):
        if bank is None:
            bank = self.banks[self.bank_i % len(self.banks)]
            self.bank_i += 1
        return Buf(self, "ps", bank * 2048, F32, shape, np_)

    def build(self):
        cfg, nc = self.cfg, self.nc
        S, L, NF = cfg.S, cfg.depth, cfg.NF
        NCOL = n_cols(cfg.d_ff)
        dt = nc.dram_tensor
        self.x_in = dt("x", [S, D], F32, kind="ExternalInput").ap()
        self.w_in = dt("w_in", [L, D, IN_COLS], F32, kind="ExternalInput").ap()
        self.w_out = dt("w_out", [L, D, D], F32, kind="ExternalInput").ap()
        self.ffn_up = dt("ffn_up", [L, D, 2 * cfg.d_ff], F32, kind="ExternalInput").ap()
        self.ffn_down = dt("ffn_down", [L, cfg.d_ff, D], F32, kind="ExternalInput").ap()
        self.pool_w = dt("pool_w", [L, 4, P, P], F32, kind="ExternalInput").ap()
        self.lru_wa = dt("lru_wa", [L, NLB, P, P], F32, kind="ExternalInput").ap()
        self.lru_wx = dt("lru_wx", [L, NLB, P, P], F32, kind="ExternalInput").ap()
        self.cols_d = dt("cols", [L, P, NCOL], F32, kind="ExternalInput").ap()
        self.hrow_d = dt("hrow", [L, NH, 2], F32, kind="ExternalInput").ap()
        self.const_d = dt("consts", [P, 6 * P + NH * P + 64], F32, kind="ExternalInput").ap()
        self.y_out = dt("y", [S, D], F32, kind="ExternalOutput").ap()
        self.xs = dt("xs", [P, ND, S], F32).ap()
        self.n_units_M = 21 + 8
        self.n_units_F = NF // 2 * 2 + NF // 2
        self.wsM = dt("wsM", [self.n_units_M, P, 4096], BF16).ap()
        self.wsF = dt("wsF", [self.n_units_F, P, 4096], BF16).ap()

        self.ARENA_BYTES = 207 * 1024
        with ExitStack() as st:
            self.arena = st.enter_context(nc.sbuf_tensor("arena", [P, self.ARENA_BYTES // 4], F32))
            self.psum = [st.enter_context(nc.psum_tensor(f"psb{i}", [P, 512], F32)) for i in range(8)]
            self.sb_top = 0
            self.ident = self.sb(F32, (P,))
            self.ones = self.sb(F32, (P,))
            cd = self.const_d
            self.dma(self.ident.v(), self.dview("c", cd[:, 0:P], 0, 1), "c0")
            self.dma(self.ones.v(), self.dview("c", cd[:, P:2 * P], 0, 1), "c0")
            self.cols = self.sb(F32, (NCOL,))
            self.hrow = self.sb(F32, (2,), np_=NH)
            self.negA = self.sb(F32, (1,), np_=NH)
            self.lruc = self.sb(F32, (NLB,))
            self.pbs = self.sb(F32, (4,))
            self.sb_persist = self.sb_top
            for l in range(L):
                self.layer_setup(l)
                self.m_pass(l)
                self.f_pass(l)
            self.rec.emit(nc, st)
        return nc

    def layer_setup(self, l):
        self.sb_top = self.sb_persist
        t6 = self.sb(F32, (8,), np_=NH)
        tl = self.sb(F32, (NLB,))
        self.dma(self.cols.v(), self.dview("cols", self.cols_d[l], l, l + 1), "cols")
        self.dma(self.hrow.v(), self.dview("hrow", self.hrow_d[l], l, l + 1), "cols")
        self.act(t6.v((slice(0, 1),)), self.hrow.v((slice(0, 1),)), AF.Exp)
        self.ts(self.negA.v(), t6.v((slice(0, 1),)), -1.0, ALU.mult)
        lam = self.cols.v((slice(CO_LLAM, CO_LLAM + NLB),))
        self.act(tl.v(), lam, AF.Exp, scale=-1.0)
        self.act(tl.v(), tl.v(), AF.Ln, bias=1.0)
        self.ts(self.lruc.v(), tl.v(), -8.0, ALU.mult)
        self.tt(self.pbs.v(), self.cols.v((slice(CO_PB, CO_PB + 4),)), self.cols.v((slice(CO_PS, CO_PS + 4),)), ALU.mult)

    def fetch_unit(self, first, dst, ws, uidx, pieces, key):
        if first:
            for (src, dstv, stv) in pieces:
                if self.stage_sel is not None:
                    sidx = self.stage_sel
                else:
                    sidx = self.stage_rr
                    self.stage_rr = (self.stage_rr + 1) % len(self.stage)
                sv = stv(self.stage[sidx])
                self.dma(sv, src, f"stg{sidx}")
                self.copy(dstv, sv, eng=self.cast_engs[self.cast_rr % len(self.cast_engs)])
                self.cast_rr += 1
            self.dma(self.dview(ws[1], ws[0][uidx], uidx, uidx + 1), View(dst.base.rearrange("p a b -> p (a b)") if len(dst.shape) == 2 else dst.base, dst.v().regs), key + "o")
        else:
            self.dma(View(dst.base.rearrange("p a b -> p (a b)") if len(dst.shape) == 2 else dst.base, dst.v().regs),
                     self.dview(ws[1], ws[0][uidx], uidx, uidx + 1), key)

    def kunit_pieces(self, wl, cols, dst):
        pieces = []
        off = 0
        for (c0, w) in cols:
            for kh in range(2):
                src = wl[kh * 1024:(kh + 1) * 1024, c0:c0 + w].rearrange("(j p) c -> p j c", p=P)
                srcv = self.dview("w", src, 0, 1)
                dstv = dst.v((slice(kh * 8, kh * 8 + 8), slice(off, off + w)))
                pieces.append((srcv, dstv, (lambda w_: (lambda sb_: View(
                    sb_.base[:, 0:8 * w_].rearrange("p (j c) -> p j c", j=8), sb_.v().regs)))(w)))
            off += w
        return pieces

    def rmsnorm(self, xT, T, wcol0, out_fn, sq, rstd):
        for hh in range(T // 512):
            ts_ = slice(hh * 512, hh * 512 + 512)
            pss = self.ps()
            for j in range(ND):
                s = sq[j % 2].v((slice(0, 512),))
                self.act(s, xT.v((j, ts_)), AF.Square)
                self.mm(pss.v(), [(self.ones.v(), s)], start=(j == 0), stop=(j == ND - 1))
            r = rstd.v((ts_,))
            self.rsqrt(r, pss.v(), 1.0 / D, EPS)
            for j in range(ND):
                self.stt(out_fn(j, hh), xT.v((j, ts_)), self.cols.v((slice(wcol0 + j, wcol0 + j + 1),)), r,
                         ALU.mult, ALU.mult, eng=("dve" if j % 2 == 0 else "pool"))

    def m_pass(self, l):
        cfg = self.cfg
        S = cfg.S
        T = TM
        self.sb_top = self.sb_persist
        xT = self.sb(F32, (ND, TM))
        hT = self.sb(BF16, (ND, TM))
        mT = self.sb(BF16, (ND, TM))
        self.mT = mT
        self.stage = [self.sb(F32, (2048,)) for _ in range(2)]
        self.stage_rr = 0
        self.cast_engs = ["dve", "act"]
        self.cast_rr = 0
        wb = [self.sb(BF16, (ND, 256)) for _ in range(4)]
        wsmall = self.sb(F32, (16, P))
        self.wsmall = wsmall
        cd = self.const_d
        self.nmU = self.sb(F32, (P,))
        self.pmL = self.sb(F32, (P,))
        self.mU01 = self.sb(F32, (P,))
        self.sel6 = self.sb(F32, (NH * P,), np_=NH)
        self.poolrc = self.sb(F32, (4, 16))
        self.dma(self.nmU.v(), self.dview("c", cd[:, 2 * P:3 * P], 0, 1), "c0")
        self.dma(self.pmL.v(), self.dview("c", cd[:, 3 * P:4 * P], 0, 1), "c0")
        self.dma(self.mU01.v(), self.dview("c", cd[:, 4 * P:5 * P], 0, 1), "c0")
        self.dma(self.sel6.v(), self.dview("c", cd[0:NH, 6 * P:6 * P + NH * P], 0, 1), "c0")
        self.dma(self.poolrc.v(), self.dview("c", cd[:, 6 * P + NH * P:6 * P + NH * P + 64].rearrange(
            "p (a b) -> p a b", a=4), 0, 1), "c0")
        self.Sst = self.sb(F32, (NH, P))
        self.hst = self.sb(F32, (NLB,))
        self.car_g = self.sb(F32, (18, 3))
        self.car_l = self.sb(F32, (NLB, 3))
        self.car_p = self.sb(F32, (4, 16))
        for b_ in (self.Sst, self.hst, self.car_g, self.car_l, self.car_p):
            self.memset(b_.v(), 0.0, eng="pool")
        wt = lambda: self.sb(F32, (TM + 16,))
        TB = [wt() for _ in range(10)]
        TA = [wt() for _ in range(2)]
        TAd = [[wt() for _ in range(4)] for _ in range(2)]
        TC = [wt() for _ in range(8)]
        xtok = Buf(self, "sb", TA[0].off, F32, (D,))
        assert TAd[0][3].off + TAd[0][3].nbytes - TA[0].off >= D * 4
        rowb = self.sb(F32, (5, TM), np_=NH)
        cvo = [self.sb(BF16, (2048,)) for _ in range(2)]
        NF = cfg.NF
        wlu, wld = self.ffn_up[l], self.ffn_down[l]
        cv_pieces = []
        for pi in range(NF // 2):
            for gv in range(2):
                for kh in range(2):
                    c0 = gv * cfg.d_ff + pi * 256
                    src = wlu[kh * 1024:(kh + 1) * 1024, c0:c0 + 256].rearrange("(j p) c -> p j c", p=P)
                    cv_pieces.append((src, pi * 2 + gv, kh, True))
        for q in range(NF // 2):
            for i in range(2):
                f = q * 2 + i
                cv_pieces.append((wld[f * P:(f + 1) * P, :], NF + q, i, False))
        cv_pos = [0]
        NPc = len(cv_pieces)
        cv_seq = []
        for i in range(NPc + 2):
            if i < NPc:
                cv_seq.append(("in", i))
            if 1 <= i <= NPc:
                cv_seq.append(("cast", i - 1))
            if 2 <= i <= NPc + 1:
                cv_seq.append(("out", i - 2))
        colb = self.sb(F32, (4, 4, NH))
        cegl = self.sb(F32, (NH, 4))
        glast = self.sb(F32, (8,), np_=NH)
        for g in range(4):
            self.dma(wsmall.v((g,)), self.dview("pw", self.pool_w[l, g], 0, 1), "wsm")
        for j in range(NLB):
            self.dma(wsmall.v((4 + j,)), self.dview("pw", self.lru_wa[l, j], 0, 1), "wsm")
            self.dma(wsmall.v((10 + j,)), self.dview("pw", self.lru_wx[l, j], 0, 1), "wsm")
        wl_in, wl_out = self.w_in[l], self.w_out[l]
        ws = (self.wsM, "wsM")
        units = [[(C_POOL, 128), (C_POOL + 128, 128)], [(C_POOL + 256, 128), (C_POOL + 384, 128)], [(C_A, 12)]]
        for h in range(NH):
            units.append([(C_Q + h * P, P), (C_K + h * P, P)])
            units.append([(C_V + h * P, P), (C_Z + h * P, P)])
        for j in range(NLB):
            units.append([(C_XR + j * P, P), (C_GR + j * P, P)])
        nb = S // TM
        assert nb >= 2
        for b in range(nb):
            first = (b == 0)
            t0 = b * TM
            tsl = slice(t0, t0 + TM)

            def fetch(u, slot):
                if u < 21:
                    pcs = self.kunit_pieces(wl_in, units[u], wb[slot])
                else:
                    pcs = self.kunit_pieces(wl_out, [((u - 21) * 256, 256)], wb[slot])
                self.fetch_unit(first, wb[slot], ws, u, pcs, f"wb{slot}")

            def proj(slot, off, width):
                pso = self.ps((TM,), np_=width)
                self.mm(pso.v(), [(wb[slot].v((k, slice(off, off + width))), hT.v((k,))) for k in range(ND)])
                return pso
            def load_x(bb):
                tb0 = bb * TM
                if l == 0:
                    for tt_ in range(TM // P):
                        self.dma(xtok.v(), self.dview("x", self.x_in[tb0 + tt_ * P:tb0 + (tt_ + 1) * P, :], 0, 1), "xtok")
                        for g4 in range(4):
                            pst = self.ps((4, P))
                            for q in range(4):
                                self.tr(pst.v((q,)), xtok.v((slice((g4 * 4 + q) * P, (g4 * 4 + q + 1) * P),)))
                            self.copy(xT.v((slice(g4 * 4, g4 * 4 + 4), slice(tt_ * P, (tt_ + 1) * P))), pst.v(),
                                      eng=("act" if g4 % 2 else "dve"))
                    self.dma(self.xsv(0, ND, tb0, tb0 + TM), xT.v(), "xTo")
                else:
                    self.dma(xT.v(), self.xsv(0, ND, tb0, tb0 + TM), "xT")

            def norm_rows():
                self.rmsnorm(xT, TM, CO_N1, lambda j, hh: hT.v((j,)), [TB[0], TB[1]], TB[2])
                psa = proj(3, 0, NH)
                psb = proj(3, NH, NH)
                self.gdn_rows(psa, psb, rowb, colb, cegl, glast)

            if b == 0:
                fetch(2, 3)
                fetch(3, 0)
                fetch(4, 1)
                fetch(0, 2)
                load_x(0)
                norm_rows()

            def stream_A(h):
                self.stage_sel = 0
                pad, cv = TA
                qn, kn, vs, zs = TAd[h % 2]
                for (slot, off, ti, dst) in ((0, 0, 0, qn), (0, P, 1, kn), (1, 0, 2, vs)):
                    psx = proj(slot, off, P)
                    self.gdn_conv(psx, 3 * h + ti, pad, cv, dst)
                psz = proj(1, P, P)
                self.act(zs.v((slice(0, T),)), psz.v(), AF.Silu)
                if h + 1 < NH:
                    fetch(3 + 2 * (h + 1), 0)
                    fetch(4 + 2 * (h + 1), 1)
                sl = (slice(0, T),)
                sq, rn = pad, cv
                for (buf, scl) in ((kn, None), (qn, P ** -0.5)):
                    self.act(sq.v(sl), buf.v(sl), AF.Square)
                    pss = self.ps()
                    self.mm(pss.v(), [(self.ones.v(), sq.v(sl))])
                    self.rsqrt(rn.v(sl), pss.v(), 1.0, EPS)
                    if scl is None:
                        self.tt(buf.v(sl), buf.v(sl), rn.v(sl), ALU.mult)
                    else:
                        self.stt(buf.v(sl), buf.v(sl), scl, rn.v(sl), ALU.mult, ALU.mult)

            c_units = [0, 1, 15, 16, 17, 18, 19, 20]
            c_slot = lambda i: 2 + (i % 2)

            def stream_C(r):
                self.stage_sel = 1
                steps = [0, 1] if r == 0 else [r + 1]
                for i in steps:
                    if i + 1 < len(c_units):
                        fetch(c_units[i + 1], c_slot(i + 1))
                    slot = c_slot(i)
                    if i < 2:
                        for q in range(2):
                            self.pool_group(first, i * 2 + q, proj(slot, q * P, P), TC)
                    else:
                        j = i - 2
                        psx = proj(slot, 0, P)
                        psg = proj(slot, P, P)
                        self.lru_block(b == 0, j, psx, psg, TC)
                if r == 6:
                    fetch(21, 0)
                    fetch(22, 1)

            def cv_op(kind, i):
                src, u, hf, is_up = cv_pieces[i]
                stg, ob = self.stage[i % 2], cvo[i % 2]
                if is_up:
                    sv = View(stg.base.rearrange("p (j c) -> p j c", j=8), stg.v().regs)
                    ov = View(ob.base.rearrange("p (j c) -> p j c", j=8), ob.v().regs)
                else:
                    sv, ov = stg.v(), ob.v()
                if kind == "in":
                    self.dma(sv, self.dview("w", src, 0, 1), f"cvi{i % 2}")
                elif kind == "cast":
                    for q4 in range(4):
                        if is_up:
                            o_ = View(ov.ap[:, 2 * q4:2 * q4 + 2, :], ov.regs)
                            i_ = View(sv.ap[:, 2 * q4:2 * q4 + 2, :], sv.regs)
                        else:
                            o_ = View(ov.ap[:, 512 * q4:512 * q4 + 512], ov.regs)
                            i_ = View(sv.ap[:, 512 * q4:512 * q4 + 512], sv.regs)
                        self.copy(o_, i_, eng="pool")
                else:
                    self.dma(self.dview("wsF", self.wsF[u][:, hf * 2048:(hf + 1) * 2048], u, u + 1), ob.v(),
                             f"cvo{i % 2}")

            def stream_D(n):
                for _ in range(n):
                    if cv_pos[0] >= len(cv_seq):
                        return
                    kind, i = cv_seq[cv_pos[0]]
                    cv_pos[0] += 1
                    cv_op(kind, i)

            n_cv = -(-len(cv_seq) // (7 * (nb - 1)))
            for r in range(7):
                lists = []
                if b >= 1:
                    lists.append(self.stream(lambda: stream_D(n_cv), []))
                if r < NH:
                    lists.append(self.stream(lambda: stream_A(r), [0, 1]))
                if r >= 1:
                    lists.append(self.stream(lambda: self.gdn_B(r - 1, TB, TAd[(r - 1) % 2], rowb, colb, cegl), [2, 3, 4]))
                lists.append(self.stream(lambda: stream_C(r), [5, 6]))
                if r == 6 and b + 1 < nb:
                    lists.append(self.stream(lambda: load_x(b + 1), [0, 1]))
                self.merge(lists)
                self.stage_sel = None
            def out_proj():
                self.stage_sel = 0
                rot = TC[0:8]
                fetch(23, 2)

                def ld(dti):
                    self.dma(rot[dti % 8].v((slice(0, TM),)), self.xsv(dti, dti + 1, t0, t0 + TM), f"xr{dti % 8}")
                for dti in range(8):
                    ld(dti)
                for u in range(8):
                    slot = u % 3
                    for i in range(2):
                        dti = u * 2 + i
                        xt_ = rot[dti % 8].v((slice(0, TM),))
                        pso = self.ps()
                        self.mm(pso.v(), [(wb[slot].v((k, slice(i * P, i * P + P))), mT.v((k,))) for k in range(ND)])
                        self.tt(xt_, xt_, pso.v(), ALU.add)
                        self.dma(self.xsv(dti, dti + 1, t0, t0 + TM), xt_, f"xw{dti % 8}")
                        if dti + 8 < ND:
                            ld(dti + 8)
                    if u + 3 < 8:
                        fetch(21 + u + 3, slot)
                    elif b + 1 < nb:
                        fetch((3, 4, 0)[slot], slot)

            def next_head():
                self.stage_sel = 1
                fetch(2, 3)
                norm_rows()

            lists = [self.stream(out_proj, [0, 1, 2])]
            if b + 1 < nb:
                lists.append(self.stream(next_head, [3, 4, 5]))
            self.merge(lists)
            self.stage_sel = None
        assert cv_pos[0] == len(cv_seq)

    def pool_group(self, first, g, psu, TC):
        upad, la, lb, dd = TC[0], TC[1], TC[2], TC[3]
        mT, wsmall = self.mT, self.wsmall
        win = 2 ** (g + 1)
        T = TM
        self.copy(upad.v((slice(0, 16),)), self.car_p.v((g,)), eng="dve")
        self.act(upad.v((slice(16, 16 + T),)), psu.v(), AF.Copy)
        self.copy(self.car_p.v((g,)), upad.v((slice(T, T + 16),)), eng="dve")
        src = upad
        sh = 1
        bufs = [la, lb]
        for lev in range(g + 1):
            dst = bufs[lev % 2]
            self.tt(dst.v((slice(sh, 16 + T),)), src.v((slice(sh, 16 + T),)), src.v((slice(0, 16 + T - sh),)), ALU.add)
            src = dst
            sh *= 2
        self.stt(dd.v((slice(0, T),)), src.v((slice(16, 16 + T),)), 1.0 / win, upad.v((slice(16, 16 + T),)),
                 ALU.mult, ALU.subtract)
        if first:
            tmp = TC[4]
            self.tt(tmp.v((slice(0, 16),)), src.v((slice(16, 32),)), self.poolrc.v((g,)), ALU.mult)
            self.tt(dd.v((slice(0, 16),)), tmp.v((slice(0, 16),)), upad.v((slice(16, 32),)), ALU.subtract)
        psy = self.ps()
        self.mm(psy.v(), [(wsmall.v((g,)), dd.v((slice(0, T),)))])
        self.act(mT.v((g,)), psy.v(), AF.Identity, bias=self.pbs.v((slice(g, g + 1),)),
                 scale=self.cols.v((slice(CO_PS + g, CO_PS + g + 1),)))

    def gelu_tanh(self, out, x, t1, t2):
        self.act(t1, x, AF.Square)
        self.ts(t1, t1, 0.044715, ALU.mult, 1.0, ALU.add)
        self.tt(t1, t1, x, ALU.mult)
        self.act(t2, t1, AF.Sigmoid, scale=1.5957691216057308)
        self.tt(out, t2, x, ALU.mult)

    def lru_block(self, seq_start, j, psx, psg, TC):
        T = TM
        sl = (slice(0, T),)
        pad, xc, r, i_, a, gt, t1, t2 = TC
        th = pad
        mT, wsmall = self.mT, self.wsmall
        self.act(gt.v(sl), psg.v(), AF.Copy)
        self.copy(pad.v((slice(0, 3),)), self.car_l.v((j,)), eng="dve")
        self.act(pad.v((slice(3, 3 + T),)), psx.v(), AF.Copy)
        self.copy(self.car_l.v((j,)), pad.v((slice(T, T + 3),)), eng="dve")
        c = lambda tap: self.cols.v((slice(CO_LCW + tap * NLB + j, CO_LCW + tap * NLB + j + 1),))
        xcv = xc.v(sl)
        self.act(xcv, psx.v(), AF.Copy, scale=c(3), bias=self.cols.v((slice(CO_LCB + j, CO_LCB + j + 1),)))
        for tap in (2, 1, 0):
            self.stt(xcv, pad.v((slice(tap, tap + T),)), c(tap), xcv, ALU.mult, ALU.add)
        psr = self.ps()
        self.mm(psr.v(), [(wsmall.v((4 + j,)), xcv)])
        psi = self.ps()
        self.mm(psi.v(), [(wsmall.v((10 + j,)), xcv)])
        self.act(r.v(sl), psr.v(), AF.Sigmoid, bias=self.cols.v((slice(CO_LBA + j, CO_LBA + j + 1),)))
        self.act(i_.v(sl), psi.v(), AF.Sigmoid, bias=self.cols.v((slice(CO_LBX + j, CO_LBX + j + 1),)))
        lc = self.lruc.v((slice(j, j + 1),))
        self.act(a.v(sl), r.v(sl), AF.Exp, scale=lc)
        self.act(th.v(sl), r.v(sl), AF.Tanh, scale=lc)
        self.tt(t1.v(sl), a.v(sl), a.v(sl), ALU.mult)
        self.stt(t1.v(sl), t1.v(sl), 1.0, th.v(sl), ALU.add, ALU.mult)
        self.act(t1.v(sl), t1.v(sl), AF.Sqrt, scale=-1.0)
        if seq_start:
            self.memset(t1.v((slice(0, 1),)), 1.0)
        self.tt(t1.v(sl), t1.v(sl), i_.v(sl), ALU.mult)
        self.tt(t1.v(sl), t1.v(sl), xcv, ALU.mult)
        hv = r.v(sl)
        self.scan(hv, a.v(sl), t1.v(sl), self.hst.v((slice(j, j + 1),)), ALU.mult, ALU.add)
        self.copy(self.hst.v((slice(j, j + 1),)), r.v((slice(T - 1, T),)), eng="dve")
        self.gelu_tanh(gt.v(sl), gt.v(sl), t2.v(sl), i_.v(sl))
        self.tt(mT.v((10 + j,)), hv, gt.v(sl), ALU.mult)

    def gdn_rows(self, psa, psb, rowb, colb, cegl, glast):
        T = TM
        R = lambda i: rowb.v((i,))
        beta, gc, bg, egl, t1 = R(0), R(1), R(2), R(3), R(4)
        p6 = (0, NH)
        self.act(beta, psb.v(), AF.Sigmoid)
        self.act(t1, psa.v(), AF.Exp, bias=self.hrow.v((slice(1, 2),)))
        self.act(t1, t1, AF.Ln, bias=1.0)
        self.ts(t1, t1, self.negA.v(), ALU.mult)
        ones6 = self.ones.v((slice(0, CH),), p=p6)
        for c in range(T // CH):
            cs = slice(c * CH, (c + 1) * CH)
            self.scan(rowb.v((1, cs)), ones6, rowb.v((4, cs)), 0.0, ALU.mult, ALU.add)
            self.copy(glast.v((slice(c, c + 1),)), rowb.v((1, slice(c * CH + CH - 1, (c + 1) * CH))))
        self.act(t1, gc, AF.Exp)
        self.tt(bg, beta, t1, ALU.mult)
        for c in range(T // CH):
            cs = slice(c * CH, (c + 1) * CH)
            self.act(rowb.v((3, cs)), rowb.v((1, cs)), AF.Exp, scale=-1.0, bias=glast.v((slice(c, c + 1),)))
        self.act(glast.v((slice(4, 8),)), glast.v((slice(0, 4),)), AF.Exp)
        psc = self.ps((4, 4, NH))
        id6 = self.ident.v((slice(0, NH),), p=p6)
        for c in range(T // CH):
            cs = slice(c * CH, (c + 1) * CH)
            for qi, row in enumerate((1, 0, 2, 3)):
                self.mm(psc.v((c, qi)), [(rowb.v((row, cs)), id6)])
        self.copy(colb.v(), psc.v())
        pse = self.ps((NH, 4))
        for h in range(NH):
            self.mm(pse.v((h,)), [(self.sel6.v((slice(h * P, (h + 1) * P),)), glast.v((slice(4, 8),)))])
        self.copy(cegl.v(), pse.v(), eng="act")

    def gdn_conv(self, psx, tile_i, pad, cv, out):
        T = TM
        car = self.car_g.v((tile_i,))
        self.copy(pad.v((slice(0, 3),)), car, eng="dve")
        self.act(pad.v((slice(3, 3 + T),)), psx.v(), AF.Copy)
        self.copy(car, pad.v((slice(T, T + 3),)), eng="dve")
        c = lambda tap: self.cols.v((slice(CO_GCW + tap * 18 + tile_i, CO_GCW + tap * 18 + tile_i + 1),))
        cvv = cv.v((slice(0, T),))
        self.act(cvv, psx.v(), AF.Copy, scale=c(3))
        for tap in (2, 1, 0):
            self.stt(cvv, pad.v((slice(tap, tap + T),)), c(tap), cvv, ALU.mult, ALU.add)
        self.act(out.v((slice(0, T),)), cvv, AF.Silu)

    def gdn_B(self, h, TB, TAq, rowb, colb, cegl):
        T = TM
        NCk = T // CH
        sl = (slice(0, T),)
        mT = self.mT
        qn, kn, vs, zs = TAq
        Rg, Rb, Re, Dm, EU, EL, EUs, kbg, kd, vb = TB
        Nb = [EL, Rg]
        Pb = [EUs, Rb]
        Q = Dm
        selh = self.sel6.v((slice(h * P, (h + 1) * P),))
        for (row, dst, eng) in ((1, Rg, "act"), (0, Rb, "dve")):
            psr = self.ps()
            self.mm(psr.v(), [(selh, rowb.v((row,)))])
            self.copy(dst.v(sl), psr.v(), eng=eng)
        self.act(Re.v(sl), Rg.v(sl), AF.Exp)

        def v3(buf):
            vv = buf.v(sl)
            return View(vv.ap.rearrange("p (c f) -> p c f", c=NCk), vv.regs)

        def colq(qi):
            vv = colb.v((slice(None), qi, slice(h, h + 1)))
            return View(vv.ap.broadcast_to([P, NCk, CH]), vv.regs)

        def bcm(m):
            vv = m.v()
            return View(vv.ap.rearrange("p (o f) -> p o f", o=1).broadcast_to([P, NCk, CH]), vv.regs)
        self.tt(v3(Dm), v3(Rg), colq(0), ALU.subtract)
        self.stt(v3(EU), v3(Dm), 0.0, bcm(self.nmU), ALU.min, ALU.add)
        self.act(EU.v(sl), EU.v(sl), AF.Exp)
        self.stt(v3(EL), v3(Dm), 0.0, bcm(self.pmL), ALU.max, ALU.add)
        self.act(EL.v(sl), EL.v(sl), AF.Exp, scale=-1.0)
        self.tt(v3(EUs), v3(EU), bcm(self.mU01), ALU.mult)
        self.tt(EUs.v(sl), EUs.v(sl), Rb.v(sl), ALU.mult)
        self.tt(v3(EL), v3(EL), colq(1), ALU.mult)
        qd = Re
        self.tt(qd.v(sl), qn.v(sl), Re.v(sl), ALU.mult)
        pkk = self.ps()
        pqk = self.ps()
        for c in range(NCk):
            cs = (slice(c * CH, (c + 1) * CH),)
            self.mm(pkk.v(cs), [(kn.v(cs), kn.v(cs))])
            self.mm(pqk.v(cs), [(kn.v(cs), qn.v(cs))])
        N0, P0, attnT = EL, EUs, EU
        self.stt(P0.v(sl), pkk.v(), -1.0, EUs.v(sl), ALU.mult, ALU.mult)
        self.stt(N0.v(sl), pkk.v(), -1.0, EL.v(sl), ALU.mult, ALU.mult)
        self.tt(attnT.v(sl), pqk.v(), EU.v(sl), ALU.mult)
        self.tt(v3(Q), v3(P0), bcm(self.ident), ALU.add)
        ptk = self.ps()
        ptv = self.ps()
        for c in range(NCk):
            cs = (slice(c * CH, (c + 1) * CH),)
            self.tr(ptk.v(cs), kn.v(cs))
            self.tr(ptv.v(cs), vs.v(cs))
        ptk3 = View(ptk.v().ap.rearrange("p (c f) -> p c f", c=NCk), ptk.v().regs)
        ptv3 = View(ptv.v().ap.rearrange("p (c f) -> p c f", c=NCk), ptv.v().regs)
        self.tt(v3(kbg), ptk3, colq(2), ALU.mult)
        self.tt(v3(kd), ptk3, colq(3), ALU.mult)
        self.tt(v3(vb), ptv3, colq(1), ALU.mult)
        for k in range(1, 7):
            Np, Pp, Nn, Pn = Nb[(k - 1) % 2], Pb[(k - 1) % 2], Nb[k % 2], Pb[k % 2]
            pn = self.ps()
            for c in range(NCk):
                cs = (slice(c * CH, (c + 1) * CH),)
                self.mm(pn.v(cs), [(Pp.v(cs), Np.v(cs))])
            self.copy(Nn.v(sl), pn.v(), eng="act")
            if k < 6:
                pp = self.ps()
                for c in range(NCk):
                    cs = (slice(c * CH, (c + 1) * CH),)
                    self.mm(pp.v(cs), [(Np.v(cs), Pp.v(cs))])
                self.copy(Pn.v(sl), pp.v(), eng="dve")
            pq = self.ps()
            for c in range(NCk):
                cs = (slice(c * CH, (c + 1) * CH),)
                self.mm(pq.v(cs), [(Nn.v(cs), Q.v(cs))])
            self.tt(Q.v(sl), Q.v(sl), pq.v(), ALU.add)
        WT, U = Nb[0], Pb[0]
        pw = self.ps()
        pu = self.ps()
        for c in range(NCk):
            cs = (slice(c * CH, (c + 1) * CH),)
            self.mm(pw.v(cs), [(kbg.v(cs), Q.v(cs))])
            self.mm(pu.v(cs), [(Q.v(cs), vb.v(cs))])
        self.copy(WT.v(sl), pw.v(), eng="act")
        self.copy(U.v(sl), pu.v(), eng="dve")
        Sh = self.Sst.v((h,))
        vnew = Nb[1]
        pso = self.ps(bank=7)
        for c in range(NCk):
            cs = (slice(c * CH, (c + 1) * CH),)
            pv = self.ps((CH,))
            self.mm(pv.v(), [(WT.v(cs), Sh)])
            self.tt(vnew.v(cs), U.v(cs), pv.v(), ALU.subtract)
            self.mm(pso.v(cs), [(Sh, qd.v(cs)), (vnew.v(cs), attnT.v(cs))])
            pss_ = self.ps((CH,))
            self.mm(pss_.v(), [(kd.v(cs), vnew.v(cs))])
            self.stt(Sh, Sh, cegl.v((h, slice(c, c + 1))), pss_.v(), ALU.mult, ALU.add)
        osq, rs, on = Dm, Rb, kbg
        self.act(osq.v(sl), pso.v(), AF.Square)
        pss2 = self.ps()
        self.mm(pss2.v(), [(self.ones.v(), osq.v(sl))])
        self.rsqrt(rs.v(sl), pss2.v(), 1.0 / P, EPS)
        self.stt(on.v(sl), pso.v(), self.cols.v((slice(CO_GNW, CO_GNW + 1),)), rs.v(sl), ALU.mult, ALU.mult)
        self.tt(mT.v((4 + h,)), on.v(sl), zs.v(sl), ALU.mult)

    def f_pass(self, l):
        cfg = self.cfg
        S, NF, L = cfg.S, cfg.NF, cfg.depth
        self.sb_top = self.sb_persist
        xT = self.sb(F32, (ND, TF))
        hT = self.sb(BF16, (ND, TF))
        NWU = 4
        wup = [self.sb(BF16, (ND, 256)) for _ in range(NWU)]
        wdn = [self.sb(BF16, (2, D)) for _ in range(4)]
        actb = [self.sb(BF16, (4, TF)) for _ in range(2)]
        self.car_f = self.sb(F32, (NF, 2))
        self.memset(self.car_f.v(), 0.0, eng="pool")
        mark = self.sb_top
        otok = self.sb(F32, (D,))
        self.sb_top = mark
        gbuf = [self.sb(F32, (2 + TF,)) for _ in range(2)]
        tb = [self.sb(F32, (512,)) for _ in range(3)]
        wlu, wld = self.ffn_up[l], self.ffn_down[l]
        ws = (self.wsF, "wsF")
        last = (l == L - 1)
        nb = S // TF
        NCH = NF // 4
        for b in range(nb):
            first = False
            t0 = b * TF
            tsl = slice(t0, t0 + TF)
            self.dma(xT.v(), self.xsv(0, ND, t0, t0 + TF), "xTf")
            rstd = gbuf[0]
            self.rmsnorm(xT, TF, CO_N2, lambda j, hh: hT.v((j, slice(hh * 512, hh * 512 + 512))), [tb[0], tb[1]], rstd)
            self.urr = 0
            self.drr = 0

            def up_chunk(ci):
                ab = actb[ci % 2]
                for pr in range(2):
                    f0 = ci * 4 + pr * 2
                    ug = wup[self.urr % NWU]
                    self.urr += 1
                    self.fetch_unit(first, ug, ws, (f0 // 2) * 2, self.kunit_pieces(wlu, [(f0 * P, 256)], ug),
                                    f"wu{(self.urr - 1) % NWU}")
                    uv = wup[self.urr % NWU]
                    self.urr += 1
                    self.fetch_unit(first, uv, ws, (f0 // 2) * 2 + 1,
                                    self.kunit_pieces(wlu, [(cfg.d_ff + f0 * P, 256)], uv), f"wu{(self.urr - 1) % NWU}")
                    for i in range(2):
                        f = f0 + i
                        gb = gbuf[f % 2]
                        self.copy(gb.v((slice(0, 2),)), self.car_f.v((f,)), eng="pool")
                        for hh in range(TF // 512):
                            hs = slice(hh * 512, hh * 512 + 512)
                            psg = self.ps()
                            self.mm(psg.v(), [(ug.v((k, slice(i * P, i * P + P))), hT.v((k, hs))) for k in range(ND)])
                            psv = self.ps()
                            self.mm(psv.v(), [(uv.v((k, slice(i * P, i * P + P))), hT.v((k, hs))) for k in range(ND)])
                            self.act(gb.v((slice(2 + hh * 512, 2 + hh * 512 + 512),)), psg.v(), AF.Copy)
                            cw = lambda tap: self.cols.v((slice(CO_FCW + tap * NF + f, CO_FCW + tap * NF + f + 1),))
                            t = tb[0].v()
                            self.act(t, psg.v(), AF.Copy, scale=cw(2))
                            self.stt(t, gb.v((slice(1 + hh * 512, 1 + hh * 512 + 512),)), cw(1), t, ALU.mult, ALU.add)
                            self.stt(t, gb.v((slice(hh * 512, hh * 512 + 512),)), cw(0), t, ALU.mult, ALU.add, eng="pool")
                            self.gelu_tanh(t, t, tb[1].v(), tb[2].v())
                            self.tt(ab.v((pr * 2 + i, hs)), t, psv.v(), ALU.mult)
                        self.copy(self.car_f.v((f,)), gb.v((slice(TF, TF + 2),)), eng="pool")

            def down_chunk(ci):
                ab = actb[ci % 2]
                wd = []
                for pr in range(2):
                    dstb = wdn[self.drr % 4]
                    self.drr += 1
                    u = NF + ci * 2 + pr
                    pieces = []
                    for i in range(2):
                        f = ci * 4 + pr * 2 + i
                        srcv = self.dview("w", wld[f * P:(f + 1) * P, :], 0, 1)
                        pieces.append((srcv, dstb.v((i,)), lambda sb_: sb_.v()))
                    self.fetch_unit(first, dstb, ws, u, pieces, f"wd{(self.drr - 1) % 4}")
                    wd.append(dstb)
                for dti in range(ND):
                    for hh in range(TF // 512):
                        hs = slice(hh * 512, hh * 512 + 512)
                        pso = self.ps()
                        self.mm(pso.v(), [(wd[q // 2].v((q % 2, slice(dti * P, dti * P + P))), ab.v((q, hs)))
                                          for q in range(4)])
                        xv = xT.v((dti, hs))
                        self.tt(xv, xv, pso.v(), ALU.add)

            up_chunk(0)
            for ci in range(1, NCH):
                up_chunk(ci)
                down_chunk(ci - 1)
            down_chunk(NCH - 1)
            if not last:
                self.dma(self.xsv(0, ND, t0, t0 + TF), xT.v(), "xTfo")
            else:
                rstd = gbuf[0]
                self.rmsnorm(xT, TF, CO_NF, lambda j, hh: xT.v((j, slice(hh * 512, hh * 512 + 512))),
                             [tb[0], tb[1]], rstd)
                for tt_ in range(TF // P):
                    for g4 in range(4):
                        pst = self.ps()
                        for q in range(4):
                            self.tr(pst.v((slice(q * P, (q + 1) * P),)), xT.v((g4 * 4 + q, slice(tt_ * P, (tt_ + 1) * P))))
                        self.copy(otok.v((slice(g4 * 512, g4 * 512 + 512),)), pst.v(), eng=("act" if g4 % 2 else "dve"))
                    self.dma(self.dview("y", self.y_out[t0 + tt_ * P:t0 + (tt_ + 1) * P, :], t0 + tt_, t0 + tt_ + 1),
                             otok.v(), "otok")


def make_consts():
    c = np.zeros((P, 6 * P + NH * P + 64), np.float32)
    p = np.arange(P)[:, None]
    f = np.arange(P)[None, :]
    c[:, 0:P] = np.eye(P)
    c[:, P:2 * P] = 1.0
    c[:, 2 * P:3 * P] = np.where(f >= p, 0.0, NEG)
    c[:, 3 * P:4 * P] = np.where(p > f, 0.0, -NEG)
    c[:, 4 * P:5 * P] = np.where(f > p, 1.0, 0.0)
    for h in range(NH):
        c[h, 6 * P + h * P:6 * P + (h + 1) * P] = 1.0
    for g in range(4):
        win = 2 ** (g + 1)
        t = np.arange(16)
        c[:, 6 * P + NH * P + g * 16:6 * P + NH * P + (g + 1) * 16] = (1.0 / np.minimum(t + 1, win))[None, :]
    return c


def make_cols(inp, L, d_ff):
    NF = d_ff // P
    cols = np.zeros((L, P, n_cols(d_ff)), np.float32)
    col = lambda v: np.ascontiguousarray(np.asarray(v, np.float32).reshape(-1, P).T)
    for l in range(L):
        c = cols[l]
        c[:, CO_N1:CO_N1 + 16] = col(inp["norm1_w"][l])
        c[:, CO_N2:CO_N2 + 16] = col(inp["norm2_w"][l])
        c[:, CO_NF:CO_NF + 16] = col(inp["final_norm_w"])
        c[:, CO_PB:CO_PB + 4] = col(inp["pool_b"][l])
        c[:, CO_PS:CO_PS + 4] = col(inp["pool_scale"][l])
        gcw = np.asarray(inp["gdn_conv_w"][l], np.float32)
        for tap in range(4):
            t18 = col(gcw[tap])
            for h in range(NH):
                for qi in range(3):
                    c[:, CO_GCW + tap * 18 + 3 * h + qi] = t18[:, qi * NH + h]
        c[:, CO_GNW] = np.asarray(inp["gdn_norm_w"][l], np.float32)
        lcw = np.asarray(inp["lru_conv_w"][l], np.float32)
        for tap in range(4):
            c[:, CO_LCW + tap * NLB:CO_LCW + (tap + 1) * NLB] = col(lcw[tap])
        c[:, CO_LCB:CO_LCB + NLB] = col(inp["lru_conv_b"][l])
        c[:, CO_LBA:CO_LBA + NLB] = col(inp["lru_ba"][l])
        c[:, CO_LBX:CO_LBX + NLB] = col(inp["lru_bx"][l])
        c[:, CO_LLAM:CO_LLAM + NLB] = col(inp["lru_lambda"][l])
        fcw = np.asarray(inp["ffn_conv_w"][l], np.float32)
        for tap in range(3):
            c[:, CO_FCW + tap * NF:CO_FCW + (tap + 1) * NF] = col(fcw[tap])
    return cols


_NC_CACHE = {}
SPREAD = True


def run(inp, S, L, d_ff, n_cores):
    cfg = Cfg(S, L, d_ff)
    keyc = (S, L, d_ff)
    if keyc not in _NC_CACHE:
        _NC_CACHE[keyc] = K(cfg).build()
    nc = _NC_CACHE[keyc]
    f = lambda a: np.ascontiguousarray(np.asarray(a, np.float32))
    hrow = np.stack([f(inp["gdn_a_log"]), f(inp["gdn_dt_bias"])], axis=-1)
    shared = {
        "w_in": f(inp["w_in"]), "w_out": f(inp["w_out"]), "ffn_up": f(inp["ffn_up"]), "ffn_down": f(inp["ffn_down"]),
        "pool_w": f(inp["pool_w"]), "lru_wa": f(inp["lru_wa"]), "lru_wx": f(inp["lru_wx"]),
        "cols": make_cols(inp, L, d_ff), "hrow": np.ascontiguousarray(hrow), "consts": make_consts(),
    }
    x = f(inp["x"])
    if n_cores == 4 and SPREAD:
        slots = [0, 1, 4, 5]
        zero = {k: np.zeros_like(v) for k, v in shared.items()}
        zero["x"] = np.zeros_like(x[0])
        in_maps = [zero] * 8
        in_maps = list(in_maps)
        for i, sl_ in enumerate(slots):
            in_maps[sl_] = dict(shared, x=np.ascontiguousarray(x[i]))
        res = run_bass_kernel_spmd(nc, in_maps, core_ids=list(range(8)))
        return np.stack([np.asarray(res.results[sl_]["y"], np.float32) for sl_ in slots], axis=0)
    in_maps = [dict(shared, x=np.ascontiguousarray(x[i])) for i in range(n_cores)]
    res = run_bass_kernel_spmd(nc, in_maps, core_ids=list(range(n_cores)))
    return np.stack([np.asarray(res.results[i]["y"], np.float32) for i in range(n_cores)], axis=0)


def kernel(**inputs):
    x = np.asarray(inputs["x"])
    B, S, _ = x.shape
    L = np.asarray(inputs["w_in"]).shape[0]
    d_ff = np.asarray(inputs["ffn_down"]).shape[1]
    return run(inputs, S, L, d_ff, B)
```

```python
import numpy as np
from contextlib import ExitStack
import concourse.bass as bass
import concourse.mybir as mybir
from concourse.bass_utils import run_bass_kernel_spmd

F32 = mybir.dt.float32
BF16 = mybir.dt.bfloat16
AF = mybir.ActivationFunctionType
ALU = mybir.AluOpType
P = 128
EPS = 1e-6
NEG = -30000.0

D = 2048
ND = D // P
POOL_W = 512
GDN_W = 768
NH = 6
LRU_W = 768
NLB = 6
IN_COLS = POOL_W + 4 * GDN_W + 2 * NH + 2 * LRU_W
C_POOL, C_Q, C_K, C_V, C_Z = 0, 512, 1280, 2048, 2816
C_A, C_B, C_XR, C_GR = 3584, 3590, 3596, 4364
TM = 512
TF = 1024
CH = 128

CO_N1, CO_N2, CO_NF, CO_PB, CO_PS, CO_GCW, CO_GNW, CO_LCW, CO_LCB, CO_LBA, CO_LBX, CO_LLAM = (
    0, 16, 32, 48, 52, 56, 128, 129, 153, 159, 165, 171)
CO_FCW = 177


def n_cols(d_ff):
    return CO_FCW + 3 * (d_ff // P)


class View:
    __slots__ = ("ap", "regs")

    def __init__(self, ap, regs):
        self.ap = ap
        self.regs = regs


class Buf:
    def __init__(self, k, space, off, dtype, shape, np_=P):
        self.k, self.space, self.off, self.dtype, self.shape, self.np = k, space, off, dtype, tuple(shape), np_
        self.esz = 4 if dtype == F32 else 2
        n = 1
        for s in shape:
            n *= s
        self.n = n
        self.nbytes = n * self.esz
        if space == "sb":
            w0 = off // 4
            base = k.arena[0:np_, w0:w0 + (self.nbytes + 3) // 4]
            if dtype != F32:
                base = base.bitcast(dtype)
        else:
            bank = off // 2048
            w0 = (off % 2048) // 4
            base = k.psum[bank][0:np_, w0:w0 + (self.nbytes + 3) // 4]
            if dtype != F32:
                base = base.bitcast(dtype)
        if len(shape) == 2:
            base = base.rearrange("p (a b) -> p a b", a=shape[0])
        elif len(shape) == 3:
            base = base.rearrange("p (a b c) -> p a b c", a=shape[0], b=shape[1])
        self.base = base

    def __getitem__(self, idx):
        if not isinstance(idx, tuple):
            idx = (idx,)
        return self.v(idx)

    def v(self, idx=(), p=None):
        shape = self.shape
        idx = tuple(idx) + (slice(None),) * (len(shape) - len(idx))
        lo = 0
        hi = 0
        stride = self.n
        for s, i in zip(shape, idx):
            stride //= s
            if isinstance(i, int):
                a, b = i, i + 1
            else:
                a = 0 if i.start is None else i.start
                b = s if i.stop is None else i.stop
                assert i.step is None
            assert 0 <= a < b <= s, (shape, idx)
            lo += a * stride
            hi += (b - 1) * stride
        hi += 1
        p0, p1 = (0, self.np) if p is None else p
        ap = self.base[(slice(p0, p1),) + idx]
        return View(ap, [(self.space, self.off + lo * self.esz, self.off + hi * self.esz)])


class Op:
    __slots__ = ("eng", "fn", "deps", "seq", "key", "val", "prev_val", "i")


ENGS = ("pe", "act", "dve", "pool", "sp")
EPOCH = 12000
BUCKET = 512


class Rec:
    def __init__(self):
        self.ops = []
        self.by_eng = {e: [] for e in ENGS}
        self.wr = {}
        self.rd = {}
        self.dma_val = {}

    def _buckets(self, sp, lo, hi):
        return [(sp, b) for b in range(lo // BUCKET, (hi - 1) // BUCKET + 1)]

    def add(self, eng, fn, reads, writes, key=None, ndma=1):
        op = Op()
        op.eng, op.fn, op.key, op.i = eng, fn, key, len(self.ops)
        deps = set()
        rregs = [r for v in reads if v is not None and isinstance(v, View) for r in v.regs]
        wregs = [r for v in writes for r in v.regs]
        for (sp, lo, hi) in rregs:
            for bk in self._buckets(sp, lo, hi):
                for (a, b, o) in self.wr.get(bk, ()):
                    if a < hi and lo < b:
                        deps.add(o)
        for (sp, lo, hi) in wregs:
            for bk in self._buckets(sp, lo, hi):
                for (a, b, o) in self.wr.get(bk, ()):
                    if a < hi and lo < b:
                        deps.add(o)
                for (a, b, _e), o in self.rd.get(bk, {}).items():
                    if a < hi and lo < b:
                        deps.add(o)
        deps.discard(op.i)
        op.deps = deps
        if key is not None:
            pv = self.dma_val.get(key, 0)
            op.prev_val = pv
            op.val = pv + 16 * ndma
            self.dma_val[key] = op.val
            op.seq = None
        else:
            op.seq = len(self.by_eng[eng])
        for (sp, lo, hi) in wregs:
            for bk in self._buckets(sp, lo, hi):
                blo, bhi = bk[1] * BUCKET, (bk[1] + 1) * BUCKET
                l = self.wr.setdefault(bk, [])
                l[:] = [(a, b, o) for (a, b, o) in l if not (lo <= max(a, blo) and min(b, bhi) <= hi)]
                l.append((lo, hi, op.i))
                d = self.rd.get(bk)
                if d:
                    for kk in [kk for kk in d if lo <= max(kk[0], blo) and min(kk[1], bhi) <= hi]:
                        del d[kk]
        ek = eng if key is None else ("dma", op.i)
        for (sp, lo, hi) in rregs:
            for bk in self._buckets(sp, lo, hi):
                self.rd.setdefault(bk, {})[(lo, hi, ek)] = op.i
        self.ops.append(op)
        self.by_eng[eng].append(op)
        return op

    def emit(self, nc, stack):
        esem = {}
        for e in ("pe", "act", "dve", "pool"):
            n = len(self.by_eng[e])
            esem[e] = [stack.enter_context(nc.semaphore(f"s_{e}{i}")) for i in range(n // EPOCH + 1)]
        dsem = {k: stack.enter_context(nc.semaphore(f"d_{k}")) for k in self.dma_val}
        ops = self.ops
        block = stack.enter_context(nc.Block())

        def run(eng, e):
            waited = {}

            def wait(sem, val, tag):
                if waited.get(tag, 0) < val:
                    e.wait_ge(sem, val)
                    waited[tag] = val

            for op in self.by_eng[eng]:
                for di in sorted(op.deps):
                    d = ops[di]
                    if d.key is not None:
                        wait(dsem[d.key], d.val, ("d", d.key))
                    else:
                        if d.eng == "pe" and eng == "pe":
                            continue
                        ep = d.seq // EPOCH
                        wait(esem[d.eng][ep], d.seq % EPOCH + 1, (d.eng, ep))
                if op.key is not None:
                    if op.prev_val:
                        wait(dsem[op.key], op.prev_val, ("d", op.key))
                    op.fn(e, dsem[op.key])
                else:
                    ins = op.fn(e)
                    ins.then_inc(esem[eng][op.seq // EPOCH], 1)
            if eng == "sp":
                for k, v in self.dma_val.items():
                    wait(dsem[k], v, ("d", k))

        @block.tensor
        def _(e):
            run("pe", e)

        @block.scalar
        def _(e):
            run("act", e)

        @block.vector
        def _(e):
            run("dve", e)

        @block.gpsimd
        def _(e):
            run("pool", e)

        @block.sync
        def _(e):
            run("sp", e)


class Cfg:
    def __init__(self, S=4096, depth=2, d_ff=6144):
        self.S, self.depth, self.d_ff = S, depth, d_ff
        self.NF = d_ff // P
        assert S % TF == 0 and self.NF % 4 == 0


class K:
    def __init__(self, cfg):
        self.cfg = cfg
        self.nc = bass.Bass("TRN2", target_bir_lowering=False)
        self.rec = Rec()
        self.cur = None
        self.stage_sel = None
        self.banks = list(range(7))
        self.bank_i = 0

    def _emit(self, eng, fn, reads, writes, key=None):
        if self.cur is not None:
            self.cur.append((eng, fn, reads, writes, key))
        else:
            self.rec.add(eng, fn, reads, writes, key=key)

    def stream(self, fn, banks):
        assert self.cur is None
        save = (self.banks, self.bank_i)
        self.cur, self.banks, self.bank_i = [], banks, 0
        fn()
        out = self.cur
        self.cur = None
        self.banks, self.bank_i = save
        return out

    def _cost(self, a):
        eng, fn, reads, writes, key = a
        n = 1
        for d in writes[0].ap.shape[1:]:
            n *= d
        if key is not None:
            return 2.0 + n * 128 * 4 / 300e3
        if eng == "pe":
            c = getattr(fn, "cost", None)
            return c if c is not None else 0.2
        if eng == "dve":
            return 0.12 + n / 960.0
        if eng == "act":
            return 0.15 + n / 1100.0
        return 0.2 + n * 0.0035

    def merge(self, lists):
        lists = [l for l in lists if l]
        deps = []
        for l in lists:
            d = []
            wr, rd = [], []
            for i, a in enumerate(l):
                rr = [r for v in a[2] if isinstance(v, View) for r in v.regs]
                ww = [r for v in a[3] for r in v.regs]
                s_ = set()
                for (sp, lo, hi) in rr:
                    for (sp2, a2, b2, o) in wr:
                        if sp == sp2 and a2 < hi and lo < b2:
                            s_.add(o)
                for (sp, lo, hi) in ww:
                    for (sp2, a2, b2, o) in wr:
                        if sp == sp2 and a2 < hi and lo < b2:
                            s_.add(o)
                    for (sp2, a2, b2, o) in rd:
                        if sp == sp2 and a2 < hi and lo < b2:
                            s_.add(o)
                for (sp, lo, hi) in ww:
                    wr = [w for w in wr if not (w[0] == sp and lo <= w[1] and w[2] <= hi)]
                    rd = [w for w in rd if not (w[0] == sp and lo <= w[1] and w[2] <= hi)]
                    wr.append((sp, lo, hi, i))
                for (sp, lo, hi) in rr:
                    rd.append((sp, lo, hi, i))
                if len(rd) > 200:
                    rd = rd[-200:]
                d.append(s_)
            deps.append(d)
        pos = [0] * len(lists)
        fin = [[0.0] * len(l) for l in lists]
        efree = {e: 0.0 for e in ENGS}
        while True:
            best, bi = None, -1
            for i, l in enumerate(lists):
                if pos[i] < len(l):
                    a = l[pos[i]]
                    rdy = 0.0
                    for o in deps[i][pos[i]]:
                        t = fin[i][o] + 0.3
                        if t > rdy:
                            rdy = t
                    st = max(rdy, efree[a[0]])
                    if best is None or st < best - 1e-9:
                        best, bi = st, i
            if bi < 0:
                break
            a = lists[bi][pos[bi]]
            c = self._cost(a)
            if a[4] is not None:
                efree[a[0]] = best + 0.05
            else:
                efree[a[0]] = best + c
            fin[bi][pos[bi]] = best + c
            pos[bi] += 1
            self.rec.add(a[0], a[1], a[2], a[3], key=a[4])

    def A(self, v):
        return v.ap if isinstance(v, View) else v

    def mm(self, out, pairs, start=True, stop=True):
        reads = [x for pr in pairs for x in pr]

        def fn(e):
            ins = None
            n = len(pairs)
            for i, (l, r) in enumerate(pairs):
                ins = e.matmul(out.ap, l.ap, r.ap, start=(start and i == 0), stop=(stop and i == n - 1))
            return ins
        cols = 1
        for d in pairs[0][1].ap.shape[1:]:
            cols *= d
        passes = 4 if pairs[0][0].ap.dtype == F32 else 1
        fn.cost = len(pairs) * (0.03 + cols * passes / 2400.0 * (0.6 if passes == 4 else 1.0))
        self._emit("pe", fn, reads + ([] if start else [out]), [out])

    def tr(self, out, in_):
        np_ = in_.ap.shape[0]
        idv = self.ident.v((slice(0, np_),), p=(0, np_))
        self._emit("pe", lambda e: e.transpose(out.ap, in_.ap, idv.ap), [in_, idv], [out])

    def act(self, out, in_, func, bias=None, scale=None, eng="act"):
        kw = {}
        if func == AF.Copy and (bias is not None or scale is not None):
            func = AF.Identity
        if bias is not None:
            kw["bias"] = self.A(bias)
        if scale is not None:
            kw["scale"] = self.A(scale)
        self._emit("act", lambda e: e.activation(out=out.ap, in_=in_.ap, func=func, **kw),
                     [in_, bias, scale], [out])

    def tt(self, out, in0, in1, op, eng="dve"):
        self._emit(eng, lambda e: e.tensor_tensor(out.ap, in0.ap, in1.ap, op), [in0, in1], [out])

    def ts(self, out, in0, s1, op0, s2=None, op1=None, eng="dve"):
        if op1 is None:
            fn = lambda e: e.tensor_scalar(out.ap, in0.ap, self.A(s1), None, op0)
        else:
            fn = lambda e: e.tensor_scalar(out.ap, in0.ap, self.A(s1), self.A(s2), op0, op1)
        self._emit(eng, fn, [in0, s1, s2], [out])

    def stt(self, out, in0, sc, in1, op0, op1, eng="dve"):
        eng = "dve"
        self._emit(eng, lambda e: e.scalar_tensor_tensor(out.ap, in0.ap, self.A(sc), in1.ap, op0, op1),
                     [in0, sc, in1], [out])

    def rsqrt(self, out, in_, mul, add):
        self.ts(out, in_, mul, ALU.mult, add, ALU.add)
        self.act(out, out, AF.Ln)
        self.act(out, out, AF.Exp, scale=-0.5)

    def scan(self, out, d0, d1, init, op0, op1, eng="dve"):
        eng = "dve"
        self._emit(eng, lambda e: e.tensor_tensor_scan(out.ap, d0.ap, d1.ap, self.A(init), op0, op1),
                     [d0, d1, init], [out])

    def copy(self, out, in_, eng="dve"):
        if eng == "act":
            self.act(out, in_, AF.Copy)
        else:
            self._emit(eng, lambda e: e.tensor_copy(out.ap, in_.ap), [in_], [out])

    def memset(self, out, val, eng="dve"):
        self._emit(eng, lambda e: e.memset(out.ap, val), [], [out])

    def dma(self, out, in_, key, eng="sp"):
        self._emit(eng, lambda e, sem: e.dma_start(out=out.ap, in_=in_.ap).then_inc(sem, 16),
                     [in_], [out], key=key)

    def xsv(self, d0, d1, t0, t1):
        if d1 - d0 == 1:
            ap = self.xs[:, d0, t0:t1]
        else:
            ap = self.xs[:, d0:d1, t0:t1]
        return View(ap, [("dr:xs%d" % d, t0, t1) for d in range(d0, d1)])

    def dview(self, name, ap, lo, hi):
        return View(ap, [("dr:" + name, lo, hi)])

    def sb(self, dtype, shape, np_=P):
        esz = 4 if dtype == F32 else 2
        n = int(np.prod(shape)) * esz
        n = (n + 31) // 32 * 32
        off = self.sb_top
        self.sb_top += n
        assert self.sb_top <= self.ARENA_BYTES, ("SBUF arena overflow", self.sb_top)
        return Buf(self, "sb", off, dtype, shape, np_)

    def ps(self, shape=(512,), np_=P, bank=None):
        if bank is None:
            bank = self.banks[self.bank_i % len(self.banks)]
            self.bank_i += 1
        return Buf(self, "ps", bank * 2048, F32, shape, np_)

    def build(self):
        cfg, nc = self.cfg, self.nc
        S, L, NF = cfg.S, cfg.depth, cfg.NF
        NCOL = n_cols(cfg.d_ff)
        dt = nc.dram_tensor
        self.x_in = dt("x", [S, D], F32, kind="ExternalInput").ap()
        self.w_in = dt("w_in", [L, D, IN_COLS], F32, kind="ExternalInput").ap()
        self.w_out = dt("w_out", [L, D, D], F32, kind="ExternalInput").ap()
        self.ffn_up = dt("ffn_up", [L, D, 2 * cfg.d_ff], F32, kind="ExternalInput").ap()
        self.ffn_down = dt("ffn_down", [L, cfg.d_ff, D], F32, kind="ExternalInput").ap()
        self.pool_w = dt("pool_w", [L, 4, P, P], F32, kind="ExternalInput").ap()
        self.lru_wa = dt("lru_wa", [L, NLB, P, P], F32, kind="ExternalInput").ap()
        self.lru_wx = dt("lru_wx", [L, NLB, P, P], F32, kind="ExternalInput").ap()
        self.cols_d = dt("cols", [L, P, NCOL], F32, kind="ExternalInput").ap()
        self.hrow_d = dt("hrow", [L, NH, 2], F32, kind="ExternalInput").ap()
        self.const_d = dt("consts", [P, 6 * P + NH * P + 64], F32, kind="ExternalInput").ap()
        self.y_out = dt("y", [S, D], F32, kind="ExternalOutput").ap()
        self.xs = dt("xs", [P, ND, S], F32).ap()
        self.n_units_M = 21 + 8
        self.n_units_F = NF // 2 * 2 + NF // 2
        self.wsM = dt("wsM", [self.n_units_M, P, 4096], BF16).ap()
        self.wsF = dt("wsF", [self.n_units_F, P, 4096], BF16).ap()

        self.ARENA_BYTES = 207 * 1024
        with ExitStack() as st:
            self.arena = st.enter_context(nc.sbuf_tensor("arena", [P, self.ARENA_BYTES // 4], F32))
            self.psum = [st.enter_context(nc.psum_tensor(f"psb{i}", [P, 512], F32)) for i in range(8)]
            self.sb_top = 0
            self.ident = self.sb(F32, (P,))
            self.ones = self.sb(F32, (P,))
            cd = self.const_d
            self.dma(self.ident.v(), self.dview("c", cd[:, 0:P], 0, 1), "c0")
            self.dma(self.ones.v(), self.dview("c", cd[:, P:2 * P], 0, 1), "c0")
            self.cols = self.sb(F32, (NCOL,))
            self.hrow = self.sb(F32, (2,), np_=NH)
            self.negA = self.sb(F32, (1,), np_=NH)
            self.lruc = self.sb(F32, (NLB,))
            self.pbs = self.sb(F32, (4,))
            self.sb_persist = self.sb_top
            for l in range(L):
                self.layer_setup(l)
                self.m_pass(l)
                self.f_pass(l)
            self.rec.emit(nc, st)
        return nc

    def layer_setup(self, l):
        self.sb_top = self.sb_persist
        t6 = self.sb(F32, (8,), np_=NH)
        tl = self.sb(F32, (NLB,))
        self.dma(self.cols.v(), self.dview("cols", self.cols_d[l], l, l + 1), "cols")
        self.dma(self.hrow.v(), self.dview("hrow", self.hrow_d[l], l, l + 1), "cols")
        self.act(t6.v((slice(0, 1),)), self.hrow.v((slice(0, 1),)), AF.Exp)
        self.ts(self.negA.v(), t6.v((slice(0, 1),)), -1.0, ALU.mult)
        lam = self.cols.v((slice(CO_LLAM, CO_LLAM + NLB),))
        self.act(tl.v(), lam, AF.Exp, scale=-1.0)
        self.act(tl.v(), tl.v(), AF.Ln, bias=1.0)
        self.ts(self.lruc.v(), tl.v(), -8.0, ALU.mult)
        self.tt(self.pbs.v(), self.cols.v((slice(CO_PB, CO_PB + 4),)), self.cols.v((slice(CO_PS, CO_PS + 4),)), ALU.mult)

    def fetch_unit(self, first, dst, ws, uidx, pieces, key):
        if first:
            for (src, dstv, stv) in pieces:
                if self.stage_sel is not None:
                    sidx = self.stage_sel
                else:
                    sidx = self.stage_rr
                    self.stage_rr = (self.stage_rr + 1) % len(self.stage)
                sv = stv(self.stage[sidx])
                self.dma(sv, src, f"stg{sidx}")
                self.copy(dstv, sv, eng=self.cast_engs[self.cast_rr % len(self.cast_engs)])
                self.cast_rr += 1
            self.dma(self.dview(ws[1], ws[0][uidx], uidx, uidx + 1), View(dst.base.rearrange("p a b -> p (a b)") if len(dst.shape) == 2 else dst.base, dst.v().regs), key + "o")
        else:
            self.dma(View(dst.base.rearrange("p a b -> p (a b)") if len(dst.shape) == 2 else dst.base, dst.v().regs),
                     self.dview(ws[1], ws[0][uidx], uidx, uidx + 1), key)

    def kunit_pieces(self, wl, cols, dst):
        pieces = []
        off = 0
        for (c0, w) in cols:
            for kh in range(2):
                src = wl[kh * 1024:(kh + 1) * 1024, c0:c0 + w].rearrange("(j p) c -> p j c", p=P)
                srcv = self.dview("w", src, 0, 1)
                dstv = dst.v((slice(kh * 8, kh * 8 + 8), slice(off, off + w)))
                pieces.append((srcv, dstv, (lambda w_: (lambda sb_: View(
                    sb_.base[:, 0:8 * w_].rearrange("p (j c) -> p j c", j=8), sb_.v().regs)))(w)))
            off += w
        return pieces

    def rmsnorm(self, xT, T, wcol0, out_fn, sq, rstd):
        for hh in range(T // 512):
            ts_ = slice(hh * 512, hh * 512 + 512)
            pss = self.ps()
            for j in range(ND):
                s = sq[j % 2].v((slice(0, 512),))
                self.act(s, xT.v((j, ts_)), AF.Square)
                self.mm(pss.v(), [(self.ones.v(), s)], start=(j == 0), stop=(j == ND - 1))
            r = rstd.v((ts_,))
            self.rsqrt(r, pss.v(), 1.0 / D, EPS)
            for j in range(ND):
                self.stt(out_fn(j, hh), xT.v((j, ts_)), self.cols.v((slice(wcol0 + j, wcol0 + j + 1),)), r,
                         ALU.mult, ALU.mult, eng=("dve" if j % 2 == 0 else "pool"))

    def m_pass(self, l):
        cfg = self.cfg
        S = cfg.S
        T = TM
        self.sb_top = self.sb_persist
        xT = self.sb(F32, (ND, TM))
        hT = self.sb(BF16, (ND, TM))
        mT = self.sb(BF16, (ND, TM))
        self.mT = mT
        self.stage = [self.sb(F32, (2048,)) for _ in range(2)]
        self.stage_rr = 0
        self.cast_engs = ["dve", "act"]
        self.cast_rr = 0
        wb = [self.sb(BF16, (ND, 256)) for _ in range(4)]
        wsmall = self.sb(F32, (16, P))
        self.wsmall = wsmall
        cd = self.const_d
        self.nmU = self.sb(F32, (P,))
        self.pmL = self.sb(F32, (P,))
        self.mU01 = self.sb(F32, (P,))
        self.sel6 = self.sb(F32, (NH * P,), np_=NH)
        self.poolrc = self.sb(F32, (4, 16))
        self.dma(self.nmU.v(), self.dview("c", cd[:, 2 * P:3 * P], 0, 1), "c0")
        self.dma(self.pmL.v(), self.dview("c", cd[:, 3 * P:4 * P], 0, 1), "c0")
        self.dma(self.mU01.v(), self.dview("c", cd[:, 4 * P:5 * P], 0, 1), "c0")
        self.dma(self.sel6.v(), self.dview("c", cd[0:NH, 6 * P:6 * P + NH * P], 0, 1), "c0")
        self.dma(self.poolrc.v(), self.dview("c", cd[:, 6 * P + NH * P:6 * P + NH * P + 64].rearrange(
            "p (a b) -> p a b", a=4), 0, 1), "c0")
        self.Sst = self.sb(F32, (NH, P))
        self.hst = self.sb(F32, (NLB,))
        self.car_g = self.sb(F32, (18, 3))
        self.car_l = self.sb(F32, (NLB, 3))
        self.car_p = self.sb(F32, (4, 16))
        for b_ in (self.Sst, self.hst, self.car_g, self.car_l, self.car_p):
            self.memset(b_.v(), 0.0, eng="pool")
        wt = lambda: self.sb(F32, (TM + 16,))
        TB = [wt() for _ in range(10)]
        TA = [wt() for _ in range(2)]
        TAd = [[wt() for _ in range(4)] for _ in range(2)]
        TC = [wt() for _ in range(8)]
        xtok = Buf(self, "sb", TA[0].off, F32, (D,))
        assert TAd[0][3].off + TAd[0][3].nbytes - TA[0].off >= D * 4
        rowb = self.sb(F32, (5, TM), np_=NH)
        cvo = [self.sb(BF16, (2048,)) for _ in range(2)]
        NF = cfg.NF
        wlu, wld = self.ffn_up[l], self.ffn_down[l]
        cv_pieces = []
        for pi in range(NF // 2):
            for gv in range(2):
                for kh in range(2):
                    c0 = gv * cfg.d_ff + pi * 256
                    src = wlu[kh * 1024:(kh + 1) * 1024, c0:c0 + 256].rearrange("(j p) c -> p j c", p=P)
                    cv_pieces.append((src, pi * 2 + gv, kh, True))
        for q in range(NF // 2):
            for i in range(2):
                f = q * 2 + i
                cv_pieces.append((wld[f * P:(f + 1) * P, :], NF + q, i, False))
        cv_pos = [0]
        NPc = len(cv_pieces)
        cv_seq = []
        for i in range(NPc + 2):
            if i < NPc:
                cv_seq.append(("in", i))
            if 1 <= i <= NPc:
                cv_seq.append(("cast", i - 1))
            if 2 <= i <= NPc + 1:
                cv_seq.append(("out", i - 2))
        colb = self.sb(F32, (4, 4, NH))
        cegl = self.sb(F32, (NH, 4))
        glast = self.sb(F32, (8,), np_=NH)
        for g in range(4):
            self.dma(wsmall.v((g,)), self.dview("pw", self.pool_w[l, g], 0, 1), "wsm")
        for j in range(NLB):
            self.dma(wsmall.v((4 + j,)), self.dview("pw", self.lru_wa[l, j], 0, 1), "wsm")
            self.dma(wsmall.v((10 + j,)), self.dview("pw", self.lru_wx[l, j], 0, 1), "wsm")
        wl_in, wl_out = self.w_in[l], self.w_out[l]
        ws = (self.wsM, "wsM")
        units = [[(C_POOL, 128), (C_POOL + 128, 128)], [(C_POOL + 256, 128), (C_POOL + 384, 128)], [(C_A, 12)]]
        for h in range(NH):
            units.append([(C_Q + h * P, P), (C_K + h * P, P)])
            units.append([(C_V + h * P, P), (C_Z + h * P, P)])
        for j in range(NLB):
            units.append([(C_XR + j * P, P), (C_GR + j * P, P)])
        nb = S // TM
        assert nb >= 2
        for b in range(nb):
            first = (b == 0)
            t0 = b * TM
            tsl = slice(t0, t0 + TM)

            def fetch(u, slot):
                if u < 21:
                    pcs = self.kunit_pieces(wl_in, units[u], wb[slot])
                else:
                    pcs = self.kunit_pieces(wl_out, [((u - 21) * 256, 256)], wb[slot])
                self.fetch_unit(first, wb[slot], ws, u, pcs, f"wb{slot}")

            def proj(slot, off, width):
                pso = self.ps((TM,), np_=width)
                self.mm(pso.v(), [(wb[slot].v((k, slice(off, off + width))), hT.v((k,))) for k in range(ND)])
                return pso
            def load_x(bb):
                tb0 = bb * TM
                if l == 0:
                    for tt_ in range(TM // P):
                        self.dma(xtok.v(), self.dview("x", self.x_in[tb0 + tt_ * P:tb0 + (tt_ + 1) * P, :], 0, 1), "xtok")
                        for g4 in range(4):
                            pst = self.ps((4, P))
                            for q in range(4):
                                self.tr(pst.v((q,)), xtok.v((slice((g4 * 4 + q) * P, (g4 * 4 + q + 1) * P),)))
                            self.copy(xT.v((slice(g4 * 4, g4 * 4 + 4), slice(tt_ * P, (tt_ + 1) * P))), pst.v(),
                                      eng=("act" if g4 % 2 else "dve"))
                    self.dma(self.xsv(0, ND, tb0, tb0 + TM), xT.v(), "xTo")
                else:
                    self.dma(xT.v(), self.xsv(0, ND, tb0, tb0 + TM), "xT")

            def norm_rows():
                self.rmsnorm(xT, TM, CO_N1, lambda j, hh: hT.v((j,)), [TB[0], TB[1]], TB[2])
                psa = proj(3, 0, NH)
                psb = proj(3, NH, NH)
                self.gdn_rows(psa, psb, rowb, colb, cegl, glast)

            if b == 0:
                fetch(2, 3)
                fetch(3, 0)
                fetch(4, 1)
                fetch(0, 2)
                load_x(0)
                norm_rows()

            def stream_A(h):
                self.stage_sel = 0
                pad, cv = TA
                qn, kn, vs, zs = TAd[h % 2]
                for (slot, off, ti, dst) in ((0, 0, 0, qn), (0, P, 1, kn), (1, 0, 2, vs)):
                    psx = proj(slot, off, P)
                    self.gdn_conv(psx, 3 * h + ti, pad, cv, dst)
                psz = proj(1, P, P)
                self.act(zs.v((slice(0, T),)), psz.v(), AF.Silu)
                if h + 1 < NH:
                    fetch(3 + 2 * (h + 1), 0)
                    fetch(4 + 2 * (h + 1), 1)
                sl = (slice(0, T),)
                sq, rn = pad, cv
                for (buf, scl) in ((kn, None), (qn, P ** -0.5)):
                    self.act(sq.v(sl), buf.v(sl), AF.Square)
                    pss = self.ps()
                    self.mm(pss.v(), [(self.ones.v(), sq.v(sl))])
                    self.rsqrt(rn.v(sl), pss.v(), 1.0, EPS)
                    if scl is None:
                        self.tt(buf.v(sl), buf.v(sl), rn.v(sl), ALU.mult)
                    else:
                        self.stt(buf.v(sl), buf.v(sl), scl, rn.v(sl), ALU.mult, ALU.mult)

            c_units = [0, 1, 15, 16, 17, 18, 19, 20]
            c_slot = lambda i: 2 + (i % 2)

            def stream_C(r):
                self.stage_sel = 1
                steps = [0, 1] if r == 0 else [r + 1]
                for i in steps:
                    if i + 1 < len(c_units):
                        fetch(c_units[i + 1], c_slot(i + 1))
                    slot = c_slot(i)
                    if i < 2:
                        for q in range(2):
                            self.pool_group(first, i * 2 + q, proj(slot, q * P, P), TC)
                    else:
                        j = i - 2
                        psx = proj(slot, 0, P)
                        psg = proj(slot, P, P)
                        self.lru_block(b == 0, j, psx, psg, TC)
                if r == 6:
                    fetch(21, 0)
                    fetch(22, 1)

            def cv_op(kind, i):
                src, u, hf, is_up = cv_pieces[i]
                stg, ob = self.stage[i % 2], cvo[i % 2]
                if is_up:
                    sv = View(stg.base.rearrange("p (j c) -> p j c", j=8), stg.v().regs)
                    ov = View(ob.base.rearrange("p (j c) -> p j c", j=8), ob.v().regs)
                else:
                    sv, ov = stg.v(), ob.v()
                if kind == "in":
                    self.dma(sv, self.dview("w", src, 0, 1), f"cvi{i % 2}")
                elif kind == "cast":
                    for q4 in range(8):
                        if is_up:
                            o_ = View(ov.ap[:, q4:q4 + 1, :], ov.regs)
                            i_ = View(sv.ap[:, q4:q4 + 1, :], sv.regs)
                        else:
                            o_ = View(ov.ap[:, 256 * q4:256 * q4 + 256], ov.regs)
                            i_ = View(sv.ap[:, 256 * q4:256 * q4 + 256], sv.regs)
                        self.copy(o_, i_, eng="pool")
                else:
                    self.dma(self.dview("wsF", self.wsF[u][:, hf * 2048:(hf + 1) * 2048], u, u + 1), ob.v(),
                             f"cvo{i % 2}")

            def stream_D(n):
                for _ in range(n):
                    if cv_pos[0] >= len(cv_seq):
                        return
                    kind, i = cv_seq[cv_pos[0]]
                    cv_pos[0] += 1
                    cv_op(kind, i)

            n_cv = -(-len(cv_seq) // (7 * (nb - 1)))
            for r in range(7):
                lists = []
                if b >= 1:
                    lists.append(self.stream(lambda: stream_D(n_cv), []))
                if r < NH:
                    lists.append(self.stream(lambda: stream_A(r), [0, 1]))
                if r >= 1:
                    lists.append(self.stream(lambda: self.gdn_B(r - 1, TB, TAd[(r - 1) % 2], rowb, colb, cegl), [2, 3, 4]))
                lists.append(self.stream(lambda: stream_C(r), [5, 6]))
                if r == 6 and b + 1 < nb:
                    lists.append(self.stream(lambda: load_x(b + 1), [0, 1]))
                self.merge(lists)
                self.stage_sel = None
            def out_proj():
                self.stage_sel = 0
                rot = TC[0:8]
                fetch(23, 2)

                def ld(dti):
                    self.dma(rot[dti % 8].v((slice(0, TM),)), self.xsv(dti, dti + 1, t0, t0 + TM), f"xr{dti % 8}")
                for dti in range(8):
                    ld(dti)
                for u in range(8):
                    slot = u % 3
                    for i in range(2):
                        dti = u * 2 + i
                        xt_ = rot[dti % 8].v((slice(0, TM),))
                        pso = self.ps()
                        self.mm(pso.v(), [(wb[slot].v((k, slice(i * P, i * P + P))), mT.v((k,))) for k in range(ND)])
                        self.tt(xt_, xt_, pso.v(), ALU.add)
                        self.dma(self.xsv(dti, dti + 1, t0, t0 + TM), xt_, f"xw{dti % 8}")
                        if dti + 8 < ND:
                            ld(dti + 8)
                    if u + 3 < 8:
                        fetch(21 + u + 3, slot)
                    elif b + 1 < nb:
                        fetch((3, 4, 0)[slot], slot)

            def next_head():
                self.stage_sel = 1
                fetch(2, 3)
                norm_rows()

            lists = [self.stream(out_proj, [0, 1, 2])]
            if b + 1 < nb:
                lists.append(self.stream(next_head, [3, 4, 5]))
            self.merge(lists)
            self.stage_sel = None
        assert cv_pos[0] == len(cv_seq)

    def pool_group(self, first, g, psu, TC):
        upad, la, lb, dd = TC[0], TC[1], TC[2], TC[3]
        mT, wsmall = self.mT, self.wsmall
        win = 2 ** (g + 1)
        T = TM
        self.copy(upad.v((slice(0, 16),)), self.car_p.v((g,)), eng="dve")
        self.act(upad.v((slice(16, 16 + T),)), psu.v(), AF.Copy)
        self.copy(self.car_p.v((g,)), upad.v((slice(T, T + 16),)), eng="dve")
        src = upad
        sh = 1
        bufs = [la, lb]
        for lev in range(g + 1):
            dst = bufs[lev % 2]
            self.tt(dst.v((slice(sh, 16 + T),)), src.v((slice(sh, 16 + T),)), src.v((slice(0, 16 + T - sh),)), ALU.add)
            src = dst
            sh *= 2
        self.stt(dd.v((slice(0, T),)), src.v((slice(16, 16 + T),)), 1.0 / win, upad.v((slice(16, 16 + T),)),
                 ALU.mult, ALU.subtract)
        if first:
            tmp = TC[4]
            self.tt(tmp.v((slice(0, 16),)), src.v((slice(16, 32),)), self.poolrc.v((g,)), ALU.mult)
            self.tt(dd.v((slice(0, 16),)), tmp.v((slice(0, 16),)), upad.v((slice(16, 32),)), ALU.subtract)
        psy = self.ps()
        self.mm(psy.v(), [(wsmall.v((g,)), dd.v((slice(0, T),)))])
        self.act(mT.v((g,)), psy.v(), AF.Identity, bias=self.pbs.v((slice(g, g + 1),)),
                 scale=self.cols.v((slice(CO_PS + g, CO_PS + g + 1),)))

    def gelu_tanh(self, out, x, t1, t2):
        self.act(t1, x, AF.Square)
        self.ts(t1, t1, 0.044715, ALU.mult, 1.0, ALU.add)
        self.tt(t1, t1, x, ALU.mult)
        self.act(t2, t1, AF.Sigmoid, scale=1.5957691216057308)
        self.tt(out, t2, x, ALU.mult)

    def lru_block(self, seq_start, j, psx, psg, TC):
        T = TM
        sl = (slice(0, T),)
        pad, xc, r, i_, a, gt, t1, t2 = TC
        th = pad
        mT, wsmall = self.mT, self.wsmall
        self.act(gt.v(sl), psg.v(), AF.Copy)
        self.copy(pad.v((slice(0, 3),)), self.car_l.v((j,)), eng="dve")
        self.act(pad.v((slice(3, 3 + T),)), psx.v(), AF.Copy)
        self.copy(self.car_l.v((j,)), pad.v((slice(T, T + 3),)), eng="dve")
        c = lambda tap: self.cols.v((slice(CO_LCW + tap * NLB + j, CO_LCW + tap * NLB + j + 1),))
        xcv = xc.v(sl)
        self.act(xcv, psx.v(), AF.Copy, scale=c(3), bias=self.cols.v((slice(CO_LCB + j, CO_LCB + j + 1),)))
        for tap in (2, 1, 0):
            self.stt(xcv, pad.v((slice(tap, tap + T),)), c(tap), xcv, ALU.mult, ALU.add)
        psr = self.ps()
        self.mm(psr.v(), [(wsmall.v((4 + j,)), xcv)])
        psi = self.ps()
        self.mm(psi.v(), [(wsmall.v((10 + j,)), xcv)])
        self.act(r.v(sl), psr.v(), AF.Sigmoid, bias=self.cols.v((slice(CO_LBA + j, CO_LBA + j + 1),)))
        self.act(i_.v(sl), psi.v(), AF.Sigmoid, bias=self.cols.v((slice(CO_LBX + j, CO_LBX + j + 1),)))
        lc = self.lruc.v((slice(j, j + 1),))
        self.act(a.v(sl), r.v(sl), AF.Exp, scale=lc)
        self.act(th.v(sl), r.v(sl), AF.Tanh, scale=lc)
        self.tt(t1.v(sl), a.v(sl), a.v(sl), ALU.mult)
        self.stt(t1.v(sl), t1.v(sl), 1.0, th.v(sl), ALU.add, ALU.mult)
        self.act(t1.v(sl), t1.v(sl), AF.Sqrt, scale=-1.0)
        if seq_start:
            self.memset(t1.v((slice(0, 1),)), 1.0)
        self.tt(t1.v(sl), t1.v(sl), i_.v(sl), ALU.mult)
        self.tt(t1.v(sl), t1.v(sl), xcv, ALU.mult)
        hv = r.v(sl)
        self.scan(hv, a.v(sl), t1.v(sl), self.hst.v((slice(j, j + 1),)), ALU.mult, ALU.add)
        self.copy(self.hst.v((slice(j, j + 1),)), r.v((slice(T - 1, T),)), eng="dve")
        self.gelu_tanh(gt.v(sl), gt.v(sl), t2.v(sl), i_.v(sl))
        self.tt(mT.v((10 + j,)), hv, gt.v(sl), ALU.mult)

    def gdn_rows(self, psa, psb, rowb, colb, cegl, glast):
        T = TM
        R = lambda i: rowb.v((i,))
        beta, gc, bg, egl, t1 = R(0), R(1), R(2), R(3), R(4)
        p6 = (0, NH)
        self.act(beta, psb.v(), AF.Sigmoid)
        self.act(t1, psa.v(), AF.Exp, bias=self.hrow.v((slice(1, 2),)))
        self.act(t1, t1, AF.Ln, bias=1.0)
        self.ts(t1, t1, self.negA.v(), ALU.mult)
        ones6 = self.ones.v((slice(0, CH),), p=p6)
        for c in range(T // CH):
            cs = slice(c * CH, (c + 1) * CH)
            self.scan(rowb.v((1, cs)), ones6, rowb.v((4, cs)), 0.0, ALU.mult, ALU.add)
            self.copy(glast.v((slice(c, c + 1),)), rowb.v((1, slice(c * CH + CH - 1, (c + 1) * CH))))
        self.act(t1, gc, AF.Exp)
        self.tt(bg, beta, t1, ALU.mult)
        for c in range(T // CH):
            cs = slice(c * CH, (c + 1) * CH)
            self.act(rowb.v((3, cs)), rowb.v((1, cs)), AF.Exp, scale=-1.0, bias=glast.v((slice(c, c + 1),)))
        self.act(glast.v((slice(4, 8),)), glast.v((slice(0, 4),)), AF.Exp)
        psc = self.ps((4, 4, NH))
        id6 = self.ident.v((slice(0, NH),), p=p6)
        for c in range(T // CH):
            cs = slice(c * CH, (c + 1) * CH)
            for qi, row in enumerate((1, 0, 2, 3)):
                self.mm(psc.v((c, qi)), [(rowb.v((row, cs)), id6)])
        self.copy(colb.v(), psc.v())
        pse = self.ps((NH, 4))
        for h in range(NH):
            self.mm(pse.v((h,)), [(self.sel6.v((slice(h * P, (h + 1) * P),)), glast.v((slice(4, 8),)))])
        self.copy(cegl.v(), pse.v(), eng="act")

    def gdn_conv(self, psx, tile_i, pad, cv, out):
        T = TM
        car = self.car_g.v((tile_i,))
        self.copy(pad.v((slice(0, 3),)), car, eng="dve")
        self.act(pad.v((slice(3, 3 + T),)), psx.v(), AF.Copy)
        self.copy(car, pad.v((slice(T, T + 3),)), eng="dve")
        c = lambda tap: self.cols.v((slice(CO_GCW + tap * 18 + tile_i, CO_GCW + tap * 18 + tile_i + 1),))
        cvv = cv.v((slice(0, T),))
        self.act(cvv, psx.v(), AF.Copy, scale=c(3))
        for tap in (2, 1, 0):
            self.stt(cvv, pad.v((slice(tap, tap + T),)), c(tap), cvv, ALU.mult, ALU.add)
        self.act(out.v((slice(0, T),)), cvv, AF.Silu)

    def gdn_B(self, h, TB, TAq, rowb, colb, cegl):
        T = TM
        NCk = T // CH
        sl = (slice(0, T),)
        mT = self.mT
        qn, kn, vs, zs = TAq
        Rg, Rb, Re, Dm, EU, EL, EUs, kbg, kd, vb = TB
        Nb = [EL, Rg]
        Pb = [EUs, Rb]
        Q = Dm
        selh = self.sel6.v((slice(h * P, (h + 1) * P),))
        for (row, dst, eng) in ((1, Rg, "act"), (0, Rb, "dve")):
            psr = self.ps()
            self.mm(psr.v(), [(selh, rowb.v((row,)))])
            self.copy(dst.v(sl), psr.v(), eng=eng)
        self.act(Re.v(sl), Rg.v(sl), AF.Exp)

        def v3(buf):
            vv = buf.v(sl)
            return View(vv.ap.rearrange("p (c f) -> p c f", c=NCk), vv.regs)

        def colq(qi):
            vv = colb.v((slice(None), qi, slice(h, h + 1)))
            return View(vv.ap.broadcast_to([P, NCk, CH]), vv.regs)

        def bcm(m):
            vv = m.v()
            return View(vv.ap.rearrange("p (o f) -> p o f", o=1).broadcast_to([P, NCk, CH]), vv.regs)
        self.tt(v3(Dm), v3(Rg), colq(0), ALU.subtract)
        self.stt(v3(EU), v3(Dm), 0.0, bcm(self.nmU), ALU.min, ALU.add)
        self.act(EU.v(sl), EU.v(sl), AF.Exp)
        self.stt(v3(EL), v3(Dm), 0.0, bcm(self.pmL), ALU.max, ALU.add)
        self.act(EL.v(sl), EL.v(sl), AF.Exp, scale=-1.0)
        self.tt(v3(EUs), v3(EU), bcm(self.mU01), ALU.mult)
        self.tt(EUs.v(sl), EUs.v(sl), Rb.v(sl), ALU.mult)
        self.tt(v3(EL), v3(EL), colq(1), ALU.mult)
        qd = Re
        self.tt(qd.v(sl), qn.v(sl), Re.v(sl), ALU.mult)
        pkk = self.ps()
        pqk = self.ps()
        for c in range(NCk):
            cs = (slice(c * CH, (c + 1) * CH),)
            self.mm(pkk.v(cs), [(kn.v(cs), kn.v(cs))])
            self.mm(pqk.v(cs), [(kn.v(cs), qn.v(cs))])
        N0, P0, attnT = EL, EUs, EU
        self.stt(P0.v(sl), pkk.v(), -1.0, EUs.v(sl), ALU.mult, ALU.mult)
        self.stt(N0.v(sl), pkk.v(), -1.0, EL.v(sl), ALU.mult, ALU.mult)
        self.tt(attnT.v(sl), pqk.v(), EU.v(sl), ALU.mult)
        self.tt(v3(Q), v3(P0), bcm(self.ident), ALU.add)
        ptk = self.ps()
        ptv = self.ps()
        for c in range(NCk):
            cs = (slice(c * CH, (c + 1) * CH),)
            self.tr(ptk.v(cs), kn.v(cs))
            self.tr(ptv.v(cs), vs.v(cs))
        ptk3 = View(ptk.v().ap.rearrange("p (c f) -> p c f", c=NCk), ptk.v().regs)
        ptv3 = View(ptv.v().ap.rearrange("p (c f) -> p c f", c=NCk), ptv.v().regs)
        self.tt(v3(kbg), ptk3, colq(2), ALU.mult)
        self.tt(v3(kd), ptk3, colq(3), ALU.mult)
        self.tt(v3(vb), ptv3, colq(1), ALU.mult)
        for k in range(1, 7):
            Np, Pp, Nn, Pn = Nb[(k - 1) % 2], Pb[(k - 1) % 2], Nb[k % 2], Pb[k % 2]
            pn = self.ps()
            for c in range(NCk):
                cs = (slice(c * CH, (c + 1) * CH),)
                self.mm(pn.v(cs), [(Pp.v(cs), Np.v(cs))])
            self.copy(Nn.v(sl), pn.v(), eng="act")
            if k < 6:
                pp = self.ps()
                for c in range(NCk):
                    cs = (slice(c * CH, (c + 1) * CH),)
                    self.mm(pp.v(cs), [(Np.v(cs), Pp.v(cs))])
                self.copy(Pn.v(sl), pp.v(), eng="dve")
            pq = self.ps()
            for c in range(NCk):
                cs = (slice(c * CH, (c + 1) * CH),)
                self.mm(pq.v(cs), [(Nn.v(cs), Q.v(cs))])
            self.tt(Q.v(sl), Q.v(sl), pq.v(), ALU.add)
        WT, U = Nb[0], Pb[0]
        pw = self.ps()
        pu = self.ps()
        for c in range(NCk):
            cs = (slice(c * CH, (c + 1) * CH),)
            self.mm(pw.v(cs), [(kbg.v(cs), Q.v(cs))])
            self.mm(pu.v(cs), [(Q.v(cs), vb.v(cs))])
        self.copy(WT.v(sl), pw.v(), eng="act")
        self.copy(U.v(sl), pu.v(), eng="dve")
        Sh = self.Sst.v((h,))
        vnew = Nb[1]
        pso = self.ps(bank=7)
        for c in range(NCk):
            cs = (slice(c * CH, (c + 1) * CH),)
            pv = self.ps((CH,))
            self.mm(pv.v(), [(WT.v(cs), Sh)])
            self.tt(vnew.v(cs), U.v(cs), pv.v(), ALU.subtract)
            self.mm(pso.v(cs), [(Sh, qd.v(cs)), (vnew.v(cs), attnT.v(cs))])
            pss_ = self.ps((CH,))
            self.mm(pss_.v(), [(kd.v(cs), vnew.v(cs))])
            self.stt(Sh, Sh, cegl.v((h, slice(c, c + 1))), pss_.v(), ALU.mult, ALU.add)
        osq, rs, on = Dm, Rb, kbg
        self.act(osq.v(sl), pso.v(), AF.Square)
        pss2 = self.ps()
        self.mm(pss2.v(), [(self.ones.v(), osq.v(sl))])
        self.rsqrt(rs.v(sl), pss2.v(), 1.0 / P, EPS)
        self.stt(on.v(sl), pso.v(), self.cols.v((slice(CO_GNW, CO_GNW + 1),)), rs.v(sl), ALU.mult, ALU.mult)
        self.tt(mT.v((4 + h,)), on.v(sl), zs.v(sl), ALU.mult)

    def f_pass(self, l):
        cfg = self.cfg
        S, NF, L = cfg.S, cfg.NF, cfg.depth
        self.sb_top = self.sb_persist
        xT = self.sb(F32, (ND, TF))
        hT = self.sb(BF16, (ND, TF))
        NWU = 4
        wup = [self.sb(BF16, (ND, 256)) for _ in range(NWU)]
        wdn = [self.sb(BF16, (2, D)) for _ in range(4)]
        actb = [self.sb(BF16, (4, TF)) for _ in range(2)]
        self.car_f = self.sb(F32, (NF, 2))
        self.memset(self.car_f.v(), 0.0, eng="pool")
        mark = self.sb_top
        otok = self.sb(F32, (D,))
        self.sb_top = mark
        gbuf = [self.sb(F32, (2 + TF,)) for _ in range(2)]
        tb = [self.sb(F32, (512,)) for _ in range(3)]
        wlu, wld = self.ffn_up[l], self.ffn_down[l]
        ws = (self.wsF, "wsF")
        last = (l == L - 1)
        nb = S // TF
        NCH = NF // 4
        for b in range(nb):
            first = False
            t0 = b * TF
            tsl = slice(t0, t0 + TF)
            self.dma(xT.v(), self.xsv(0, ND, t0, t0 + TF), "xTf")
            rstd = gbuf[0]
            self.rmsnorm(xT, TF, CO_N2, lambda j, hh: hT.v((j, slice(hh * 512, hh * 512 + 512))), [tb[0], tb[1]], rstd)
            self.urr = 0
            self.drr = 0

            def up_chunk(ci):
                ab = actb[ci % 2]
                for pr in range(2):
                    f0 = ci * 4 + pr * 2
                    ug = wup[self.urr % NWU]
                    self.urr += 1
                    self.fetch_unit(first, ug, ws, (f0 // 2) * 2, self.kunit_pieces(wlu, [(f0 * P, 256)], ug),
                                    f"wu{(self.urr - 1) % NWU}")
                    uv = wup[self.urr % NWU]
                    self.urr += 1
                    self.fetch_unit(first, uv, ws, (f0 // 2) * 2 + 1,
                                    self.kunit_pieces(wlu, [(cfg.d_ff + f0 * P, 256)], uv), f"wu{(self.urr - 1) % NWU}")
                    for i in range(2):
                        f = f0 + i
                        gb = gbuf[f % 2]
                        self.copy(gb.v((slice(0, 2),)), self.car_f.v((f,)), eng="pool")
                        for hh in range(TF // 512):
                            hs = slice(hh * 512, hh * 512 + 512)
                            psg = self.ps()
                            self.mm(psg.v(), [(ug.v((k, slice(i * P, i * P + P))), hT.v((k, hs))) for k in range(ND)])
                            psv = self.ps()
                            self.mm(psv.v(), [(uv.v((k, slice(i * P, i * P + P))), hT.v((k, hs))) for k in range(ND)])
                            self.act(gb.v((slice(2 + hh * 512, 2 + hh * 512 + 512),)), psg.v(), AF.Copy)
                            cw = lambda tap: self.cols.v((slice(CO_FCW + tap * NF + f, CO_FCW + tap * NF + f + 1),))
                            t = tb[0].v()
                            self.act(t, psg.v(), AF.Copy, scale=cw(2))
                            self.stt(t, gb.v((slice(1 + hh * 512, 1 + hh * 512 + 512),)), cw(1), t, ALU.mult, ALU.add)
                            self.stt(t, gb.v((slice(hh * 512, hh * 512 + 512),)), cw(0), t, ALU.mult, ALU.add, eng="pool")
                            self.gelu_tanh(t, t, tb[1].v(), tb[2].v())
                            self.tt(ab.v((pr * 2 + i, hs)), t, psv.v(), ALU.mult)
                        self.copy(self.car_f.v((f,)), gb.v((slice(TF, TF + 2),)), eng="pool")

            def down_chunk(ci):
                ab = actb[ci % 2]
                wd = []
                for pr in range(2):
                    dstb = wdn[self.drr % 4]
                    self.drr += 1
                    u = NF + ci * 2 + pr
                    pieces = []
                    for i in range(2):
                        f = ci * 4 + pr * 2 + i
                        srcv = self.dview("w", wld[f * P:(f + 1) * P, :], 0, 1)
                        pieces.append((srcv, dstb.v((i,)), lambda sb_: sb_.v()))
                    self.fetch_unit(first, dstb, ws, u, pieces, f"wd{(self.drr - 1) % 4}")
                    wd.append(dstb)
                for dti in range(ND):
                    for hh in range(TF // 512):
                        hs = slice(hh * 512, hh * 512 + 512)
                        pso = self.ps()
                        self.mm(pso.v(), [(wd[q // 2].v((q % 2, slice(dti * P, dti * P + P))), ab.v((q, hs)))
                                          for q in range(4)])
                        xv = xT.v((dti, hs))
                        self.tt(xv, xv, pso.v(), ALU.add)

            up_chunk(0)
            for ci in range(1, NCH):
                up_chunk(ci)
                down_chunk(ci - 1)
            down_chunk(NCH - 1)
            if not last:
                self.dma(self.xsv(0, ND, t0, t0 + TF), xT.v(), "xTfo")
            else:
                rstd = gbuf[0]
                self.rmsnorm(xT, TF, CO_NF, lambda j, hh: xT.v((j, slice(hh * 512, hh * 512 + 512))),
                             [tb[0], tb[1]], rstd)
                for tt_ in range(TF // P):
                    for g4 in range(4):
                        pst = self.ps()
                        for q in range(4):
                            self.tr(pst.v((slice(q * P, (q + 1) * P),)), xT.v((g4 * 4 + q, slice(tt_ * P, (tt_ + 1) * P))))
                        self.copy(otok.v((slice(g4 * 512, g4 * 512 + 512),)), pst.v(), eng=("act" if g4 % 2 else "dve"))
                    self.dma(self.dview("y", self.y_out[t0 + tt_ * P:t0 + (tt_ + 1) * P, :], t0 + tt_, t0 + tt_ + 1),
                             otok.v(), "otok")


def make_consts():
    c = np.zeros((P, 6 * P + NH * P + 64), np.float32)
    p = np.arange(P)[:, None]
    f = np.arange(P)[None, :]
    c[:, 0:P] = np.eye(P)
    c[:, P:2 * P] = 1.0
    c[:, 2 * P:3 * P] = np.where(f >= p, 0.0, NEG)
    c[:, 3 * P:4 * P] = np.where(p > f, 0.0, -NEG)
    c[:, 4 * P:5 * P] = np.where(f > p, 1.0, 0.0)
    for h in range(NH):
        c[h, 6 * P + h * P:6 * P + (h + 1) * P] = 1.0
    for g in range(4):
        win = 2 ** (g + 1)
        t = np.arange(16)
        c[:, 6 * P + NH * P + g * 16:6 * P + NH * P + (g + 1) * 16] = (1.0 / np.minimum(t + 1, win))[None, :]
    return c


def make_cols(inp, L, d_ff):
    NF = d_ff // P
    cols = np.zeros((L, P, n_cols(d_ff)), np.float32)
    col = lambda v: np.ascontiguousarray(np.asarray(v, np.float32).reshape(-1, P).T)
    for l in range(L):
        c = cols[l]
        c[:, CO_N1:CO_N1 + 16] = col(inp["norm1_w"][l])
        c[:, CO_N2:CO_N2 + 16] = col(inp["norm2_w"][l])
        c[:, CO_NF:CO_NF + 16] = col(inp["final_norm_w"])
        c[:, CO_PB:CO_PB + 4] = col(inp["pool_b"][l])
        c[:, CO_PS:CO_PS + 4] = col(inp["pool_scale"][l])
        gcw = np.asarray(inp["gdn_conv_w"][l], np.float32)
        for tap in range(4):
            t18 = col(gcw[tap])
            for h in range(NH):
                for qi in range(3):
                    c[:, CO_GCW + tap * 18 + 3 * h + qi] = t18[:, qi * NH + h]
        c[:, CO_GNW] = np.asarray(inp["gdn_norm_w"][l], np.float32)
        lcw = np.asarray(inp["lru_conv_w"][l], np.float32)
        for tap in range(4):
            c[:, CO_LCW + tap * NLB:CO_LCW + (tap + 1) * NLB] = col(lcw[tap])
        c[:, CO_LCB:CO_LCB + NLB] = col(inp["lru_conv_b"][l])
        c[:, CO_LBA:CO_LBA + NLB] = col(inp["lru_ba"][l])
        c[:, CO_LBX:CO_LBX + NLB] = col(inp["lru_bx"][l])
        c[:, CO_LLAM:CO_LLAM + NLB] = col(inp["lru_lambda"][l])
        fcw = np.asarray(inp["ffn_conv_w"][l], np.float32)
        for tap in range(3):
            c[:, CO_FCW + tap * NF:CO_FCW + (tap + 1) * NF] = col(fcw[tap])
    return cols


_NC_CACHE = {}
SPREAD = True


def run(inp, S, L, d_ff, n_cores):
    cfg = Cfg(S, L, d_ff)
    keyc = (S, L, d_ff)
    if keyc not in _NC_CACHE:
        _NC_CACHE[keyc] = K(cfg).build()
    nc = _NC_CACHE[keyc]
    f = lambda a: np.ascontiguousarray(np.asarray(a, np.float32))
    hrow = np.stack([f(inp["gdn_a_log"]), f(inp["gdn_dt_bias"])], axis=-1)
    shared = {
        "w_in": f(inp["w_in"]), "w_out": f(inp["w_out"]), "ffn_up": f(inp["ffn_up"]), "ffn_down": f(inp["ffn_down"]),
        "pool_w": f(inp["pool_w"]), "lru_wa": f(inp["lru_wa"]), "lru_wx": f(inp["lru_wx"]),
        "cols": make_cols(inp, L, d_ff), "hrow": np.ascontiguousarray(hrow), "consts": make_consts(),
    }
    x = f(inp["x"])
    if n_cores == 4 and SPREAD:
        slots = [0, 1, 4, 5]
        zero = {k: np.zeros_like(v) for k, v in shared.items()}
        zero["x"] = np.zeros_like(x[0])
        in_maps = [zero] * 8
        in_maps = list(in_maps)
        for i, sl_ in enumerate(slots):
            in_maps[sl_] = dict(shared, x=np.ascontiguousarray(x[i]))
        res = run_bass_kernel_spmd(nc, in_maps, core_ids=list(range(8)))
        return np.stack([np.asarray(res.results[sl_]["y"], np.float32) for sl_ in slots], axis=0)
    in_maps = [dict(shared, x=np.ascontiguousarray(x[i])) for i in range(n_cores)]
    res = run_bass_kernel_spmd(nc, in_maps, core_ids=list(range(n_cores)))
    return np.stack([np.asarray(res.results[i]["y"], np.float32) for i in range(n_cores)], axis=0)


def kernel(**inputs):
    x = np.asarray(inputs["x"])
    B, S, _ = x.shape
    L = np.asarray(inputs["w_in"]).shape[0]
    d_ff = np.asarray(inputs["ffn_down"]).shape[1]
    return run(inputs, S, L, d_ff, B)
```
